# Optimizing a Trainium2 kernel written in Bass

```python
import math
import jax
import jax.numpy as jnp
from jax import lax
import numpy as np

D_MODEL = 1024
BATCH = 4
SEQ = 8192
DEPTH = 4

GRID_W = 64
EPS = 1e-6

A_HEADS = 8
A_KV_HEADS = 2
A_HEAD_DIM = 64
A_GROUP = A_HEADS // A_KV_HEADS
A_BLOCK = 128
ROPE_THETA = 10000.0
ROPE_FREQS = A_HEAD_DIM // 4

DN_HEADS = 8
DN_DK = 64
DN_DV = 64
DN_CONV_W = 5
DN_CHUNK = 64

D_FF = 2816
FFN_CONV_W = 3

N_BRANCH = 2
A_Q = A_HEADS * A_HEAD_DIM
A_KV = A_KV_HEADS * A_HEAD_DIM
DN_QK = DN_HEADS * DN_DK
DN_V = DN_HEADS * DN_DV
SPLIT_SIZES = (A_Q, A_KV, A_KV, DN_QK, DN_QK, DN_V, 2 * DN_HEADS, 2 * DN_HEADS, DN_V, N_BRANCH * D_MODEL)
N_IN = A_Q + 2 * A_KV + 2 * DN_QK + 2 * DN_V + 4 * DN_HEADS + N_BRANCH * D_MODEL

kernel_name = "hybrid_gridrope_gqa_gated_deltanet_convffn_encoder"


def rmsnorm(x, g):
    xf = x.astype(jnp.float32)
    y = xf * lax.rsqrt(jnp.mean(xf * xf, axis=-1, keepdims=True) + EPS)
    return (y * g.astype(jnp.float32)).astype(x.dtype)


def l2norm(x):
    return x * lax.rsqrt(jnp.sum(x * x, axis=-1, keepdims=True) + EPS)


def centred_depthwise_conv(x, w):
    k = w.shape[0]
    c = x.shape[-1]
    return lax.conv_general_dilated(
        x, w.astype(x.dtype)[:, None, :], window_strides=(1,),
        padding=[(k // 2, k // 2)], dimension_numbers=("NWC", "WIO", "NWC"),
        feature_group_count=c)


def grid_rope_tables(seq_len):
    rows_n = seq_len // GRID_W
    row = jnp.repeat(jnp.arange(rows_n), GRID_W).astype(jnp.float32)
    col = jnp.tile(jnp.arange(GRID_W), rows_n).astype(jnp.float32)
    inv_freq = ROPE_THETA ** (-jnp.arange(ROPE_FREQS, dtype=jnp.float32) / ROPE_FREQS)
    ang = jnp.stack([row[:, None] * inv_freq, col[:, None] * inv_freq], axis=1)
    return jnp.cos(ang), jnp.sin(ang)


def apply_grid_rope(x, cos, sin):
    b, s, h, d = x.shape
    xr = x.astype(jnp.float32).reshape(b, s, h, 2, 2, ROPE_FREQS)
    x1, x2 = xr[..., 0, :], xr[..., 1, :]
    c = cos[None, :, None]
    sn = sin[None, :, None]
    out = jnp.stack([x1 * c - x2 * sn, x2 * c + x1 * sn], axis=-2)
    return out.reshape(b, s, h, d).astype(x.dtype)


def grid_attention(q, k, v):
    b, s, _, dh = q.shape
    nb = s // A_BLOCK
    qb = jnp.moveaxis(q.reshape(b, nb, A_BLOCK, A_KV_HEADS, A_GROUP, dh), 1, 0)
    scale = dh ** -0.5

    def block(qi):
        sc = jnp.einsum("bqkgd,bskd->bkgqs", qi, k).astype(jnp.float32) * scale
        p = jax.nn.softmax(sc, axis=-1).astype(v.dtype)
        return jnp.einsum("bkgqs,bskd->bqkgd", p, v)

    o = lax.map(block, qb)
    return jnp.moveaxis(o, 0, 1).reshape(b, s, A_HEADS * dh)


def gated_delta_chunked(q, k, v, beta, g):
    b, s, h, dk = q.shape
    dv = v.shape[-1]
    c = DN_CHUNK
    n = s // c

    def to_chunks(t):
        t = jnp.moveaxis(t, 2, 1)
        return t.reshape((b, h, n, c) + t.shape[3:])

    q, k, v, beta, g = (to_chunks(t) for t in (q, k, v, beta, g))
    gcum = jnp.cumsum(g, axis=-1)
    idx = jnp.arange(c)
    incl = idx[:, None] >= idx[None, :]
    strict = idx[:, None] > idx[None, :]
    decay = jnp.exp(jnp.where(incl, gcum[..., :, None] - gcum[..., None, :], -jnp.inf))
    k_beta = k * beta[..., None]
    a_kk = jnp.where(strict, jnp.einsum("bhnid,bhnjd->bhnij", k_beta, k) * decay, 0.0)
    eye = jnp.eye(c, dtype=q.dtype)
    t_inv = lax.linalg.triangular_solve(eye + a_kk, jnp.broadcast_to(eye, a_kk.shape),
                                        left_side=True, lower=True, unit_diagonal=True)
    u = jnp.einsum("bhnij,bhnjd->bhnid", t_inv, v * beta[..., None])
    w = jnp.einsum("bhnij,bhnjd->bhnid", t_inv, k_beta * jnp.exp(gcum)[..., None])
    qk = jnp.einsum("bhnid,bhnjd->bhnij", q, k) * decay
    q_dec = q * jnp.exp(gcum)[..., None]
    k_dec = k * jnp.exp(gcum[..., -1:] - gcum)[..., None]
    g_tot = jnp.exp(gcum[..., -1])

    def step(state, inp):
        qd, kd, ui, wi, qki, gt = inp
        v_new = ui - jnp.einsum("bhcd,bhde->bhce", wi, state)
        o = jnp.einsum("bhcd,bhde->bhce", qd, state) + jnp.einsum("bhcj,bhje->bhce", qki, v_new)
        state = state * gt[..., None, None] + jnp.einsum("bhcd,bhce->bhde", kd, v_new)
        return state, o

    xs = tuple(jnp.moveaxis(t, 2, 0) for t in (q_dec, k_dec, u, w, qk, g_tot))
    state0 = jnp.zeros((b, h, dk, dv), q.dtype)
    _, o = lax.scan(step, state0, xs)
    o = jnp.moveaxis(o, 0, 2).reshape(b, h, s, dv)
    return jnp.moveaxis(o, 1, 2)


def bidir_gated_deltanet(q, k, v, beta, g):
    fwd = gated_delta_chunked(q, k, v, beta[:, :, 0], g[:, :, 0])
    flip = lambda t: jnp.flip(t, axis=1)
    bwd = flip(gated_delta_chunked(flip(q), flip(k), flip(v), flip(beta[:, :, 1]), flip(g[:, :, 1])))
    return fwd + bwd


def setup_inputs(seed: int = 0) -> dict:
    key = jax.random.key(seed)
    ks = jax.random.split(key, 17)
    f32 = jnp.float32

    def normal(k, shape, scale):
        return jax.random.normal(k, shape, f32) * scale

    def gain(k, shape):
        return 1.0 + 0.05 * jax.random.normal(k, shape, f32)

    dt = jnp.exp(jax.random.uniform(ks[7], (DEPTH, 2, DN_HEADS), f32, math.log(1e-3), math.log(1e-1)))
    return {
        "x": jax.random.normal(ks[0], (BATCH, SEQ, D_MODEL), f32),
        "norm_mix_g": gain(ks[1], (DEPTH, D_MODEL)),
        "w_in": normal(ks[2], (DEPTH, D_MODEL, N_IN), D_MODEL ** -0.5),
        "q_norm_g": gain(ks[3], (DEPTH, A_HEAD_DIM)),
        "k_norm_g": gain(ks[4], (DEPTH, A_HEAD_DIM)),
        "dn_conv_w": normal(ks[5], (DEPTH, DN_CONV_W, 2 * DN_QK + DN_V), DN_CONV_W ** -0.5),
        "dn_a_log": jnp.log(jax.random.uniform(ks[6], (DEPTH, 2, DN_HEADS), f32, 1.0, 16.0)),
        "dn_dt_bias": dt + jnp.log(-jnp.expm1(-dt)),
        "dn_out_norm_g": gain(ks[8], (DEPTH, DN_DV)),
        "w_o_attn": normal(ks[9], (DEPTH, A_Q, D_MODEL), A_Q ** -0.5),
        "w_o_dn": normal(ks[10], (DEPTH, DN_V, D_MODEL), DN_V ** -0.5),
        "w_out": normal(ks[11], (DEPTH, D_MODEL, D_MODEL), D_MODEL ** -0.5),
        "norm_ffn_g": gain(ks[12], (DEPTH, D_MODEL)),
        "w_up": normal(ks[13], (DEPTH, D_MODEL, 2 * D_FF), D_MODEL ** -0.5),
        "ffn_conv_w": normal(ks[14], (DEPTH, FFN_CONV_W, 2 * D_FF), FFN_CONV_W ** -0.5),
        "w_down": normal(ks[15], (DEPTH, D_FF, D_MODEL), D_FF ** -0.5),
    }


def reference(x, norm_mix_g, w_in, q_norm_g, k_norm_g, dn_conv_w, dn_a_log, dn_dt_bias,
              dn_out_norm_g, w_o_attn, w_o_dn, w_out, norm_ffn_g, w_up, ffn_conv_w, w_down):
    f32 = jnp.float32
    b, s, _ = x.shape
    cos, sin = grid_rope_tables(s)
    offsets = [int(o) for o in np.cumsum(SPLIT_SIZES)[:-1]]
    for l in range(DEPTH):
        h = rmsnorm(x, norm_mix_g[l])
        proj = h @ w_in[l]
        (qa, ka, va, qd, kd, vd, beta_logit, decay_logit, z, gate_logit) = jnp.split(proj, offsets, axis=-1)

        qa = apply_grid_rope(rmsnorm(qa.reshape(b, s, A_HEADS, A_HEAD_DIM), q_norm_g[l]), cos, sin)
        ka = apply_grid_rope(rmsnorm(ka.reshape(b, s, A_KV_HEADS, A_HEAD_DIM), k_norm_g[l]), cos, sin)
        va = va.reshape(b, s, A_KV_HEADS, A_HEAD_DIM)
        y_attn = grid_attention(qa, ka, va) @ w_o_attn[l]

        qkv = jax.nn.silu(centred_depthwise_conv(jnp.concatenate([qd, kd, vd], axis=-1), dn_conv_w[l])).astype(f32)
        qd, kd, vd = jnp.split(qkv, [DN_QK, 2 * DN_QK], axis=-1)
        qd = l2norm(qd.reshape(b, s, DN_HEADS, DN_DK)) * (DN_DK ** -0.5)
        kd = l2norm(kd.reshape(b, s, DN_HEADS, DN_DK))
        vd = vd.reshape(b, s, DN_HEADS, DN_DV)
        beta = jax.nn.sigmoid(beta_logit.astype(f32)).reshape(b, s, 2, DN_HEADS)
        g = -jnp.exp(dn_a_log[l].astype(f32)) * jax.nn.softplus(
            decay_logit.astype(f32).reshape(b, s, 2, DN_HEADS) + dn_dt_bias[l].astype(f32))
        o = bidir_gated_deltanet(qd, kd, vd, beta, g)
        o = rmsnorm(o, dn_out_norm_g[l]) * jax.nn.silu(z.astype(f32).reshape(b, s, DN_HEADS, DN_DV))
        y_dn = o.reshape(b, s, DN_V).astype(x.dtype) @ w_o_dn[l]

        gates = jax.nn.sigmoid(gate_logit).reshape(b, s, N_BRANCH, D_MODEL)
        mixed = gates[:, :, 0] * y_attn + gates[:, :, 1] * y_dn
        x = x + (mixed @ w_out[l]).astype(x.dtype)

        h = rmsnorm(x, norm_ffn_g[l])
        u = centred_depthwise_conv(h @ w_up[l], ffn_conv_w[l])
        u_gate, u_val = jnp.split(u, 2, axis=-1)
        x = x + ((jax.nn.silu(u_gate) * u_val) @ w_down[l]).astype(x.dtype)
    return x
```

```python
import contextlib
import numpy as np
import ml_dtypes
import concourse.bass as bass
import concourse.mybir as mybir
from concourse.bass_utils import run_bass_kernel_spmd

F32 = mybir.dt.float32
BF16 = mybir.dt.bfloat16
ALU = mybir.AluOpType
AF = mybir.ActivationFunctionType

D = 1024
SEQ = 8192
DEPTH = 4
NIN = 4896
DFF = 2816
EPS = 1e-6

SEM_WRAP = 6000
DMA_RING = 8
NCF = 13 * 128 + 2


class Res:
    __slots__ = ("name", "last_w", "readers")

    def __init__(self, name):
        self.name = name
        self.last_w = None
        self.readers = []


class Op:
    __slots__ = ("eng", "fn", "deps", "is_dma", "has_dep", "sem", "val", "slot_prev", "barrier")

    def __init__(self, eng, fn, is_dma):
        self.eng = eng
        self.fn = fn
        self.deps = []
        self.is_dma = is_dma
        self.has_dep = False
        self.sem = None
        self.val = 0
        self.slot_prev = None
        self.barrier = False


class Prog:
    ENGS = ("pe", "act", "dve", "pool", "sp")

    def __init__(self, nc):
        self.nc = nc
        self.ops = []
        self.stack = contextlib.ExitStack()
        self.res_all = []
        self.n = 0
        self.last_op = {e: None for e in self.ENGS}
        self.dma_last = {e: {} for e in self.ENGS}
        self.dma_cnt = {e: 0 for e in self.ENGS}
        self.bar_deps = {e: None for e in self.ENGS}

    def sbuf(self, name, shape, dt, stack=None):
        self.n += 1
        t = (stack or self.stack).enter_context(self.nc.sbuf_tensor(f"{name}_{self.n}", list(shape), dt))
        return t

    def psum(self, name, shape, dt, stack=None):
        self.n += 1
        t = (stack or self.stack).enter_context(self.nc.psum_tensor(f"{name}_{self.n}", list(shape), dt))
        return t

    def res(self, name="r"):
        r = Res(name)
        self.res_all.append(r)
        return r

    def _add(self, eng, fn, reads, writes, is_dma):
        o = Op(eng, fn, is_dma)
        deps = []
        seen = set()

        def add(d):
            if d is not None and id(d) not in seen:
                seen.add(id(d))
                deps.append(d)

        for r in reads:
            add(r.last_w)
        for w in writes:
            add(w.last_w)
            for rd in w.readers:
                add(rd)
        for r in reads:
            if not is_dma:
                r.readers = [x for x in r.readers if x.is_dma or x.eng != eng]
            r.readers.append(o)
        for w in writes:
            w.last_w = o
            w.readers = []
        if self.bar_deps[eng] is not None:
            for d in self.bar_deps[eng]:
                add(d)
            self.bar_deps[eng] = None
        if eng == "pe" and not is_dma:
            deps = [d for d in deps if d.is_dma or d.eng != "pe"]
        o.deps = deps
        for d in deps:
            d.has_dep = True
        if is_dma:
            i = self.dma_cnt[eng]
            self.dma_cnt[eng] += 1
            slot = i % DMA_RING
            o.val = 16 * (i // DMA_RING + 1)
            o.sem = (eng, slot)
            o.slot_prev = self.dma_last[eng].get(slot)
            self.dma_last[eng][slot] = o
        else:
            self.last_op[eng] = o
        self.ops.append(o)
        return o

    def op(self, eng, fn, reads=(), writes=()):
        return self._add(eng, fn, reads, writes, False)

    def dma(self, eng, fn, reads=(), writes=()):
        return self._add(eng, fn, reads, writes, True)

    def _tails(self):
        b = [o for o in self.last_op.values() if o is not None]
        for q in self.dma_last.values():
            b.extend(q.values())
        for o in b:
            o.has_dep = True
        return b

    def barrier(self):
        b = self._tails()
        for e in self.ENGS:
            self.bar_deps[e] = list(b)

    def emit(self):
        nc = self.nc
        ops = self.ops
        final = self._tails()
        per_eng = {e: [] for e in self.ENGS}
        cnt = {e: 0 for e in self.ENGS}
        semlist = {e: [] for e in self.ENGS}
        dma_ring = {}
        stack = self.stack

        def new_sem(name):
            return stack.enter_context(nc.semaphore(name))

        for o in ops:
            e = o.eng
            if o.is_dma:
                if o.sem not in dma_ring:
                    dma_ring[o.sem] = new_sem(f"dq_{o.sem[0]}_{o.sem[1]}")
                o.sem = dma_ring[o.sem]
            elif o.has_dep:
                c = cnt[e]
                cnt[e] += 1
                si = c // SEM_WRAP
                if len(semlist[e]) <= si:
                    semlist[e].append(new_sem(f"s_{e}_{si}"))
                o.sem = semlist[e][si]
                o.val = c % SEM_WRAP + 1
            per_eng[e].append(o)
        self.stats = dict(cnt=cnt, dma=dict(self.dma_cnt), nops=len(ops),
                          nsem=sum(len(v) for v in semlist.values()) + len(dma_ring))

        engmap = {"pe": "tensor", "act": "scalar", "dve": "vector", "pool": "gpsimd", "sp": "sync"}

        def run_engine(ename, eng):
            waited = {}

            def wait(sem, val):
                k = id(sem)
                if waited.get(k, 0) >= val:
                    return
                waited[k] = val
                eng.wait_ge(sem, val)

            for o in per_eng[ename]:
                for d in o.deps:
                    if d.sem is None:
                        continue
                    if (not d.is_dma) and d.eng == ename and ename == "pe":
                        continue
                    wait(d.sem, d.val)
                if o.is_dma:
                    if o.slot_prev is not None:
                        wait(o.slot_prev.sem, o.slot_prev.val)
                    o.fn(eng).then_inc(o.sem, 16)
                else:
                    ins = o.fn(eng)
                    if o.sem is not None:
                        ins.then_inc(o.sem, 1)
            for d in final:
                wait(d.sem, d.val)

        with nc.Block() as block:
            for ename in self.ENGS:
                getattr(block, engmap[ename])(lambda eng, ename=ename: run_engine(ename, eng))


def _bf(a):
    return np.ascontiguousarray(a).astype(ml_dtypes.bfloat16)


class Builder:
    def __init__(self, nc, T=SEQ):
        self.nc = nc
        self.T = T
        self.P = Prog(nc)
        self.d = {}
        self.debug = False

    def dram_in(self, name, shape, dt):
        self.d[name] = self.nc.dram_tensor(name, list(shape), dt, kind="ExternalInput").ap()
        return self.d[name]

    def dram_out(self, name, shape, dt):
        self.d[name] = self.nc.dram_tensor(name, list(shape), dt, kind="ExternalOutput").ap()
        return self.d[name]

    def dram_tmp(self, name, shape, dt):
        self.d[name] = self.nc.dram_tensor(name, list(shape), dt,
                                           kind="ExternalOutput" if self.debug else "Internal").ap()
        return self.d[name]

    def load_consts(self):
        P, nc = self.P, self.nc
        self.ones_f = P.sbuf("ones_f", [128, 128], F32)
        r = self.r_const = P.res("const")
        P.op("pool", lambda e: e.memset(self.ones_f[:], 1.0), writes=[r])
        self.eps_c = P.sbuf("eps_c", [128, 1], F32)
        P.op("pool", lambda e: e.memset(self.eps_c[:], EPS), writes=[r])
        self.cf = P.sbuf("cf", [128, NCF], F32)
        P.dma("sp", lambda e: e.dma_start(out=self.cf[:], in_=self.d["cf32"][:, :]), writes=[r])
        self.blk_f = self.cf[:, 0:128]
        self.rot_f = self.cf[:, 128:256]
        c = lambda i: self.cf[:, i * 128:(i + 1) * 128]
        self.identf = c(2)
        self.tri = [c(3), c(4)]
        self.maskS = [c(5), c(6)]
        self.maskI = [c(7), c(8)]
        self.selc = self.cf[:, 13 * 128:13 * 128 + 2]
        self.identb = P.sbuf("identb", [128, 128], BF16)
        P.op("dve", lambda e: e.tensor_copy(out=self.identb[:], in_=self.identf), reads=[r], writes=[r])
        self.zero_f = P.sbuf("zero_f", [128, 512], F32)
        P.op("pool", lambda e: e.memset(self.zero_f[:], 0.0), writes=[r])
        self.zero_b = P.sbuf("zero_b", [128, 512], BF16)
        P.op("pool", lambda e: e.memset(self.zero_b[:], 0.0), writes=[r])

    def ffn(self, l, xin, xout, rx_in, rx_out):
        P, nc, T = self.P, self.nc, self.T
        d = self.d
        W = 510
        ntile = (T + W - 1) // W
        HC = DFF // 2
        NJ = HC // 128
        with contextlib.ExitStack() as st:
            g_sb = P.sbuf("ffn_g", [128, 8], F32, st)
            cw = P.sbuf("ffn_cw", [128, 3, 2 * NJ], F32, st)
            wup = P.sbuf("wup", [128, 8, 2 * HC], BF16, st)
            wdn = P.sbuf("wdn", [128, NJ, D], BF16, st)
            stg = [P.sbuf("stg", [128, HC], F32, st) for _ in range(2)]
            xt = [P.sbuf("xt", [128, 8, 512], F32, st) for _ in range(2)]
            xr = [P.sbuf("xr", [128, 512], F32, st) for _ in range(2)]
            sq = P.sbuf("sq", [128, 8, 512], F32, st)
            rstd = P.sbuf("rstd", [128, 512], F32, st)
            hT = P.sbuf("hT", [128, 8, 512], BF16, st)
            act = P.sbuf("act", [128, NJ, 512], BF16, st)
            tg = [P.sbuf("tg", [128, 512], F32, st) for _ in range(2)]
            tv = [P.sbuf("tv", [128, 512], F32, st) for _ in range(2)]
            sg = [P.sbuf("sg", [128, 512], F32, st) for _ in range(2)]
            xo = [P.sbuf("xo", [128, 512], F32, st) for _ in range(2)]
            ps_ss = P.psum("ps_ss", [128, 512], F32, st)
            ps_g = [P.psum("ps_g", [128, 512], F32, st) for _ in range(2)]
            ps_v = [P.psum("ps_v", [128, 512], F32, st) for _ in range(2)]
            ps_y = [P.psum("ps_y", [128, 512], F32, st) for _ in range(2)]
            R = P.res
            r_g, r_cw, r_wup, r_wdn = R(), R(), R(), R()
            r_stg = [R(), R()]
            r_xt = [R(), R()]
            r_xr = [R(), R()]
            r_sq, r_rstd, r_hT, r_act = R(), R(), R(), R()
            r_tg, r_tv, r_sg, r_xo = [R(), R()], [R(), R()], [R(), R()], [R(), R()]
            r_pss = R()
            r_psg, r_psv, r_psy = [R(), R()], [R(), R()], [R(), R()]

            P.dma("sp", lambda e: e.dma_start(out=g_sb[:], in_=d["norm_ffn_g"][l].rearrange("(kt p) -> p kt", p=128),
                                              allow_slow_non_contiguous=True), writes=[r_g])
            xtiled_in = xin.rearrange("(kt p) n -> p kt n", p=128)
            xtiled_out = xout.rearrange("(kt p) n -> p kt n", p=128)
            for ph in range(2):
                c0 = ph * HC
                for tap in range(3):
                    for half in range(2):
                        P.dma("sp", lambda e, tap=tap, half=half, c0=c0: e.dma_start(
                            out=cw[:, tap, half * NJ:(half + 1) * NJ],
                            in_=d["ffn_conv_w"][l][tap, half * DFF + c0:half * DFF + c0 + HC].rearrange("(m p) -> p m", p=128),
                            allow_slow_non_contiguous=True), writes=[r_cw])
                k = 0
                for kt in range(8):
                    for half in range(2):
                        col = half * DFF + c0
                        s = k % 2
                        k += 1
                        P.dma("sp", lambda e, s=s, kt=kt, col=col: e.dma_start(
                            out=stg[s][:], in_=d["w_up"][l][kt * 128:(kt + 1) * 128, col:col + HC]),
                            writes=[r_stg[s]])
                        P.op("pool", lambda e, s=s, kt=kt, half=half: e.tensor_scalar(
                            out=wup[:, kt, half * HC:(half + 1) * HC], in0=stg[s][:], scalar1=g_sb[:, kt:kt + 1],
                            scalar2=None, op0=ALU.mult), reads=[r_stg[s], r_g], writes=[r_wup])
                for j in range(NJ):
                    s = k % 2
                    k += 1
                    P.dma("sp", lambda e, s=s, j=j, c0=c0: e.dma_start(
                        out=stg[s][:, 0:D], in_=d["w_down"][l][c0 + j * 128:c0 + (j + 1) * 128, :]),
                        writes=[r_stg[s]])
                    P.op("pool", lambda e, s=s, j=j: e.tensor_copy(out=wdn[:, j, :], in_=stg[s][:, 0:D]),
                         reads=[r_stg[s]], writes=[r_wdn])
                for ti in range(ntile):
                    b = ti % 2
                    s0 = ti * W
                    nout = min(W, T - s0)
                    lo = max(s0 - 1, 0)
                    hi = min(s0 + nout + 1, T)
                    off = lo - (s0 - 1)
                    full = (off == 0 and hi - lo == 512)
                    if not full:
                        P.op("pool", lambda e, b=b: e.memset(xt[b][:], 0.0), writes=[r_xt[b]])
                    P.dma("sp", lambda e, b=b, lo=lo, hi=hi, off=off: e.dma_start(
                        out=xt[b][:, :, off:off + hi - lo], in_=xtiled_in[:, :, lo:hi]), reads=[rx_in], writes=[r_xt[b]])
                    P.op("act", lambda e, b=b: e.activation(out=sq[:], in_=xt[b][:], func=AF.Square),
                         reads=[r_xt[b]], writes=[r_sq])
                    for kt in range(8):
                        P.op("pe", lambda e, kt=kt: e.matmul(ps_ss[:], lhsT=self.ones_f[:], rhs=sq[:, kt, :],
                                                             start=(kt == 0), stop=(kt == 7)),
                             reads=[r_sq, self.r_const], writes=[r_pss])
                    P.op("act", lambda e: e.activation(out=rstd[:], in_=ps_ss[:], func=AF.Sqrt, bias=self.eps_c[:],
                                                       scale=1.0 / D), reads=[r_pss, self.r_const], writes=[r_rstd])
                    P.op("dve", lambda e: e.reciprocal(out=rstd[:], in_=rstd[:]), reads=[r_rstd], writes=[r_rstd])
                    for kt in range(8):
                        P.op("dve", lambda e, b=b, kt=kt: e.scalar_tensor_tensor(
                            out=hT[:, kt, :], in0=xt[b][:, kt, :], scalar=1.0, in1=rstd[:], op0=ALU.mult,
                            op1=ALU.mult), reads=[r_xt[b], r_rstd], writes=[r_hT])
                    for j in range(NJ):
                        q = j % 2
                        for kt in range(8):
                            P.op("pe", lambda e, q=q, kt=kt, j=j: e.matmul(
                                ps_g[q][:], lhsT=wup[:, kt, j * 128:(j + 1) * 128], rhs=hT[:, kt, :],
                                start=(kt == 0), stop=(kt == 7)), reads=[r_wup, r_hT], writes=[r_psg[q]])
                        for kt in range(8):
                            P.op("pe", lambda e, q=q, kt=kt, j=j: e.matmul(
                                ps_v[q][:], lhsT=wup[:, kt, HC + j * 128:HC + (j + 1) * 128], rhs=hT[:, kt, :],
                                start=(kt == 0), stop=(kt == 7)), reads=[r_wup, r_hT], writes=[r_psv[q]])
                        for (ps, rps, t, rt, jj) in ((ps_g[q], r_psg[q], tg[q], r_tg[q], j),
                                                     (ps_v[q], r_psv[q], tv[q], r_tv[q], NJ + j)):
                            P.op("act", lambda e, ps=ps, t=t, jj=jj: e.activation(
                                out=t[:, 1:511], in_=ps[:, 0:510], func=AF.Copy, scale=cw[:, 0, jj:jj + 1]),
                                reads=[rps, r_cw], writes=[rt])
                            for tap in (1, 2):
                                P.op("dve", lambda e, ps=ps, t=t, jj=jj, tap=tap: e.scalar_tensor_tensor(
                                    out=t[:, 1:511], in0=ps[:, tap:tap + 510], scalar=cw[:, tap, jj:jj + 1],
                                    in1=t[:, 1:511], op0=ALU.mult, op1=ALU.add),
                                    reads=[rps, r_cw, rt], writes=[rt])
                        P.op("act", lambda e, q=q: e.activation(out=sg[q][:, 1:511], in_=tg[q][:, 1:511], func=AF.Silu),
                             reads=[r_tg[q]], writes=[r_sg[q]])
                        P.op("pool", lambda e, q=q, j=j: e.tensor_tensor(
                            out=act[:, j, 1:511], in0=sg[q][:, 1:511], in1=tv[q][:, 1:511], op=ALU.mult),
                            reads=[r_sg[q], r_tv[q]], writes=[r_act])
                    for mo in range(8):
                        q = mo % 2
                        if ph == 1:
                            P.dma("sp", lambda e, q=q, mo=mo, s0=s0, nout=nout: e.dma_start(
                                out=xr[q][:, 1:1 + nout], in_=xout[mo * 128:(mo + 1) * 128, s0:s0 + nout]),
                                reads=[rx_out], writes=[r_xr[q]])
                        for j in range(NJ):
                            P.op("pe", lambda e, q=q, j=j, mo=mo: e.matmul(
                                ps_y[q][:, 1:511], lhsT=wdn[:, j, mo * 128:(mo + 1) * 128], rhs=act[:, j, 1:511],
                                start=(j == 0), stop=(j == NJ - 1)), reads=[r_wdn, r_act], writes=[r_psy[q]])
                        if ph == 1:
                            P.op("dve", lambda e, q=q: e.tensor_tensor(
                                out=xo[q][:, 1:511], in0=ps_y[q][:, 1:511], in1=xr[q][:, 1:511], op=ALU.add),
                                reads=[r_psy[q], r_xr[q]], writes=[r_xo[q]])
                        else:
                            P.op("dve", lambda e, q=q, mo=mo, b=b: e.tensor_tensor(
                                out=xo[q][:, 1:511], in0=ps_y[q][:, 1:511], in1=xt[b][:, mo, 1:511], op=ALU.add),
                                reads=[r_psy[q], r_xt[b]], writes=[r_xo[q]])
                        P.dma("sp", lambda e, q=q, mo=mo, s0=s0, nout=nout: e.dma_start(
                            out=xout[mo * 128:(mo + 1) * 128, s0:s0 + nout], in_=xo[q][:, 1:1 + nout]),
                            reads=[r_xo[q]], writes=[rx_out])
            P.barrier()

    def load_w(self, st, dst, rdst, src, ncols, scale=None, rscale=None, stg=None, rstg=None, eng="pool"):
        P = self.P
        nk = src.shape[0] // 128
        CH = stg[0].shape[1]
        k = 0
        for kt in range(nk):
            for c0 in range(0, ncols, CH):
                cn = min(CH, ncols - c0)
                s = k % 2
                k += 1
                P.dma("sp", lambda e, s=s, kt=kt, c0=c0, cn=cn: e.dma_start(
                    out=stg[s][:, 0:cn], in_=src[kt * 128:(kt + 1) * 128, c0:c0 + cn]), writes=[rstg[s]])
                if scale is not None:
                    P.op(eng, lambda e, s=s, kt=kt, c0=c0, cn=cn: e.tensor_scalar(
                        out=dst[:, kt, c0:c0 + cn], in0=stg[s][:, 0:cn], scalar1=scale[:, kt:kt + 1], scalar2=None,
                        op0=ALU.mult), reads=[rstg[s], rscale], writes=[rdst])
                else:
                    P.op(eng, lambda e, s=s, kt=kt, c0=c0, cn=cn: e.tensor_copy(
                        out=dst[:, kt, c0:c0 + cn], in_=stg[s][:, 0:cn]), reads=[rstg[s]], writes=[rdst])

    def rms_tile(self, xt, r_xt, hT, r_hT, tmp):
        P = self.P
        sqt, r_sqt, ps_ss, r_pss, rstd, r_rstd = tmp
        for kt in range(8):
            s = kt % 2
            P.op("act", lambda e, s=s, kt=kt: e.activation(out=sqt[s][:], in_=xt[:, kt, :], func=AF.Square),
                 reads=[r_xt], writes=[r_sqt[s]])
            P.op("pe", lambda e, s=s, kt=kt: e.matmul(ps_ss[:], lhsT=self.ones_f[:], rhs=sqt[s][:],
                                                      start=(kt == 0), stop=(kt == 7)),
                 reads=[r_sqt[s], self.r_const], writes=[r_pss])
        P.op("act", lambda e: e.activation(out=rstd[:], in_=ps_ss[:], func=AF.Sqrt, bias=self.eps_c[:],
                                           scale=1.0 / D), reads=[r_pss, self.r_const], writes=[r_rstd])
        P.op("dve", lambda e: e.reciprocal(out=rstd[:], in_=rstd[:]), reads=[r_rstd], writes=[r_rstd])
        for kt in range(8):
            P.op("dve" if kt % 2 == 0 else "pool", lambda e, kt=kt: e.tensor_tensor(
                out=hT[:, kt, :], in0=xt[:, kt, :], in1=rstd[:], op=ALU.mult),
                reads=[r_xt, r_rstd], writes=[r_hT])

    def proj(self, l, xin, rx_in):
        P, nc, T, d = self.P, self.nc, self.T, self.d
        R = P.res
        NT = T // 512
        OQA, OKA, OVA, OQD, OBD, OZ, OG = 0, 512, 640, 768, 2304, 2336, 2848
        with contextlib.ExitStack() as st:
            g_sb = P.sbuf("mix_g", [128, 8], F32, st)
            win = P.sbuf("win", [128, 8, NIN], BF16, st)
            wk2 = P.sbuf("wk2", [128, 8, 256], BF16, st)
            stg = [P.sbuf("stg", [128, 1224], F32, st) for _ in range(2)]
            qg = P.sbuf("qg", [128, 2], F32, st)
            xt = P.sbuf("xt", [128, 8, 512], F32, st)
            hT = P.sbuf("hT", [128, 8, 512], BF16, st)
            sqt = [P.sbuf("sqt", [128, 512], F32, st) for _ in range(2)]
            rstd = P.sbuf("rstd", [128, 512], F32, st)
            cs = [P.sbuf("cs", [128, 2, 512], F32, st) for _ in range(2)]
            sqq = [P.sbuf("sqq", [128, 512], F32, st) for _ in range(2)]
            rs = [P.sbuf("rs", [128, 512], F32, st) for _ in range(2)]
            qn = [P.sbuf("qn", [128, 512], F32, st) for _ in range(2)]
            t1 = [P.sbuf("t1", [128, 512], F32, st) for _ in range(2)]
            t2 = [P.sbuf("t2", [128, 512], F32, st) for _ in range(2)]
            ob = [P.sbuf("ob", [128, 512], BF16, st) for _ in range(3)]
            of = [P.sbuf("of", [128, 512], F32, st) for _ in range(3)]
            vb = [P.sbuf("vb", [128, 130], BF16, st) for _ in range(2)]
            ps_ss = P.psum("ps_ss", [128, 512], F32, st)
            ps_m = [P.psum("ps_m", [128, 512], F32, st) for _ in range(3)]
            ps_a = [P.psum("ps_a", [128, 512], F32, st) for _ in range(2)]
            r_g, r_win, r_wk2, r_qg, r_xt, r_hT, r_rstd, r_pss = R(), R(), R(), R(), R(), R(), R(), R()
            r_stg, r_sqt, r_cs, r_sqq, r_rs, r_qn, r_t1, r_t2 = ([R(), R()] for _ in range(8))
            r_ob, r_of, r_psm = ([R(), R(), R()] for _ in range(3))
            r_vb, r_psa = [R(), R()], [R(), R()]
            rd = self.rd

            P.dma("sp", lambda e: e.dma_start(out=g_sb[:], in_=d["norm_mix_g"][l].rearrange("(kt p) -> p kt", p=128),
                                              allow_slow_non_contiguous=True), writes=[r_g])
            import os
            SKIP = os.environ.get("SKIP", "")
            for h2 in range(0 if "qg" in SKIP else 2):
                P.dma("sp", lambda e, h2=h2: e.dma_start(out=qg[h2 * 64:(h2 + 1) * 64, 0:1],
                                                         in_=d["q_norm_g"][l].rearrange("(p o) -> p o", o=1),
                                                         allow_slow_non_contiguous=True), writes=[r_qg])
                P.dma("sp", lambda e, h2=h2: e.dma_start(out=qg[h2 * 64:(h2 + 1) * 64, 1:2],
                                                         in_=d["k_norm_g"][l].rearrange("(p o) -> p o", o=1),
                                                         allow_slow_non_contiguous=True), writes=[r_qg])
            P.op("dve", lambda e: e.tensor_scalar(out=qg[:, 0:1], in0=qg[:, 0:1], scalar1=0.125, scalar2=None,
                                                  op0=ALU.mult), reads=[r_qg], writes=[r_qg])
            self.load_w(st, win, r_win, d["w_in"][l], NIN, scale=g_sb, rscale=r_g, stg=stg, rstg=r_stg)
            for kt in range(0 if "wk2" in SKIP else 8):
                for g in range(2):
                    for dup in range(2):
                        P.op("pool", lambda e, kt=kt, g=g, dup=dup: e.tensor_copy(
                            out=wk2[:, kt, g * 128 + dup * 64:g * 128 + dup * 64 + 64],
                            in_=win[:, kt, OKA + g * 64:OKA + g * 64 + 64]), reads=[r_win], writes=[r_wk2])
            for b in range(0 if "vb" in SKIP else 2):
                for g in range(2):
                    P.op("pool", lambda e, b=b, g=g: e.memset(vb[b][:, g * 65 + 64:g * 65 + 65], 1.0), writes=[r_vb[b]])
            xtiled = xin.rearrange("(kt p) n -> p kt n", p=128)
            cnt = {"m": 0, "a": 0, "o": 0, "f": 0, "q": 0}

            def fm_group(lhs_fn, kind, dst, rdst, row0, c0, ti):
                i = cnt["m"] % 3
                cnt["m"] += 1
                for kt in range(8):
                    P.op("pe", lambda e, kt=kt, i=i: e.matmul(ps_m[i][:], lhsT=lhs_fn(kt), rhs=hT[:, kt, :],
                                                              start=(kt == 0), stop=(kt == 7)),
                         reads=[r_win, r_wk2, r_hT], writes=[r_psm[i]])
                if kind == "f32":
                    j = cnt["f"] % 3
                    cnt["f"] += 1
                    P.op("act", lambda e, i=i, j=j: e.copy(out=of[j][:], in_=ps_m[i][:]), reads=[r_psm[i]],
                         writes=[r_of[j]])
                    P.dma("sp", lambda e, j=j: e.dma_start(out=dst[row0:row0 + 128, c0:c0 + 512], in_=of[j][:]),
                          reads=[r_of[j]], writes=[rdst])
                elif kind == "sig":
                    j = cnt["o"] % 3
                    cnt["o"] += 1
                    P.op("act", lambda e, i=i, j=j: e.activation(out=ob[j][:], in_=ps_m[i][:], func=AF.Sigmoid),
                         reads=[r_psm[i]], writes=[r_ob[j]])
                    P.dma("sp", lambda e, j=j: e.dma_start(out=dst[row0:row0 + 128, c0:c0 + 512], in_=ob[j][:]),
                          reads=[r_ob[j]], writes=[rdst])
                else:
                    q = cnt["q"] % 2
                    cnt["q"] += 1
                    a = cnt["a"] % 2
                    cnt["a"] += 1
                    cb = ti % 2
                    P.op("act", lambda e, i=i, q=q: e.activation(out=sqq[q][:], in_=ps_m[i][:], func=AF.Square),
                         reads=[r_psm[i]], writes=[r_sqq[q]])
                    P.op("pe", lambda e, a=a, q=q: e.matmul(ps_a[a][:], lhsT=self.blk_f[:], rhs=sqq[q][:],
                                                            start=True, stop=True),
                         reads=[r_sqq[q], self.r_const], writes=[r_psa[a]])
                    P.op("act", lambda e, a=a, q=q: e.activation(out=rs[q][:], in_=ps_a[a][:], func=AF.Sqrt,
                                                                 bias=self.eps_c[:], scale=1.0 / 64),
                         reads=[r_psa[a], self.r_const], writes=[r_rs[q]])
                    P.op("dve", lambda e, q=q: e.reciprocal(out=rs[q][:], in_=rs[q][:]), reads=[r_rs[q]], writes=[r_rs[q]])
                    P.op("dve", lambda e, i=i, q=q: e.scalar_tensor_tensor(
                        out=qn[q][:], in0=ps_m[i][:], scalar=qg[:, kind:kind + 1], in1=rs[q][:], op0=ALU.mult,
                        op1=ALU.mult), reads=[r_psm[i], r_qg, r_rs[q]], writes=[r_qn[q]])
                    a2 = cnt["a"] % 2
                    cnt["a"] += 1
                    P.op("pe", lambda e, a2=a2, q=q: e.matmul(ps_a[a2][:], lhsT=self.rot_f[:], rhs=qn[q][:],
                                                              start=True, stop=True),
                         reads=[r_qn[q], self.r_const], writes=[r_psa[a2]])
                    P.op("pool", lambda e, q=q, cb=cb: e.tensor_tensor(out=t1[q][:], in0=qn[q][:], in1=cs[cb][:, 0, :],
                                                                       op=ALU.mult),
                         reads=[r_qn[q], r_cs[cb]], writes=[r_t1[q]])
                    P.op("dve", lambda e, q=q, cb=cb, a2=a2: e.tensor_tensor(out=t2[q][:], in0=ps_a[a2][:],
                                                                             in1=cs[cb][:, 1, :], op=ALU.mult),
                         reads=[r_psa[a2], r_cs[cb]], writes=[r_t2[q]])
                    j = cnt["o"] % 3
                    cnt["o"] += 1
                    P.op("pool", lambda e, q=q, j=j: e.tensor_tensor(out=ob[j][:], in0=t1[q][:], in1=t2[q][:], op=ALU.add),
                         reads=[r_t1[q], r_t2[q]], writes=[r_ob[j]])
                    P.dma("sp", lambda e, j=j: e.dma_start(out=dst[row0:row0 + 128, c0:c0 + 512], in_=ob[j][:]),
                          reads=[r_ob[j]], writes=[rdst])

            for ti in range(NT):
                c0 = ti * 512
                P.dma("sp", lambda e, c0=c0: e.dma_start(out=xt[:], in_=xtiled[:, :, c0:c0 + 512]),
                      reads=[rx_in], writes=[r_xt])
                P.dma("sp", lambda e, c0=c0, ti=ti: e.dma_start(out=cs[ti % 2][:, 0, :], in_=d["ropec"][:, c0:c0 + 512]),
                      writes=[r_cs[ti % 2]])
                P.dma("sp", lambda e, c0=c0, ti=ti: e.dma_start(out=cs[ti % 2][:, 1, :], in_=d["ropes"][:, c0:c0 + 512]),
                      writes=[r_cs[ti % 2]])
                self.rms_tile(xt, r_xt, hT, r_hT, (sqt, r_sqt, ps_ss, r_pss, rstd, r_rstd))
                import os
                PARTS = os.environ.get("PROJ_PARTS", "qdgt")
                for m in range(4 if "q" in PARTS else 0):
                    fm_group(lambda kt, m=m: win[:, kt, OQA + m * 128:OQA + (m + 1) * 128], 0, d["qaT"], rd["qaT"],
                             m * 128, c0, ti)
                for g in range(2 if "q" in PARTS else 0):
                    fm_group(lambda kt, g=g: wk2[:, kt, g * 128:(g + 1) * 128], 1, d["kaT"], rd["kaT"], g * 128, c0, ti)
                for m in range(12 if "d" in PARTS else 0):
                    fm_group(lambda kt, m=m: win[:, kt, OQD + m * 128:OQD + (m + 1) * 128], "f32", d["dpre"], rd["dpre"],
                             m * 128, c0 + 2, ti)
                for m in range(16 if "g" in PARTS else 0):
                    fm_group(lambda kt, m=m: win[:, kt, OG + m * 128:OG + (m + 1) * 128], "sig", d["gT"], rd["gT"],
                             m * 128, c0, ti)
                for sub in range(4 if "t" in PARTS else 0):
                    r0 = c0 + sub * 128
                    i = cnt["m"] % 3
                    cnt["m"] += 1
                    for kt in range(8):
                        P.op("pe", lambda e, kt=kt, i=i, sub=sub: e.matmul(
                            ps_m[i][:, 0:128], lhsT=hT[:, kt, sub * 128:(sub + 1) * 128], rhs=win[:, kt, OVA:OVA + 128],
                            start=(kt == 0), stop=(kt == 7)), reads=[r_win, r_hT], writes=[r_psm[i]])
                    b = sub % 2
                    for g in range(2):
                        P.op("act", lambda e, i=i, b=b, g=g: e.copy(out=vb[b][:, g * 65:g * 65 + 64],
                                                                     in_=ps_m[i][:, g * 64:(g + 1) * 64]),
                             reads=[r_psm[i]], writes=[r_vb[b]])
                    P.dma("sp", lambda e, b=b, r0=r0: e.dma_start(out=d["va"][r0:r0 + 128, :], in_=vb[b][:]),
                          reads=[r_vb[b]], writes=[rd["va"]])
                    for (oc, ncol, dst, rdst) in ((OBD, 32, d["bd"], rd["bd"]), (OZ, 512, d["z"], rd["z"])):
                        i = cnt["m"] % 3
                        cnt["m"] += 1
                        for kt in range(8):
                            P.op("pe", lambda e, kt=kt, i=i, sub=sub, oc=oc, ncol=ncol: e.matmul(
                                ps_m[i][:, 0:ncol], lhsT=hT[:, kt, sub * 128:(sub + 1) * 128],
                                rhs=win[:, kt, oc:oc + ncol], start=(kt == 0), stop=(kt == 7)),
                                reads=[r_win, r_hT], writes=[r_psm[i]])
                        j = cnt["f"] % 3
                        cnt["f"] += 1
                        P.op("act", lambda e, i=i, j=j, ncol=ncol: e.copy(out=of[j][:, 0:ncol], in_=ps_m[i][:, 0:ncol]),
                             reads=[r_psm[i]], writes=[r_of[j]])
                        P.dma("sp", lambda e, j=j, r0=r0, ncol=ncol, dst=dst: e.dma_start(
                            out=dst[r0:r0 + 128, :], in_=of[j][:, 0:ncol]), reads=[r_of[j]], writes=[rdst])
            P.barrier()

    def attention(self, l):
        P, nc, T, d, rd = self.P, self.nc, self.T, self.d, self.rd
        R = P.res
        NKT = T // 128
        NQC = T // 512
        with contextlib.ExitStack() as st:
            kg = P.sbuf("kg", [128, T], BF16, st)
            vg = P.sbuf("vg", [128, NKT, 65], BF16, st)
            qt = [P.sbuf("qt", [128, 512], BF16, st) for _ in range(2)]
            pT = [P.sbuf("pT", [128, 512], BF16, st) for _ in range(4)]
            rsum = P.sbuf("rsum", [128, 512], F32, st)
            ocp = [P.sbuf("ocp", [64, 512], F32, st) for _ in range(2)]
            oo = [P.sbuf("oo", [64, 512], BF16, st) for _ in range(2)]
            ps_s = [P.psum("ps_s", [128, 512], F32, st) for _ in range(4)]
            ps_o = [P.psum("ps_o", [128, 512], F32, st) for _ in range(2)]
            ps_b = P.psum("ps_b", [128, 512], F32, st)
            r_kg, r_vg, r_rsum, r_psb = R(), R(), R(), R()
            r_qt, r_ocp, r_oo, r_pso = ([R(), R()] for _ in range(4))
            r_pT, r_pss = ([R(), R(), R(), R()] for _ in range(2))
            it = 0
            hi = 0
            for g in range(2):
                P.dma("sp", lambda e, g=g: e.dma_start(out=kg[:], in_=d["kaT"][g * 128:(g + 1) * 128, :]),
                      reads=[rd["kaT"]], writes=[r_kg])
                P.dma("sp", lambda e, g=g: e.dma_start(
                    out=vg[:], in_=d["va"][:, g * 65:(g + 1) * 65].rearrange("(kt p) c -> p kt c", p=128)),
                    reads=[rd["va"]], writes=[r_vg])
                for qc in range(NQC):
                    for pair in range(2):
                        qb = (qc * 2 + pair) % 2
                        mrow = (g * 2 + pair) * 128
                        P.dma("sp", lambda e, qb=qb, mrow=mrow, qc=qc: e.dma_start(
                            out=qt[qb][:], in_=d["qaT"][mrow:mrow + 128, qc * 512:(qc + 1) * 512]),
                            reads=[rd["qaT"]], writes=[r_qt[qb]])
                        for h2 in range(2):
                            head = g * 4 + pair * 2 + h2
                            ob_ = hi % 2
                            hi += 1
                            pl, ph = h2 * 64, h2 * 64 + 64
                            for kt in range(NKT):
                                s = it % 4
                                it += 1
                                P.op("pe", lambda e, s=s, kt=kt, qb=qb, pl=pl, ph=ph: e.matmul(
                                    ps_s[s][:], lhsT=kg[pl:ph, kt * 128:(kt + 1) * 128], rhs=qt[qb][pl:ph, :],
                                    start=True, stop=True), reads=[r_kg, r_qt[qb]], writes=[r_pss[s]])
                                P.op("act", lambda e, s=s: e.activation(out=pT[s][:], in_=ps_s[s][:], func=AF.Exp),
                                     reads=[r_pss[s]], writes=[r_pT[s]])
                                P.op("pe", lambda e, s=s, kt=kt, ob_=ob_: e.matmul(
                                    ps_o[ob_][0:65, :], lhsT=vg[:, kt, :], rhs=pT[s][:], start=(kt == 0),
                                    stop=(kt == NKT - 1)), reads=[r_vg, r_pT[s]], writes=[r_pso[ob_]])
                            P.op("dve", lambda e, ob_=ob_: e.reciprocal(out=rsum[64:65, :], in_=ps_o[ob_][64:65, :]),
                                 reads=[r_pso[ob_]], writes=[r_rsum])
                            P.op("pe", lambda e: e.matmul(ps_b[0:64, :], lhsT=self.ones_f[64:65, 0:64], rhs=rsum[64:65, :],
                                                          start=True, stop=True),
                                 reads=[r_rsum, self.r_const], writes=[r_psb])
                            P.op("act", lambda e, ob_=ob_: e.copy(out=ocp[ob_][:], in_=ps_o[ob_][0:64, :]),
                                 reads=[r_pso[ob_]], writes=[r_ocp[ob_]])
                            P.op("dve", lambda e, ob_=ob_: e.tensor_tensor(out=oo[ob_][:], in0=ocp[ob_][:],
                                                                           in1=ps_b[0:64, :], op=ALU.mult),
                                 reads=[r_ocp[ob_], r_psb], writes=[r_oo[ob_]])
                            P.dma("sp", lambda e, ob_=ob_, head=head, qc=qc: e.dma_start(
                                out=d["oaT"][head * 64:(head + 1) * 64, qc * 512:(qc + 1) * 512], in_=oo[ob_][:]),
                                reads=[r_oo[ob_]], writes=[rd["oaT"]])
            P.barrier()

    def merge(self, l, xin, xout, rx_in, rx_out):
        P, nc, T, d, rd = self.P, self.nc, self.T, self.d, self.rd
        R = P.res
        NT = T // 512
        with contextlib.ExitStack() as st:
            woa = P.sbuf("woa", [128, 4, D], BF16, st)
            wod = P.sbuf("wod", [128, 4, D], BF16, st)
            wo = P.sbuf("wo", [128, 8, D], BF16, st)
            stg = [P.sbuf("stg", [128, 1024], F32, st) for _ in range(2)]
            oa = [P.sbuf("oa", [128, 4, 512], BF16, st) for _ in range(2)]
            od = [P.sbuf("od", [128, 4, 512], BF16, st) for _ in range(2)]
            gt = [P.sbuf("gt", [128, 2, 512], BF16, st) for _ in range(2)]
            ta = [P.sbuf("ta", [128, 512], F32, st) for _ in range(2)]
            mx = P.sbuf("mx", [128, 8, 512], BF16, st)
            xr = [P.sbuf("xr", [128, 512], F32, st) for _ in range(2)]
            xo = [P.sbuf("xo", [128, 512], F32, st) for _ in range(2)]
            ps_a = [P.psum("ps_a", [128, 512], F32, st) for _ in range(2)]
            ps_d = [P.psum("ps_d", [128, 512], F32, st) for _ in range(2)]
            ps_y = [P.psum("ps_y", [128, 512], F32, st) for _ in range(2)]
            r_woa, r_wod, r_wo, r_mx = R(), R(), R(), R()
            r_stg, r_oa, r_od, r_gt, r_ta, r_xr, r_xo, r_psa, r_psd, r_psy = ([R(), R()] for _ in range(10))
            self.load_w(st, woa, r_woa, d["w_o_attn"][l], D, stg=stg, rstg=r_stg)
            self.load_w(st, wod, r_wod, d["w_o_dn"][l], D, stg=stg, rstg=r_stg)
            self.load_w(st, wo, r_wo, d["w_out"][l], D, stg=stg, rstg=r_stg)
            oaT = d["oaT"].rearrange("(kt p) n -> p kt n", p=128)
            odT = d["odT"].rearrange("(kt p) n -> p kt n", p=128)
            k = 0
            for ti in range(NT):
                c0 = ti * 512
                b = ti % 2
                P.dma("sp", lambda e, b=b, c0=c0: e.dma_start(out=oa[b][:], in_=oaT[:, :, c0:c0 + 512]),
                      reads=[rd["oaT"]], writes=[r_oa[b]])
                P.dma("sp", lambda e, b=b, c0=c0: e.dma_start(out=od[b][:], in_=odT[:, :, c0:c0 + 512]),
                      reads=[rd["odT"]], writes=[r_od[b]])
                for mo in range(8):
                    q = k % 2
                    k += 1
                    for br in range(2):
                        P.dma("sp", lambda e, q=q, br=br, mo=mo, c0=c0: e.dma_start(
                            out=gt[q][:, br, :], in_=d["gT"][br * D + mo * 128:br * D + (mo + 1) * 128, c0:c0 + 512]),
                            reads=[rd["gT"]], writes=[r_gt[q]])
                    for kt in range(4):
                        P.op("pe", lambda e, q=q, kt=kt, mo=mo, b=b: e.matmul(
                            ps_a[q][:], lhsT=woa[:, kt, mo * 128:(mo + 1) * 128], rhs=oa[b][:, kt, :],
                            start=(kt == 0), stop=(kt == 3)), reads=[r_woa, r_oa[b]], writes=[r_psa[q]])
                    for kt in range(4):
                        P.op("pe", lambda e, q=q, kt=kt, mo=mo, b=b: e.matmul(
                            ps_d[q][:], lhsT=wod[:, kt, mo * 128:(mo + 1) * 128], rhs=od[b][:, kt, :],
                            start=(kt == 0), stop=(kt == 3)), reads=[r_wod, r_od[b]], writes=[r_psd[q]])
                    P.op("dve", lambda e, q=q: e.tensor_tensor(out=ta[q][:], in0=ps_a[q][:], in1=gt[q][:, 0, :],
                                                               op=ALU.mult), reads=[r_psa[q], r_gt[q]], writes=[r_ta[q]])
                    P.op("dve", lambda e, q=q: e.tensor_tensor(out=xo[q][:], in0=ps_d[q][:], in1=gt[q][:, 1, :],
                                                               op=ALU.mult), reads=[r_psd[q], r_gt[q]], writes=[r_xo[q]])
                    P.op("pool", lambda e, q=q, mo=mo: e.tensor_tensor(out=mx[:, mo, :], in0=ta[q][:], in1=xo[q][:],
                                                                       op=ALU.add),
                         reads=[r_ta[q], r_xo[q]], writes=[r_mx])
                for mo in range(8):
                    q = k % 2
                    k += 1
                    P.dma("sp", lambda e, q=q, mo=mo, c0=c0: e.dma_start(
                        out=xr[q][:], in_=xin[mo * 128:(mo + 1) * 128, c0:c0 + 512]), reads=[rx_in], writes=[r_xr[q]])
                    for kt in range(8):
                        P.op("pe", lambda e, q=q, kt=kt, mo=mo: e.matmul(
                            ps_y[q][:], lhsT=wo[:, kt, mo * 128:(mo + 1) * 128], rhs=mx[:, kt, :],
                            start=(kt == 0), stop=(kt == 7)), reads=[r_wo, r_mx], writes=[r_psy[q]])
                    P.op("dve", lambda e, q=q: e.tensor_tensor(out=xo[q][:], in0=ps_y[q][:], in1=xr[q][:], op=ALU.add),
                         reads=[r_psy[q], r_xr[q]], writes=[r_xo[q]])
                    P.dma("sp", lambda e, q=q, mo=mo, c0=c0: e.dma_start(
                        out=xout[mo * 128:(mo + 1) * 128, c0:c0 + 512], in_=xo[q][:]), reads=[r_xo[q]], writes=[rx_out])
            P.barrier()

    def zero_dram(self, ap, rres, bf=False):
        P = self.P
        z = self.zero_b if bf else self.zero_f
        rows, cols = ap.shape
        for r0 in range(0, rows, 128):
            rn = min(128, rows - r0)
            for c0 in range(0, cols, 512):
                cn = min(512, cols - c0)
                P.dma("sp", lambda e, r0=r0, rn=rn, c0=c0, cn=cn: e.dma_start(
                    out=ap[r0:r0 + rn, c0:c0 + cn], in_=z[0:rn, 0:cn]), reads=[self.r_const], writes=[rres])


    def deltanet(self, l):
        P, nc, T, d, rd = self.P, self.nc, self.T, self.d, self.rd
        R = P.res
        NT = T // 512
        NCP = T // 128
        self.zero_dram(d["dpre"][:, 0:2], rd["dpre"])
        self.zero_dram(d["dpre"][:, T + 2:T + 4], rd["dpre"])
        with contextlib.ExitStack() as st:
            cwd = P.sbuf("cwd", [128, 5, 12], F32, st)
            xin = [P.sbuf("xin", [128, 516], F32, st) for _ in range(2)]
            acc = [P.sbuf("acc", [128, 512], F32, st) for _ in range(2)]
            sl = [P.sbuf("sl", [128, 512], F32, st) for _ in range(2)]
            sq = [P.sbuf("sq", [128, 512], F32, st) for _ in range(2)]
            rs = [P.sbuf("rs", [128, 512], F32, st) for _ in range(2)]
            ob = [P.sbuf("ob", [128, 512], BF16, st) for _ in range(2)]
            tmo = [P.sbuf("tmo", [128, 4, 128], BF16, st) for _ in range(2)]
            ps = [P.psum("ps", [128, 512], F32, st) for _ in range(2)]
            pst = [P.psum("pst", [128, 4, 128], BF16, st) for _ in range(2)]
            r_cwd = R()
            r_xin, r_acc, r_sl, r_sq, r_rs, r_ob, r_tmo, r_ps, r_pst = ([R(), R()] for _ in range(9))
            for tap in range(5):
                P.dma("sp", lambda e, tap=tap: e.dma_start(
                    out=cwd[:, tap, :], in_=d["dn_conv_w"][l][tap, :].rearrange("(m p) -> p m", p=128),
                    allow_slow_non_contiguous=True), writes=[r_cwd])
            it = 0
            for m in range(12):
                for ti in range(NT):
                    b = it % 2
                    it += 1
                    c0 = ti * 512
                    P.dma("sp", lambda e, b=b, m=m, c0=c0: e.dma_start(
                        out=xin[b][:], in_=d["dpre"][m * 128:(m + 1) * 128, c0:c0 + 516]),
                        reads=[rd["dpre"]], writes=[r_xin[b]])
                    P.op("act", lambda e, b=b, m=m: e.activation(out=acc[b][:], in_=xin[b][:, 0:512], func=AF.Copy,
                                                                 scale=cwd[:, 0, m:m + 1]),
                         reads=[r_xin[b], r_cwd], writes=[r_acc[b]])
                    for tap in range(1, 5):
                        P.op("dve", lambda e, b=b, m=m, tap=tap: e.scalar_tensor_tensor(
                            out=acc[b][:], in0=xin[b][:, tap:tap + 512], scalar=cwd[:, tap, m:m + 1], in1=acc[b][:],
                            op0=ALU.mult, op1=ALU.add), reads=[r_xin[b], r_cwd, r_acc[b]], writes=[r_acc[b]])
                    P.op("act", lambda e, b=b: e.activation(out=sl[b][:], in_=acc[b][:], func=AF.Silu),
                         reads=[r_acc[b]], writes=[r_sl[b]])
                    if m < 8:
                        P.op("act", lambda e, b=b: e.activation(out=sq[b][:], in_=sl[b][:], func=AF.Square),
                             reads=[r_sl[b]], writes=[r_sq[b]])
                        P.op("pe", lambda e, b=b: e.matmul(ps[b][:], lhsT=self.blk_f, rhs=sq[b][:], start=True, stop=True),
                             reads=[r_sq[b], self.r_const], writes=[r_ps[b]])
                        P.op("act", lambda e, b=b: e.activation(out=rs[b][:], in_=ps[b][:], func=AF.Sqrt,
                                                                bias=self.eps_c[:], scale=1.0),
                             reads=[r_ps[b], self.r_const], writes=[r_rs[b]])
                        P.op("dve", lambda e, b=b: e.reciprocal(out=rs[b][:], in_=rs[b][:]), reads=[r_rs[b]], writes=[r_rs[b]])
                        scl = 0.125 if m < 4 else 1.0
                        P.op("dve", lambda e, b=b, scl=scl: e.scalar_tensor_tensor(
                            out=ob[b][:], in0=sl[b][:], scalar=scl, in1=rs[b][:], op0=ALU.mult, op1=ALU.mult),
                            reads=[r_sl[b], r_rs[b]], writes=[r_ob[b]])
                        dst, rdst = (d["dqT"], rd["dqT"]) if m < 4 else (d["dkT"], rd["dkT"])
                        P.dma("sp", lambda e, b=b, m=m, c0=c0, dst=dst: e.dma_start(
                            out=dst[(m % 4) * 128:(m % 4 + 1) * 128, c0:c0 + 512], in_=ob[b][:]),
                            reads=[r_ob[b]], writes=[rdst])
                    else:
                        P.op("pool", lambda e, b=b: e.tensor_copy(out=ob[b][:], in_=sl[b][:]), reads=[r_sl[b]],
                             writes=[r_ob[b]])
                    if m >= 4:
                        for sub in range(4):
                            P.op("pe", lambda e, b=b, sub=sub: e.transpose(
                                out=pst[b][:, sub, :], in_=ob[b][:, sub * 128:(sub + 1) * 128], identity=self.identb[:]),
                                reads=[r_ob[b], self.r_const], writes=[r_pst[b]])
                        P.op("act", lambda e, b=b: e.copy(out=tmo[b][:], in_=pst[b][:]), reads=[r_pst[b]], writes=[r_tmo[b]])
                        dst, rdst = (d["dk_tm"], rd["dk_tm"]) if m < 8 else (d["dv_tm"], rd["dv_tm"])
                        mc = (m % 4) * 128
                        P.dma("sp", lambda e, b=b, mc=mc, c0=c0, dst=dst: e.dma_start(
                            out=dst[c0:c0 + 512, mc:mc + 128].rearrange("(s p) c -> p s c", p=128), in_=tmo[b][:]),
                            reads=[r_tmo[b]], writes=[rdst])
            P.barrier()
        import os
        DN_STOP = os.environ.get("DN_STOP", "")
        if DN_STOP == "A":
            return
        with contextlib.ExitStack() as st:
            bdl = P.sbuf("bdl", [128, NCP, 32], F32, st)
            tmp16 = P.sbuf("tmp16", [128, NCP, 16], F32, st)
            dtb = P.sbuf("dtb", [128, 16], F32, st)
            nea = P.sbuf("nea", [128, 16], F32, st)
            names = ("beta", "nbeta", "g", "gc", "ngc", "be", "e2")
            ga = {n: P.sbuf(n, [128, 2, NCP, 8], F32, st) for n in names}
            r_gate = R()
            pq = [P.psum("pq", [128, 4, 128], F32, st) for _ in range(5)]
            psg = [pq[i][:].rearrange("p a d -> p (a d)") for i in range(2)]
            r_psg = [R(), R()]
            P.dma("sp", lambda e: e.dma_start(out=bdl[:], in_=d["bd"].rearrange("(cp p) c -> p cp c", p=128)),
                  reads=[rd["bd"]], writes=[r_gate])
            P.dma("sp", lambda e: e.dma_start(
                out=dtb[:], in_=d["dn_dt_bias"][l].rearrange("a h -> (a h)").partition_broadcast(128)), writes=[r_gate])
            P.dma("sp", lambda e: e.dma_start(
                out=nea[:], in_=d["dn_a_log"][l].rearrange("a h -> (a h)").partition_broadcast(128)), writes=[r_gate])
            G = [r_gate]
            P.op("act", lambda e: e.activation(out=nea[:], in_=nea[:], func=AF.Exp), reads=G, writes=G)
            P.op("dve", lambda e: e.tensor_scalar(out=nea[:], in0=nea[:], scalar1=-1.0, scalar2=None, op0=ALU.mult),
                 reads=G, writes=G)
            P.op("dve", lambda e: e.tensor_tensor(out=tmp16[:], in0=bdl[:, :, 16:32],
                                                  in1=dtb[:].unsqueeze(1).broadcast_to([128, NCP, 16]), op=ALU.add),
                 reads=G, writes=G)
            P.op("act", lambda e: e.activation(out=tmp16[:], in_=tmp16[:], func=AF.Exp), reads=G, writes=G)
            P.op("act", lambda e: e.activation(out=tmp16[:], in_=tmp16[:], func=AF.Ln, bias=self.ones_f[:, 0:1], scale=1.0),
                 reads=G + [self.r_const], writes=G)
            for dr in range(2):
                P.op("dve", lambda e, dr=dr: e.tensor_tensor(
                    out=ga["g"][:, dr], in0=tmp16[:, :, dr * 8:(dr + 1) * 8],
                    in1=nea[:, dr * 8:(dr + 1) * 8].unsqueeze(1).broadcast_to([128, NCP, 8]), op=ALU.mult),
                    reads=G, writes=G)
                P.op("act", lambda e, dr=dr: e.activation(out=ga["beta"][:, dr], in_=bdl[:, :, dr * 8:(dr + 1) * 8],
                                                          func=AF.Sigmoid), reads=G, writes=G)
            P.op("dve", lambda e: e.tensor_scalar(out=ga["nbeta"][:], in0=ga["beta"][:], scalar1=-1.0, scalar2=None,
                                                  op0=ALU.mult), reads=G, writes=G)
            NB = NCP * 8
            for dr in range(2):
                gflat = ga["g"][:, dr].rearrange("p c h -> p (c h)")
                gcflat = ga["gc"][:, dr].rearrange("p c h -> p (c h)")
                e2flat = ga["e2"][:, dr].rearrange("p c h -> p (c h)")
                for c0 in range(0, NB, 512):
                    cn = min(512, NB - c0)
                    P.op("pe", lambda e, dr=dr, c0=c0, cn=cn, gflat=gflat: e.matmul(
                        psg[0][:, 0:cn], lhsT=self.tri[dr], rhs=gflat[:, c0:c0 + cn], start=True, stop=True),
                        reads=G + [self.r_const], writes=[r_psg[0]])
                    P.op("act", lambda e, c0=c0, cn=cn, gcflat=gcflat: e.copy(out=gcflat[:, c0:c0 + cn], in_=psg[0][:, 0:cn]),
                         reads=[r_psg[0]], writes=G)
                    P.op("pe", lambda e, c0=c0, cn=cn, gflat=gflat: e.matmul(
                        psg[1][:, 0:cn], lhsT=self.blk_f, rhs=gflat[:, c0:c0 + cn], start=True, stop=True),
                        reads=G + [self.r_const], writes=[r_psg[1]])
                    P.op("dve", lambda e, c0=c0, cn=cn, gcflat=gcflat, e2flat=e2flat: e.tensor_tensor(
                        out=e2flat[:, c0:c0 + cn], in0=psg[1][:, 0:cn], in1=gcflat[:, c0:c0 + cn], op=ALU.subtract),
                        reads=[r_psg[1]] + G, writes=G)
            P.op("act", lambda e: e.activation(out=ga["e2"][:], in_=ga["e2"][:], func=AF.Exp), reads=G, writes=G)
            P.op("dve", lambda e: e.tensor_scalar(out=ga["ngc"][:], in0=ga["gc"][:], scalar1=-1.0, scalar2=None,
                                                  op0=ALU.mult), reads=G, writes=G)
            P.op("act", lambda e: e.activation(out=ga["be"][:], in_=ga["gc"][:], func=AF.Exp), reads=G, writes=G)
            P.op("pool", lambda e: e.tensor_tensor(out=ga["be"][:], in0=ga["be"][:], in1=ga["beta"][:], op=ALU.mult),
                 reads=G, writes=G)

            def sb2(name, shape, dt, n=2):
                return [P.sbuf(name, shape, dt, st) for _ in range(n)], [R() for _ in range(n)]
            kq, r_kq = sb2("kq", [128, 2, 128], BF16, 3)
            ktm, r_ktm = sb2("ktm", [128, 2, 64], BF16, 3)
            vtm, r_vtm = sb2("vtm", [128, 2, 64], BF16, 3)
            G2, r_G2 = sb2("G2", [128, 2, 64], F32)
            Eg, r_Eg = sb2("Eg", [128, 128], F32)
            qdec, r_qdec = sb2("qdec", [128, 128], BF16)
            gtc, r_gtc = sb2("gtc", [128, 2], F32)
            vb_, r_vb = sb2("vb", [128, 2, 64], BF16)
            kbe, r_kbe = sb2("kbe", [128, 2, 64], BF16)
            kdec, r_kdec = sb2("kdec", [128, 2, 64], BF16)
            G1, r_G1 = sb2("G1", [128, 128], F32)
            G1n, r_G1n = sb2("G1n", [128, 128], F32)
            bcol, r_bcol = sb2("bcol", [128, 16], F32)
            Dm, r_Dm = sb2("Dm", [128, 128], F32)
            DmT, r_DmT = sb2("DmT", [128, 128], F32)
            Nn, r_Nn = sb2("Nn", [128, 128], BF16, 4)
            Mm, r_Mm = sb2("Mm", [128, 128], BF16, 4)
            TT, r_TT = sb2("TT", [128, 128], BF16)
            qkT, r_qkT = sb2("qkT", [128, 128], BF16, 4)
            u_, r_u = sb2("u", [128, 128], F32)
            wT2, r_wT2 = sb2("wT2", [128, 128], BF16)
            vn, r_vn = sb2("vn", [128, 128], BF16)
            osb, r_osb = sb2("osb", [128, 128], F32, 3)
            tS, r_tS = sb2("tS", [128, 128], F32)
            Sf = P.sbuf("Sf", [128, 128], F32, st)
            Sb = P.sbuf("Sb", [128, 128], BF16, st)
            r_S, r_Sb = R(), R()
            P.barrier()
            r_pq = [[R() for _ in range(4)] for _ in range(5)]
            ptb = P.psum("ptb", [128, 4, 128], BF16, st)
            r_ptb = [R() for _ in range(4)]
            psc = P.psum("psc", [128, 4, 128], F32, st)
            r_psc = [R() for _ in range(4)]
            cq = {}

            def scan_ps(kind):
                i = {"V": 0, "O": 1, "S": 2}[kind]
                return psc[:, i, :], r_psc[0]

            def PQ(kind, nb, bank):
                i = cq.get(kind, 0)
                cq[kind] = i + 1
                qn_ = nb[i % len(nb)]
                return pq[bank][:, qn_, :], r_pq[bank][0]

            cnt = {}

            def rot(name, n):
                i = cnt.get(name, 0)
                cnt[name] = i + 1
                return i % n

            CUT = os.environ.get("DN_CUT", "")
            ACTV = os.environ.get("ACTV", "")

            def ACTKW(b):
                if ACTV == "v1":
                    return dict()
                if ACTV == "v2":
                    return dict(bias=b)
                return dict(bias=b)

            class _Stop(Exception):
                pass

            def cut(tag):
                if CUT == tag:
                    raise _Stop()

            for hp in range(0 if DN_STOP == "B" else 4):
              try:
                  for dr in range(2):
                      P.op("pool", lambda e: e.memset(Sf[:], 0.0), writes=[r_S])
                      P.op("pool", lambda e: e.memset(Sb[:], 0.0), writes=[r_Sb])
                      odst, rodst = (d["o_f"], rd["o_f"]) if dr == 0 else (d["o_b"], rd["o_b"])
                      for step in range(NCP):
                          cp = step if dr == 0 else NCP - 1 - step
                          t0 = cp * 128
                          i3 = rot("ld", 3)
                          P.dma("sp", lambda e, i3=i3, hp=hp, t0=t0: e.dma_start(
                              out=kq[i3][:, 0, :], in_=d["dkT"][hp * 128:(hp + 1) * 128, t0:t0 + 128]),
                              reads=[rd["dkT"]], writes=[r_kq[i3]])
                          P.dma("sp", lambda e, i3=i3, hp=hp, t0=t0: e.dma_start(
                              out=kq[i3][:, 1, :], in_=d["dqT"][hp * 128:(hp + 1) * 128, t0:t0 + 128]),
                              reads=[rd["dqT"]], writes=[r_kq[i3]])
                          P.dma("sp", lambda e, i3=i3, hp=hp, t0=t0: e.dma_start(
                              out=ktm[i3][:].rearrange("p a d -> p (a d)"), in_=d["dk_tm"][t0:t0 + 128, hp * 128:(hp + 1) * 128]),
                              reads=[rd["dk_tm"]], writes=[r_ktm[i3]])
                          P.dma("sp", lambda e, i3=i3, hp=hp, t0=t0: e.dma_start(
                              out=vtm[i3][:].rearrange("p a d -> p (a d)"), in_=d["dv_tm"][t0:t0 + 128, hp * 128:(hp + 1) * 128]),
                              reads=[rd["dv_tm"]], writes=[r_vtm[i3]])
                          kT2 = kq[i3][:, 0, :]
                          qT2 = kq[i3][:, 1, :]
                          i2 = rot("u", 2)

                          def col(name, h0, n=2, dr=dr, cp=cp):
                              return ga[name][:, dr, cp, h0:h0 + n]
                          P.op("dve", lambda e, i2=i2, c=col("g", 2 * hp): e.tensor_copy(
                              out=G2[i2][:], in_=c.unsqueeze(2).broadcast_to([128, 2, 64])), reads=G, writes=[r_G2[i2]])
                          G2f = G2[i2][:].rearrange("p a d -> p (a d)")
                          pE, rpE = PQ("E", [0, 1], 0)
                          P.op("pe", lambda e, pE=pE, G2f=G2f, dr=dr: e.matmul(pE, lhsT=G2f, rhs=self.tri[dr], start=True, stop=True),
                               reads=[r_G2[i2], self.r_const], writes=[rpE])
                          P.op("act", lambda e, pE=pE, i2=i2: e.activation(out=Eg[i2][:], in_=pE, func=AF.Exp),
                               reads=[rpE], writes=[r_Eg[i2]])
                          P.op("dve", lambda e, i2=i2, qT2=qT2: e.tensor_tensor(out=qdec[i2][:], in0=qT2, in1=Eg[i2][:], op=ALU.mult),
                               reads=[r_kq[i3], r_Eg[i2]], writes=[r_qdec[i2]])
                          pG, rpG = PQ("G", [2, 3], 0)
                          P.op("pe", lambda e, pG=pG, G2f=G2f: e.matmul(pG[:, 0:2], lhsT=G2f, rhs=self.selc, start=True, stop=True),
                               reads=[r_G2[i2], self.r_const], writes=[rpG])
                          P.op("act", lambda e, pG=pG, i2=i2: e.activation(out=gtc[i2][:], in_=pG[:, 0:2], func=AF.Exp),
                               reads=[rpG], writes=[r_gtc[i2]])
                          for (dst_, rdst_, src_, rsrc_, cname) in ((vb_, r_vb, vtm, r_vtm, "beta"), (kbe, r_kbe, ktm, r_ktm, "be"),
                                                                    (kdec, r_kdec, ktm, r_ktm, "e2")):
                              P.op("dve", lambda e, i2=i2, i3=i3, dst_=dst_, src_=src_, c=col(cname, 2 * hp): e.tensor_tensor(
                                  out=dst_[i2][:], in0=src_[i3][:], in1=c.unsqueeze(2).broadcast_to([128, 2, 64]), op=ALU.mult),
                                  reads=[rsrc_[i3]] + G, writes=[rdst_[i2]])
                          cut("C1")
                          pU, rpU = PQ("U", [0, 1], 3)
                          qk_idx = []
                          for a in range(2):
                              h = 2 * hp + a
                              pa = slice(64 * a, 64 * a + 64)
                              ia = rot("a", 2)
                              P.op("dve", lambda e, ia=ia, c=col("g", h, 1): e.tensor_copy(
                                  out=G1[ia][:], in_=c.broadcast_to([128, 128])), reads=G, writes=[r_G1[ia]])
                              P.op("dve", lambda e, ia=ia, c=col("gc", h, 1): e.tensor_copy(out=bcol[ia][:, 0:1], in_=c),
                                   reads=G, writes=[r_bcol[ia]])
                              P.op("dve", lambda e, ia=ia, c=col("ngc", h, 1): e.tensor_copy(out=bcol[ia][:, 8:9], in_=c),
                                   reads=G, writes=[r_bcol[ia]])
                              cut("C1a")
                              pA, rpA = PQ("A", [0, 1], 1)
                              P.op("dve", lambda e, ia=ia, c=col("g", h, 1): e.tensor_scalar(
                                  out=G1n[ia][:], in0=c.broadcast_to([128, 128]), scalar1=-1.0, scalar2=None, op0=ALU.mult),
                                  reads=G, writes=[r_G1n[ia]])
                              P.op("pe", lambda e, pA=pA, ia=ia, dr=dr: e.matmul(pA, lhsT=G1n[ia][:], rhs=self.tri[dr], start=True, stop=False),
                                   reads=[r_G1n[ia], self.r_const], writes=[rpA])
                              P.op("pe", lambda e, pA=pA, dr=dr: e.matmul(pA, lhsT=self.identf, rhs=self.maskS[dr], start=False, stop=True),
                                   reads=[self.r_const], writes=[rpA])
                              cut("C1b")
                              P.op("act", lambda e, pA=pA, ia=ia, c=col("gc", h, 1): e.activation(
                                  out=Dm[ia][:], in_=pA, func=AF.Exp, **ACTKW(bcol[ia][:, 0:1])), reads=[rpA, r_bcol[ia]], writes=[r_Dm[ia]])
                              cut("C1c")
                              pB, rpB = PQ("B", [2, 3], 1)
                              P.op("pe", lambda e, pB=pB, ia=ia, dr=dr: e.matmul(pB, lhsT=(G1n if os.environ.get("BX") == "1" else G1)[ia][:], rhs=self.tri[dr], start=True, stop=False),
                                   reads=[r_G1[ia], r_G1n[ia], self.r_const], writes=[rpB])
                              P.op("pe", lambda e, pB=pB, dr=dr: e.matmul(pB, lhsT=self.identf, rhs=self.maskI[dr], start=False, stop=True),
                                   reads=[self.r_const], writes=[rpB])
                              cut("C1c2")
                              P.op("act", lambda e, pB=pB, ia=ia, c=col("ngc", h, 1): e.activation(
                                  out=DmT[ia][:], in_=pB, func=AF.Exp, bias=bcol[ia][:, 8:9]), reads=[rpB, r_bcol[ia]], writes=[r_DmT[ia]])
                              cut("C1d")
                              pK, rpK = PQ("K", [0, 1], 2)
                              P.op("pe", lambda e, pK=pK, kT2=kT2, pa=pa: e.matmul(pK, lhsT=kT2[pa, :], rhs=kT2[pa, :], start=True, stop=True),
                                   reads=[r_kq[i3]], writes=[rpK])
                              n0 = rot("N", 4)
                              P.op("dve", lambda e, pK=pK, n0=n0, ia=ia, c=col("nbeta", h, 1): e.scalar_tensor_tensor(
                                  out=Nn[n0][:], in0=pK, scalar=c, in1=Dm[ia][:], op0=ALU.mult, op1=ALU.mult),
                                  reads=[rpK, r_Dm[ia]] + G, writes=[r_Nn[n0]])
                              cut("C1e")
                              pQ_, rpQ = PQ("Q", [2, 3], 2)
                              P.op("pe", lambda e, pQ_=pQ_, kT2=kT2, qT2=qT2, pa=pa: e.matmul(pQ_, lhsT=kT2[pa, :], rhs=qT2[pa, :],
                                                                                            start=True, stop=True),
                                   reads=[r_kq[i3]], writes=[rpQ])
                              iq = rot("qk", 4)
                              qk_idx.append(iq)
                              P.op("dve", lambda e, pQ_=pQ_, iq=iq, ia=ia: e.tensor_tensor(out=qkT[iq][:], in0=pQ_, in1=DmT[ia][:], op=ALU.mult),
                                   reads=[rpQ, r_DmT[ia]], writes=[r_qkT[iq]])
                              cut("C2")
                              tq = rot("tb", 4)
                              P.op("pe", lambda e, tq=tq, n0=n0: e.transpose(out=ptb[:, tq, :], in_=Nn[n0][:], identity=self.identb[:]),
                                   reads=[r_Nn[n0], self.r_const], writes=[r_ptb[0]])
                              m0 = rot("M", 4)
                              P.op("act", lambda e, tq=tq, m0=m0: e.copy(out=Mm[m0][:], in_=ptb[:, tq, :]),
                                   reads=[r_ptb[0]], writes=[r_Mm[m0]])
                              P.op("pool", lambda e, ia=ia, m0=m0: e.tensor_tensor(out=TT[ia][:], in0=Mm[m0][:], in1=self.identb[:], op=ALU.add),
                                   reads=[r_Mm[m0], self.r_const], writes=[r_TT[ia]])
                              nk, mk = n0, m0
                              for k in range(1, 6):
                                  pN, rpN = PQ("N", [0, 1], 4)
                                  P.op("pe", lambda e, pN=pN, nk=nk, mk=mk: e.matmul(pN, lhsT=Mm[mk][:], rhs=Nn[nk][:], start=True, stop=True),
                                       reads=[r_Mm[mk], r_Nn[nk]], writes=[rpN])
                                  n1 = rot("N", 4)
                                  P.op("act", lambda e, pN=pN, n1=n1: e.copy(out=Nn[n1][:], in_=pN), reads=[rpN], writes=[r_Nn[n1]])
                                  m1 = mk
                                  if k < 5:
                                      pM, rpM = PQ("Mq", [2], 4)
                                      P.op("pe", lambda e, pM=pM, nk=nk, mk=mk: e.matmul(pM, lhsT=Nn[nk][:], rhs=Mm[mk][:], start=True, stop=True),
                                           reads=[r_Mm[mk], r_Nn[nk]], writes=[rpM])
                                      m1 = rot("M", 4)
                                      P.op("dve", lambda e, pM=pM, m1=m1: e.tensor_copy(out=Mm[m1][:], in_=pM), reads=[rpM], writes=[r_Mm[m1]])
                                  pT_, rpT = PQ("T", [3], 4)
                                  P.op("pe", lambda e, pT_=pT_, n1=n1, ia=ia: e.matmul(pT_, lhsT=Nn[n1][:], rhs=TT[ia][:], start=True, stop=True),
                                       reads=[r_Nn[n1], r_TT[ia]], writes=[rpT])
                                  P.op("dve", lambda e, pT_=pT_, ia=ia: e.tensor_tensor(out=TT[ia][:], in0=pT_, in1=TT[ia][:], op=ALU.add),
                                       reads=[rpT, r_TT[ia]], writes=[r_TT[ia]])
                                  nk, mk = n1, m1
                              cut("C3")
                              P.op("pe", lambda e, pU=pU, ia=ia, i2=i2, a=a: e.matmul(pU[:, 64 * a:64 * a + 64], lhsT=TT[ia][:], rhs=vb_[i2][:, a, :],
                                                                                     start=True, stop=True),
                                   reads=[r_TT[ia], r_vb[i2]], writes=[rpU])
                              pW, rpW = PQ("W", [2, 3], 3)
                              P.op("pe", lambda e, pW=pW, ia=ia, i2=i2: e.matmul(pW, lhsT=kbe[i2][:].rearrange("p a d -> p (a d)"), rhs=TT[ia][:],
                                                                                start=True, stop=True),
                                   reads=[r_TT[ia], r_kbe[i2]], writes=[rpW])
                              P.op("act", lambda e, pW=pW, i2=i2, pa=pa: e.copy(out=wT2[i2][pa, :], in_=pW[pa, :]),
                                   reads=[rpW], writes=[r_wT2[i2]])
                          P.op("act", lambda e, pU=pU, i2=i2: e.copy(out=u_[i2][:], in_=pU), reads=[rpU], writes=[r_u[i2]])
                          cut("C4")
                          for c2 in ((0, 1) if dr == 0 else (1, 0)):
                              ch = slice(64 * c2, 64 * c2 + 64)
                              pV, rpV = PQ("sV", [0, 1], 2) if False else scan_ps("V")
                              P.op("pe", lambda e, pV=pV, i2=i2: e.matmul(pV, lhsT=wT2[i2][:], rhs=Sb[:], start=True, stop=True),
                                   reads=[r_wT2[i2], r_Sb], writes=[rpV])
                              iv = rot("vn", 2)
                              P.op("dve", lambda e, pV=pV, i2=i2, iv=iv, ch=ch: e.tensor_tensor(out=vn[iv][ch, :], in0=u_[i2][ch, :], in1=pV[ch, :],
                                                                                             op=ALU.subtract),
                                   reads=[rpV, r_u[i2]], writes=[r_vn[iv]])
                              pO, rpO = scan_ps("O")
                              P.op("pe", lambda e, pO=pO, i2=i2: e.matmul(pO, lhsT=qdec[i2][:], rhs=Sb[:], start=True, stop=False),
                                   reads=[r_qdec[i2], r_Sb], writes=[rpO])
                              for a in range(2):
                                  P.op("pe", lambda e, pO=pO, a=a, iv=iv, ch=ch, iq=qk_idx[a]: e.matmul(
                                      pO[:, 64 * a:64 * a + 64], lhsT=qkT[iq][ch, :], rhs=vn[iv][ch, 64 * a:64 * a + 64],
                                      start=False, stop=(a == 1)), reads=[r_qkT[qk_idx[a]], r_vn[iv]], writes=[rpO])
                              io = rot("o", 3)
                              P.op("act", lambda e, pO=pO, io=io, ch=ch: e.copy(out=osb[io][ch, :], in_=pO[ch, :]),
                                   reads=[rpO], writes=[r_osb[io]])
                              P.dma("sp", lambda e, io=io, ch=ch, t0=t0, c2=c2, hp=hp, odst=odst: e.dma_start(
                                  out=odst[t0 + 64 * c2:t0 + 64 * c2 + 64, hp * 128:(hp + 1) * 128], in_=osb[io][ch, :]),
                                  reads=[r_osb[io]], writes=[rodst])
                              pS, rpS = scan_ps("S")
                              P.op("pe", lambda e, pS=pS, i2=i2, iv=iv, ch=ch: e.matmul(
                                  pS, lhsT=kdec[i2][ch].rearrange("p a d -> p (a d)"), rhs=vn[iv][ch, :], start=True, stop=True),
                                  reads=[r_kdec[i2], r_vn[iv]], writes=[rpS])
                              its = rot("tS", 2)
                              P.op("dve", lambda e, pS=pS, its=its: e.tensor_tensor(out=tS[its][:], in0=pS, in1=self.blk_f, op=ALU.mult),
                                   reads=[rpS, self.r_const], writes=[r_tS[its]])
                              P.op("dve", lambda e, its=its, i2=i2, c2=c2: e.scalar_tensor_tensor(
                                  out=Sf[:], in0=Sf[:], scalar=gtc[i2][:, c2:c2 + 1], in1=tS[its][:], op0=ALU.mult, op1=ALU.add),
                                  reads=[r_S, r_gtc[i2], r_tS[its]], writes=[r_S])
                              P.op("pool", lambda e: e.tensor_copy(out=Sb[:], in_=Sf[:]), reads=[r_S], writes=[r_Sb])
              except _Stop:
                break
            P.barrier()
        if DN_STOP in ("B", "C"):
            return
        with contextlib.ExitStack() as st:
            gng = P.sbuf("gng", [128, 64], F32, st)
            of_ = [P.sbuf("of", [128, 512], F32, st) for _ in range(2)]
            ob_ = [P.sbuf("obk", [128, 512], F32, st) for _ in range(2)]
            zt = [P.sbuf("zt", [128, 512], F32, st) for _ in range(2)]
            sq = [P.sbuf("sqd", [128, 512], F32, st) for _ in range(2)]
            ss = [P.sbuf("ss", [128, 8], F32, st) for _ in range(2)]
            obf = [P.sbuf("obf", [128, 512], BF16, st) for _ in range(2)]
            oT = [P.sbuf("oT", [128, 4, 128], BF16, st) for _ in range(2)]
            pst = [P.psum("pstd", [128, 4, 128], BF16, st) for _ in range(2)]
            r_gng = R()
            r_of, r_ob, r_zt, r_sq, r_ss, r_obf, r_oT, r_pst = ([R(), R()] for _ in range(8))
            P.dma("sp", lambda e: e.dma_start(out=gng[:], in_=d["dn_out_norm_g"][l].partition_broadcast(128)), writes=[r_gng])
            odT = d["odT"].rearrange("(kt p) n -> p kt n", p=128)
            for tt in range(NCP):
                b = tt % 2
                t0 = tt * 128
                P.dma("sp", lambda e, b=b, t0=t0: e.dma_start(out=of_[b][:], in_=d["o_f"][t0:t0 + 128, :]), reads=[rd["o_f"]], writes=[r_of[b]])
                P.dma("sp", lambda e, b=b, t0=t0: e.dma_start(out=ob_[b][:], in_=d["o_b"][t0:t0 + 128, :]), reads=[rd["o_b"]], writes=[r_ob[b]])
                P.dma("sp", lambda e, b=b, t0=t0: e.dma_start(out=zt[b][:], in_=d["z"][t0:t0 + 128, :]), reads=[rd["z"]], writes=[r_zt[b]])
                P.op("pool", lambda e, b=b: e.tensor_tensor(out=of_[b][:], in0=of_[b][:], in1=ob_[b][:], op=ALU.add),
                     reads=[r_of[b], r_ob[b]], writes=[r_of[b]])
                P.op("act", lambda e, b=b: e.activation(out=sq[b][:], in_=of_[b][:], func=AF.Square), reads=[r_of[b]], writes=[r_sq[b]])
                P.op("dve", lambda e, b=b: e.tensor_reduce(out=ss[b][:], in_=sq[b][:].rearrange("p (h d) -> p h d", h=8),
                                                           axis=mybir.AxisListType.X, op=ALU.add), reads=[r_sq[b]], writes=[r_ss[b]])
                P.op("act", lambda e, b=b: e.activation(out=ss[b][:], in_=ss[b][:], func=AF.Sqrt, bias=self.eps_c[:], scale=1.0 / 64),
                     reads=[r_ss[b], self.r_const], writes=[r_ss[b]])
                P.op("dve", lambda e, b=b: e.reciprocal(out=ss[b][:], in_=ss[b][:]), reads=[r_ss[b]], writes=[r_ss[b]])
                P.op("dve", lambda e, b=b: e.tensor_tensor(
                    out=of_[b][:].rearrange("p (h d) -> p h d", h=8), in0=of_[b][:].rearrange("p (h d) -> p h d", h=8),
                    in1=ss[b][:].unsqueeze(2).broadcast_to([128, 8, 64]), op=ALU.mult), reads=[r_of[b], r_ss[b]], writes=[r_of[b]])
                P.op("dve", lambda e, b=b: e.tensor_tensor(
                    out=of_[b][:].rearrange("p (h d) -> p h d", h=8), in0=of_[b][:].rearrange("p (h d) -> p h d", h=8),
                    in1=gng[:].unsqueeze(1).broadcast_to([128, 8, 64]), op=ALU.mult), reads=[r_of[b], r_gng], writes=[r_of[b]])
                P.op("act", lambda e, b=b: e.activation(out=zt[b][:], in_=zt[b][:], func=AF.Silu), reads=[r_zt[b]], writes=[r_zt[b]])
                P.op("dve", lambda e, b=b: e.tensor_tensor(out=obf[b][:], in0=of_[b][:], in1=zt[b][:], op=ALU.mult),
                     reads=[r_of[b], r_zt[b]], writes=[r_obf[b]])
                for kt in range(4):
                    P.op("pe", lambda e, b=b, kt=kt: e.transpose(out=pst[b][:, kt, :], in_=obf[b][:, kt * 128:(kt + 1) * 128],
                                                                 identity=self.identb[:]),
                         reads=[r_obf[b], self.r_const], writes=[r_pst[b]])
                P.op("act", lambda e, b=b: e.copy(out=oT[b][:], in_=pst[b][:]), reads=[r_pst[b]], writes=[r_oT[b]])
                P.dma("sp", lambda e, b=b, t0=t0: e.dma_start(out=odT[:, :, t0:t0 + 128], in_=oT[b][:]),
                      reads=[r_oT[b]], writes=[rd["odT"]])
            P.barrier()


WEIGHT_SPECS = [
    ("norm_mix_g", (DEPTH, D)), ("w_in", (DEPTH, D, NIN)), ("q_norm_g", (DEPTH, 64)), ("k_norm_g", (DEPTH, 64)),
    ("dn_conv_w", (DEPTH, 5, 1536)), ("dn_a_log", (DEPTH, 2, 8)), ("dn_dt_bias", (DEPTH, 2, 8)),
    ("dn_out_norm_g", (DEPTH, 64)), ("w_o_attn", (DEPTH, 512, D)), ("w_o_dn", (DEPTH, 512, D)),
    ("w_out", (DEPTH, D, D)), ("norm_ffn_g", (DEPTH, D)), ("w_up", (DEPTH, D, 2 * DFF)),
    ("ffn_conv_w", (DEPTH, 3, 2 * DFF)), ("w_down", (DEPTH, DFF, D)),
]


def host_consts(T):
    blk = np.zeros((128, 128), np.float32)
    blk[:64, :64] = 1
    blk[64:, 64:] = 1
    rot = np.zeros((128, 128), np.float32)
    for h2 in range(2):
        for ax in range(2):
            for f in range(16):
                m0 = h2 * 64 + ax * 32 + f
                m1 = m0 + 16
                rot[m1, m0] = -1.0
                rot[m0, m1] = 1.0
    ident = np.eye(128, dtype=np.float32)
    p = np.arange(128)[:, None]
    f = np.arange(128)[None, :]
    same = (p // 64) == (f // 64)
    triF = (same & (p <= f)).astype(np.float32)
    triB = (same & (p >= f)).astype(np.float32)
    BIG = 30000.0
    mSf = np.where(same & (p > f), 0.0, -BIG).astype(np.float32)
    mSb = np.where(same & (p < f), 0.0, -BIG).astype(np.float32)
    mIf = np.where(same & (f >= p), 0.0, -BIG).astype(np.float32)
    mIb = np.where(same & (f <= p), 0.0, -BIG).astype(np.float32)
    z = np.zeros((128, 128), np.float32)
    selc = np.zeros((128, 2), np.float32)
    selc[:64, 0] = 1
    selc[64:, 1] = 1
    cf = np.concatenate([blk, rot, ident, triF, triB, mSf, mSb, mIf, mIb, z, z, z, z, selc], axis=1)
    t = np.arange(T)
    row = (t // 64).astype(np.float32)
    col = (t % 64).astype(np.float32)
    inv = (np.float32(10000.0) ** (-np.arange(16, dtype=np.float32) / np.float32(16))).astype(np.float32)
    ang = np.stack([row[:, None] * inv, col[:, None] * inv], axis=1)
    c = np.cos(ang).astype(np.float32)
    sn = np.sin(ang).astype(np.float32)
    C = np.zeros((128, T), np.float32)
    S = np.zeros((128, T), np.float32)
    for h2 in range(2):
        for ax in range(2):
            for half in range(2):
                r0 = h2 * 64 + ax * 32 + half * 16
                C[r0:r0 + 16] = c[:, ax, :].T
                S[r0:r0 + 16] = sn[:, ax, :].T
    return dict(cf32=cf, ropec=C, ropes=S)


def build(T=SEQ, depth=DEPTH, with_dn=True, with_ffn=True, debug=False, stages="padm"):
    nc = bass.Bass("TRN2", target_bir_lowering=False)
    B = Builder(nc, T=T)
    B.debug = debug
    P = B.P
    x0 = B.dram_in("xT", [D, T], F32)
    for name, shp in WEIGHT_SPECS:
        B.dram_in(name, list(shp), F32)
    B.dram_in("cf32", [128, NCF], F32)
    B.dram_in("ropec", [128, T], F32)
    B.dram_in("ropes", [128, T], F32)
    out = B.dram_out("yT", [D, T], F32)
    xa = B.dram_tmp("xa", [D, T], F32)
    xb = B.dram_tmp("xb", [D, T], F32)
    B.dram_tmp("qaT", [512, T], BF16)
    B.dram_tmp("kaT", [256, T], BF16)
    B.dram_tmp("va", [T, 130], BF16)
    B.dram_tmp("dpre", [1536, T + 4], F32)
    B.dram_tmp("bd", [T, 32], F32)
    B.dram_tmp("z", [T, 512], F32)
    B.dram_tmp("gT", [2048, T], BF16)
    B.dram_tmp("oaT", [512, T], BF16)
    B.dram_tmp("odT", [512, T], BF16)
    B.dram_tmp("dqT", [512, T], BF16)
    B.dram_tmp("dkT", [512, T], BF16)
    B.dram_tmp("dk_tm", [T, 512], BF16)
    B.dram_tmp("dv_tm", [T, 512], BF16)
    B.dram_tmp("o_f", [T, 512], F32)
    B.dram_tmp("o_b", [T, 512], F32)
    B.rd = {k: P.res(k) for k in ("qaT", "kaT", "va", "dpre", "bd", "z", "gT", "oaT", "odT", "dqT", "dkT", "dk_tm", "dv_tm",
                                    "o_f", "o_b")}
    rx = {"x0": P.res(), "xa": P.res(), "xb": P.res(), "out": P.res()}
    with P.stack:
        B.load_consts()
        tch = P.sbuf("touch", [1, 16], F32)
        rt = P.res()
        for name, shp in WEIGHT_SPECS:
            ap = B.d[name]
            idx = tuple([0] * (len(shp) - 1))
            P.dma("sp", lambda e, ap=ap, idx=idx: e.dma_start(out=tch[0:1, 0:8], in_=ap[idx][0:8].rearrange("(o n) -> o n", o=1)),
                  writes=[rt])
        import os
        if not with_dn and "zero" not in os.environ.get("SKIP", ""):
            B.zero_dram(B.d["odT"], B.rd["odT"], bf=True)
        P.barrier()
        for l in range(depth):
            src, rsrc = (x0, rx["x0"]) if l == 0 else (xa, rx["xa"])
            last = (l == depth - 1)
            if "p" in stages:
                B.proj(l, src, rsrc)
            if "a" in stages:
                B.attention(l)
            if with_dn and "d" in stages:
                B.deltanet(l)
            if "m" not in stages:
                continue
            if with_ffn:
                B.merge(l, src, xb, rsrc, rx["xb"])
                dst, rdst = (out, rx["out"]) if last else (xa, rx["xa"])
                B.ffn(l, xb, dst, rx["xb"], rdst)
            else:
                B.merge(l, src, out, rsrc, rx["out"])
        P.emit()
        B.nc_stats = P.stats
    return nc, B


_CACHE = {}


def kernel(**inputs):
    T = SEQ
    if "nc" not in _CACHE:
        _CACHE["nc"] = build(T=T, depth=DEPTH)[0]
        _CACHE["consts"] = host_consts(T)
    nc = _CACHE["nc"]
    x = np.asarray(inputs["x"], dtype=np.float32)
    nb = x.shape[0]
    base = {k: np.ascontiguousarray(np.asarray(inputs[k], dtype=np.float32)) for k, _ in WEIGHT_SPECS}
    base.update(_CACHE["consts"])
    in_maps = []
    for c in range(8):
        m = dict(base)
        m["xT"] = np.ascontiguousarray(x[c % nb].T)
        in_maps.append(m)
    res = run_bass_kernel_spmd(nc, in_maps, core_ids=list(range(8)))
    out = np.stack([np.ascontiguousarray(res.results[b]["yT"].T) for b in range(nb)], axis=0)
    return out.astype(np.float32)
```

```python
import contextlib
import numpy as np
import ml_dtypes
import concourse.bass as bass
import concourse.mybir as mybir
from concourse.bass_utils import run_bass_kernel_spmd

F32 = mybir.dt.float32
BF16 = mybir.dt.bfloat16
ALU = mybir.AluOpType
AF = mybir.ActivationFunctionType

D = 1024
SEQ = 8192
DEPTH = 4
NIN = 4896
DFF = 2816
EPS = 1e-6

SEM_WRAP = 6000
DMA_RING = 8
NCF = 13 * 128 + 2


class Res:
    __slots__ = ("name", "last_w", "readers")

    def __init__(self, name):
        self.name = name
        self.last_w = None
        self.readers = []


class Op:
    __slots__ = ("eng", "fn", "deps", "is_dma", "has_dep", "sem", "val", "slot_prev", "barrier")

    def __init__(self, eng, fn, is_dma):
        self.eng = eng
        self.fn = fn
        self.deps = []
        self.is_dma = is_dma
        self.has_dep = False
        self.sem = None
        self.val = 0
        self.slot_prev = None
        self.barrier = False


class Prog:
    ENGS = ("pe", "act", "dve", "pool", "sp")

    def __init__(self, nc):
        self.nc = nc
        self.ops = []
        self.stack = contextlib.ExitStack()
        self.res_all = []
        self.n = 0
        self.last_op = {e: None for e in self.ENGS}
        self.dma_last = {e: {} for e in self.ENGS}
        self.dma_cnt = {e: 0 for e in self.ENGS}
        self.bar_deps = {e: None for e in self.ENGS}

    def sbuf(self, name, shape, dt, stack=None):
        self.n += 1
        t = (stack or self.stack).enter_context(self.nc.sbuf_tensor(f"{name}_{self.n}", list(shape), dt))
        return t

    def psum(self, name, shape, dt, stack=None):
        self.n += 1
        t = (stack or self.stack).enter_context(self.nc.psum_tensor(f"{name}_{self.n}", list(shape), dt))
        return t

    def res(self, name="r"):
        r = Res(name)
        self.res_all.append(r)
        return r

    def _add(self, eng, fn, reads, writes, is_dma):
        o = Op(eng, fn, is_dma)
        deps = []
        seen = set()

        def add(d):
            if d is not None and id(d) not in seen:
                seen.add(id(d))
                deps.append(d)

        for r in reads:
            add(r.last_w)
        for w in writes:
            add(w.last_w)
            for rd in w.readers:
                add(rd)
        for r in reads:
            if not is_dma:
                r.readers = [x for x in r.readers if x.is_dma or x.eng != eng]
            r.readers.append(o)
        for w in writes:
            w.last_w = o
            w.readers = []
        if self.bar_deps[eng] is not None:
            for d in self.bar_deps[eng]:
                add(d)
            self.bar_deps[eng] = None
        if eng == "pe" and not is_dma:
            deps = [d for d in deps if d.is_dma or d.eng != "pe"]
        o.deps = deps
        for d in deps:
            d.has_dep = True
        if is_dma:
            i = self.dma_cnt[eng]
            self.dma_cnt[eng] += 1
            slot = i % DMA_RING
            o.val = 16 * (i // DMA_RING + 1)
            o.sem = (eng, slot)
            o.slot_prev = self.dma_last[eng].get(slot)
            self.dma_last[eng][slot] = o
        else:
            self.last_op[eng] = o
        self.ops.append(o)
        return o

    def op(self, eng, fn, reads=(), writes=()):
        return self._add(eng, fn, reads, writes, False)

    def dma(self, eng, fn, reads=(), writes=()):
        return self._add(eng, fn, reads, writes, True)

    def _tails(self):
        b = [o for o in self.last_op.values() if o is not None]
        for q in self.dma_last.values():
            b.extend(q.values())
        for o in b:
            o.has_dep = True
        return b

    def barrier(self):
        b = self._tails()
        for e in self.ENGS:
            self.bar_deps[e] = list(b)

    def emit(self):
        nc = self.nc
        ops = self.ops
        final = self._tails()
        per_eng = {e: [] for e in self.ENGS}
        cnt = {e: 0 for e in self.ENGS}
        semlist = {e: [] for e in self.ENGS}
        dma_ring = {}
        stack = self.stack

        def new_sem(name):
            return stack.enter_context(nc.semaphore(name))

        for o in ops:
            e = o.eng
            if o.is_dma:
                if o.sem not in dma_ring:
                    dma_ring[o.sem] = new_sem(f"dq_{o.sem[0]}_{o.sem[1]}")
                o.sem = dma_ring[o.sem]
            elif o.has_dep:
                c = cnt[e]
                cnt[e] += 1
                si = c // SEM_WRAP
                if len(semlist[e]) <= si:
                    semlist[e].append(new_sem(f"s_{e}_{si}"))
                o.sem = semlist[e][si]
                o.val = c % SEM_WRAP + 1
            per_eng[e].append(o)
        self.stats = dict(cnt=cnt, dma=dict(self.dma_cnt), nops=len(ops),
                          nsem=sum(len(v) for v in semlist.values()) + len(dma_ring))

        engmap = {"pe": "tensor", "act": "scalar", "dve": "vector", "pool": "gpsimd", "sp": "sync"}

        def run_engine(ename, eng):
            waited = {}

            def wait(sem, val):
                k = id(sem)
                if waited.get(k, 0) >= val:
                    return
                waited[k] = val
                eng.wait_ge(sem, val)

            for o in per_eng[ename]:
                for d in o.deps:
                    if d.sem is None:
                        continue
                    if (not d.is_dma) and d.eng == ename and ename == "pe":
                        continue
                    wait(d.sem, d.val)
                if o.is_dma:
                    if o.slot_prev is not None:
                        wait(o.slot_prev.sem, o.slot_prev.val)
                    o.fn(eng).then_inc(o.sem, 16)
                else:
                    ins = o.fn(eng)
                    if o.sem is not None:
                        ins.then_inc(o.sem, 1)
            for d in final:
                wait(d.sem, d.val)

        with nc.Block() as block:
            for ename in self.ENGS:
                getattr(block, engmap[ename])(lambda eng, ename=ename: run_engine(ename, eng))


def _bf(a):
    return np.ascontiguousarray(a).astype(ml_dtypes.bfloat16)


class Builder:
    def __init__(self, nc, T=SEQ):
        self.nc = nc
        self.T = T
        self.P = Prog(nc)
        self.d = {}
        self.debug = False

    def dram_in(self, name, shape, dt):
        self.d[name] = self.nc.dram_tensor(name, list(shape), dt, kind="ExternalInput").ap()
        return self.d[name]

    def dram_out(self, name, shape, dt):
        self.d[name] = self.nc.dram_tensor(name, list(shape), dt, kind="ExternalOutput").ap()
        return self.d[name]

    def dram_tmp(self, name, shape, dt):
        self.d[name] = self.nc.dram_tensor(name, list(shape), dt,
                                           kind="ExternalOutput" if self.debug else "Internal").ap()
        return self.d[name]

    def load_consts(self):
        P, nc = self.P, self.nc
        self.ones_f = P.sbuf("ones_f", [128, 128], F32)
        r = self.r_const = P.res("const")
        P.op("pool", lambda e: e.memset(self.ones_f[:], 1.0), writes=[r])
        self.eps_c = P.sbuf("eps_c", [128, 1], F32)
        P.op("pool", lambda e: e.memset(self.eps_c[:], EPS), writes=[r])
        self.cf = P.sbuf("cf", [128, NCF], F32)
        P.dma("sp", lambda e: e.dma_start(out=self.cf[:], in_=self.d["cf32"][:, :]), writes=[r])
        self.blk_f = self.cf[:, 0:128]
        self.rot_f = self.cf[:, 128:256]
        c = lambda i: self.cf[:, i * 128:(i + 1) * 128]
        self.identf = c(2)
        self.tri = [c(3), c(4)]
        self.maskS = [c(5), c(6)]
        self.maskI = [c(7), c(8)]
        self.ntri = [c(9), c(10)]
        self.selc = self.cf[:, 13 * 128:13 * 128 + 2]
        self.identb = P.sbuf("identb", [128, 128], BF16)
        P.op("dve", lambda e: e.tensor_copy(out=self.identb[:], in_=self.identf), reads=[r], writes=[r])
        self.zero_f = P.sbuf("zero_f", [128, 512], F32)
        P.op("pool", lambda e: e.memset(self.zero_f[:], 0.0), writes=[r])
        self.zero_b = P.sbuf("zero_b", [128, 512], BF16)
        P.op("pool", lambda e: e.memset(self.zero_b[:], 0.0), writes=[r])

    def ffn(self, l, xin, xout, rx_in, rx_out):
        P, nc, T = self.P, self.nc, self.T
        d = self.d
        W = 510
        ntile = (T + W - 1) // W
        HC = DFF // 2
        NJ = HC // 128
        with contextlib.ExitStack() as st:
            g_sb = P.sbuf("ffn_g", [128, 8], F32, st)
            cw = P.sbuf("ffn_cw", [128, 3, 2 * NJ], F32, st)
            wup = P.sbuf("wup", [128, 8, 2 * HC], BF16, st)
            wdn = P.sbuf("wdn", [128, NJ, D], BF16, st)
            stg = [P.sbuf("stg", [128, HC], F32, st) for _ in range(2)]
            xt = [P.sbuf("xt", [128, 8, 512], F32, st) for _ in range(2)]
            xr = [P.sbuf("xr", [128, 512], F32, st) for _ in range(2)]
            sq = P.sbuf("sq", [128, 8, 512], F32, st)
            rstd = P.sbuf("rstd", [128, 512], F32, st)
            hT = P.sbuf("hT", [128, 8, 512], BF16, st)
            act = P.sbuf("act", [128, NJ, 512], BF16, st)
            tg = [P.sbuf("tg", [128, 512], F32, st) for _ in range(2)]
            tv = [P.sbuf("tv", [128, 512], F32, st) for _ in range(2)]
            sg = [P.sbuf("sg", [128, 512], F32, st) for _ in range(2)]
            xo = [P.sbuf("xo", [128, 512], F32, st) for _ in range(2)]
            ps_ss = P.psum("ps_ss", [128, 512], F32, st)
            ps_g = [P.psum("ps_g", [128, 512], F32, st) for _ in range(2)]
            ps_v = [P.psum("ps_v", [128, 512], F32, st) for _ in range(2)]
            ps_y = [P.psum("ps_y", [128, 512], F32, st) for _ in range(2)]
            R = P.res
            r_g, r_cw, r_wup, r_wdn = R(), R(), R(), R()
            r_stg = [R(), R()]
            r_xt = [R(), R()]
            r_xr = [R(), R()]
            r_sq, r_rstd, r_hT, r_act = R(), R(), R(), R()
            r_tg, r_tv, r_sg, r_xo = [R(), R()], [R(), R()], [R(), R()], [R(), R()]
            r_pss = R()
            r_psg, r_psv, r_psy = [R(), R()], [R(), R()], [R(), R()]

            P.dma("sp", lambda e: e.dma_start(out=g_sb[:], in_=d["norm_ffn_g"][l].rearrange("(kt p) -> p kt", p=128),
                                              allow_slow_non_contiguous=True), writes=[r_g])
            xtiled_in = xin.rearrange("(kt p) n -> p kt n", p=128)
            xtiled_out = xout.rearrange("(kt p) n -> p kt n", p=128)
            for ph in range(2):
                c0 = ph * HC
                for tap in range(3):
                    for half in range(2):
                        P.dma("sp", lambda e, tap=tap, half=half, c0=c0: e.dma_start(
                            out=cw[:, tap, half * NJ:(half + 1) * NJ],
                            in_=d["ffn_conv_w"][l][tap, half * DFF + c0:half * DFF + c0 + HC].rearrange("(m p) -> p m", p=128),
                            allow_slow_non_contiguous=True), writes=[r_cw])
                k = 0
                for kt in range(8):
                    for half in range(2):
                        col = half * DFF + c0
                        s = k % 2
                        k += 1
                        P.dma("sp", lambda e, s=s, kt=kt, col=col: e.dma_start(
                            out=stg[s][:], in_=d["w_up"][l][kt * 128:(kt + 1) * 128, col:col + HC]),
                            writes=[r_stg[s]])
                        P.op("pool", lambda e, s=s, kt=kt, half=half: e.tensor_scalar(
                            out=wup[:, kt, half * HC:(half + 1) * HC], in0=stg[s][:], scalar1=g_sb[:, kt:kt + 1],
                            scalar2=None, op0=ALU.mult), reads=[r_stg[s], r_g], writes=[r_wup])
                for j in range(NJ):
                    s = k % 2
                    k += 1
                    P.dma("sp", lambda e, s=s, j=j, c0=c0: e.dma_start(
                        out=stg[s][:, 0:D], in_=d["w_down"][l][c0 + j * 128:c0 + (j + 1) * 128, :]),
                        writes=[r_stg[s]])
                    P.op("pool", lambda e, s=s, j=j: e.tensor_copy(out=wdn[:, j, :], in_=stg[s][:, 0:D]),
                         reads=[r_stg[s]], writes=[r_wdn])
                for ti in range(ntile):
                    b = ti % 2
                    s0 = ti * W
                    nout = min(W, T - s0)
                    lo = max(s0 - 1, 0)
                    hi = min(s0 + nout + 1, T)
                    off = lo - (s0 - 1)
                    full = (off == 0 and hi - lo == 512)
                    if not full:
                        P.op("pool", lambda e, b=b: e.memset(xt[b][:], 0.0), writes=[r_xt[b]])
                    P.dma("sp", lambda e, b=b, lo=lo, hi=hi, off=off: e.dma_start(
                        out=xt[b][:, :, off:off + hi - lo], in_=xtiled_in[:, :, lo:hi]), reads=[rx_in], writes=[r_xt[b]])
                    P.op("act", lambda e, b=b: e.activation(out=sq[:], in_=xt[b][:], func=AF.Square),
                         reads=[r_xt[b]], writes=[r_sq])
                    for kt in range(8):
                        P.op("pe", lambda e, kt=kt: e.matmul(ps_ss[:], lhsT=self.ones_f[:], rhs=sq[:, kt, :],
                                                             start=(kt == 0), stop=(kt == 7)),
                             reads=[r_sq, self.r_const], writes=[r_pss])
                    P.op("act", lambda e: e.activation(out=rstd[:], in_=ps_ss[:], func=AF.Sqrt, bias=self.eps_c[:],
                                                       scale=1.0 / D), reads=[r_pss, self.r_const], writes=[r_rstd])
                    P.op("dve", lambda e: e.reciprocal(out=rstd[:], in_=rstd[:]), reads=[r_rstd], writes=[r_rstd])
                    for kt in range(8):
                        P.op("dve", lambda e, b=b, kt=kt: e.scalar_tensor_tensor(
                            out=hT[:, kt, :], in0=xt[b][:, kt, :], scalar=1.0, in1=rstd[:], op0=ALU.mult,
                            op1=ALU.mult), reads=[r_xt[b], r_rstd], writes=[r_hT])
                    for j in range(NJ):
                        q = j % 2
                        for kt in range(8):
                            P.op("pe", lambda e, q=q, kt=kt, j=j: e.matmul(
                                ps_g[q][:], lhsT=wup[:, kt, j * 128:(j + 1) * 128], rhs=hT[:, kt, :],
                                start=(kt == 0), stop=(kt == 7)), reads=[r_wup, r_hT], writes=[r_psg[q]])
                        for kt in range(8):
                            P.op("pe", lambda e, q=q, kt=kt, j=j: e.matmul(
                                ps_v[q][:], lhsT=wup[:, kt, HC + j * 128:HC + (j + 1) * 128], rhs=hT[:, kt, :],
                                start=(kt == 0), stop=(kt == 7)), reads=[r_wup, r_hT], writes=[r_psv[q]])
                        for (ps, rps, t, rt, jj) in ((ps_g[q], r_psg[q], tg[q], r_tg[q], j),
                                                     (ps_v[q], r_psv[q], tv[q], r_tv[q], NJ + j)):
                            P.op("act", lambda e, ps=ps, t=t, jj=jj: e.activation(
                                out=t[:, 1:511], in_=ps[:, 0:510], func=AF.Copy, scale=cw[:, 0, jj:jj + 1]),
                                reads=[rps, r_cw], writes=[rt])
                            for tap in (1, 2):
                                P.op("dve", lambda e, ps=ps, t=t, jj=jj, tap=tap: e.scalar_tensor_tensor(
                                    out=t[:, 1:511], in0=ps[:, tap:tap + 510], scalar=cw[:, tap, jj:jj + 1],
                                    in1=t[:, 1:511], op0=ALU.mult, op1=ALU.add),
                                    reads=[rps, r_cw, rt], writes=[rt])
                        P.op("act", lambda e, q=q: e.activation(out=sg[q][:, 1:511], in_=tg[q][:, 1:511], func=AF.Silu),
                             reads=[r_tg[q]], writes=[r_sg[q]])
                        P.op("pool", lambda e, q=q, j=j: e.tensor_tensor(
                            out=act[:, j, 1:511], in0=sg[q][:, 1:511], in1=tv[q][:, 1:511], op=ALU.mult),
                            reads=[r_sg[q], r_tv[q]], writes=[r_act])
                    for mo in range(8):
                        q = mo % 2
                        if ph == 1:
                            P.dma("sp", lambda e, q=q, mo=mo, s0=s0, nout=nout: e.dma_start(
                                out=xr[q][:, 1:1 + nout], in_=xout[mo * 128:(mo + 1) * 128, s0:s0 + nout]),
                                reads=[rx_out], writes=[r_xr[q]])
                        for j in range(NJ):
                            P.op("pe", lambda e, q=q, j=j, mo=mo: e.matmul(
                                ps_y[q][:, 1:511], lhsT=wdn[:, j, mo * 128:(mo + 1) * 128], rhs=act[:, j, 1:511],
                                start=(j == 0), stop=(j == NJ - 1)), reads=[r_wdn, r_act], writes=[r_psy[q]])
                        if ph == 1:
                            P.op("dve", lambda e, q=q: e.tensor_tensor(
                                out=xo[q][:, 1:511], in0=ps_y[q][:, 1:511], in1=xr[q][:, 1:511], op=ALU.add),
                                reads=[r_psy[q], r_xr[q]], writes=[r_xo[q]])
                        else:
                            P.op("dve", lambda e, q=q, mo=mo, b=b: e.tensor_tensor(
                                out=xo[q][:, 1:511], in0=ps_y[q][:, 1:511], in1=xt[b][:, mo, 1:511], op=ALU.add),
                                reads=[r_psy[q], r_xt[b]], writes=[r_xo[q]])
                        P.dma("sp", lambda e, q=q, mo=mo, s0=s0, nout=nout: e.dma_start(
                            out=xout[mo * 128:(mo + 1) * 128, s0:s0 + nout], in_=xo[q][:, 1:1 + nout]),
                            reads=[r_xo[q]], writes=[rx_out])
            P.barrier()

    def load_w(self, st, dst, rdst, src, ncols, scale=None, rscale=None, stg=None, rstg=None, eng="pool"):
        P = self.P
        nk = src.shape[0] // 128
        CH = stg[0].shape[1]
        k = 0
        for kt in range(nk):
            for c0 in range(0, ncols, CH):
                cn = min(CH, ncols - c0)
                s = k % 2
                k += 1
                P.dma("sp", lambda e, s=s, kt=kt, c0=c0, cn=cn: e.dma_start(
                    out=stg[s][:, 0:cn], in_=src[kt * 128:(kt + 1) * 128, c0:c0 + cn]), writes=[rstg[s]])
                if scale is not None:
                    P.op(eng, lambda e, s=s, kt=kt, c0=c0, cn=cn: e.tensor_scalar(
                        out=dst[:, kt, c0:c0 + cn], in0=stg[s][:, 0:cn], scalar1=scale[:, kt:kt + 1], scalar2=None,
                        op0=ALU.mult), reads=[rstg[s], rscale], writes=[rdst])
                else:
                    P.op(eng, lambda e, s=s, kt=kt, c0=c0, cn=cn: e.tensor_copy(
                        out=dst[:, kt, c0:c0 + cn], in_=stg[s][:, 0:cn]), reads=[rstg[s]], writes=[rdst])

    def rms_tile(self, xt, r_xt, hT, r_hT, tmp):
        P = self.P
        sqt, r_sqt, ps_ss, r_pss, rstd, r_rstd = tmp
        for kt in range(8):
            s = kt % 2
            P.op("act", lambda e, s=s, kt=kt: e.activation(out=sqt[s][:], in_=xt[:, kt, :], func=AF.Square),
                 reads=[r_xt], writes=[r_sqt[s]])
            P.op("pe", lambda e, s=s, kt=kt: e.matmul(ps_ss[:], lhsT=self.ones_f[:], rhs=sqt[s][:],
                                                      start=(kt == 0), stop=(kt == 7)),
                 reads=[r_sqt[s], self.r_const], writes=[r_pss])
        P.op("act", lambda e: e.activation(out=rstd[:], in_=ps_ss[:], func=AF.Sqrt, bias=self.eps_c[:],
                                           scale=1.0 / D), reads=[r_pss, self.r_const], writes=[r_rstd])
        P.op("dve", lambda e: e.reciprocal(out=rstd[:], in_=rstd[:]), reads=[r_rstd], writes=[r_rstd])
        for kt in range(8):
            P.op("dve" if kt % 2 == 0 else "pool", lambda e, kt=kt: e.tensor_tensor(
                out=hT[:, kt, :], in0=xt[:, kt, :], in1=rstd[:], op=ALU.mult),
                reads=[r_xt, r_rstd], writes=[r_hT])

    def proj(self, l, xin, rx_in):
        P, nc, T, d = self.P, self.nc, self.T, self.d
        R = P.res
        NT = T // 512
        OQA, OKA, OVA, OQD, OBD, OZ, OG = 0, 512, 640, 768, 2304, 2336, 2848
        with contextlib.ExitStack() as st:
            g_sb = P.sbuf("mix_g", [128, 8], F32, st)
            win = P.sbuf("win", [128, 8, NIN], BF16, st)
            wk2 = P.sbuf("wk2", [128, 8, 256], BF16, st)
            stg = [P.sbuf("stg", [128, 1224], F32, st) for _ in range(2)]
            qg = P.sbuf("qg", [128, 2], F32, st)
            xt = P.sbuf("xt", [128, 8, 512], F32, st)
            hT = P.sbuf("hT", [128, 8, 512], BF16, st)
            sqt = [P.sbuf("sqt", [128, 512], F32, st) for _ in range(2)]
            rstd = P.sbuf("rstd", [128, 512], F32, st)
            cs = [P.sbuf("cs", [128, 2, 512], F32, st) for _ in range(2)]
            sqq = [P.sbuf("sqq", [128, 512], F32, st) for _ in range(2)]
            rs = [P.sbuf("rs", [128, 512], F32, st) for _ in range(2)]
            qn = [P.sbuf("qn", [128, 512], F32, st) for _ in range(2)]
            t1 = [P.sbuf("t1", [128, 512], F32, st) for _ in range(2)]
            t2 = [P.sbuf("t2", [128, 512], F32, st) for _ in range(2)]
            ob = [P.sbuf("ob", [128, 512], BF16, st) for _ in range(3)]
            of = [P.sbuf("of", [128, 512], F32, st) for _ in range(3)]
            vb = [P.sbuf("vb", [128, 130], BF16, st) for _ in range(2)]
            ps_ss = P.psum("ps_ss", [128, 512], F32, st)
            ps_m = [P.psum("ps_m", [128, 512], F32, st) for _ in range(3)]
            ps_a = [P.psum("ps_a", [128, 512], F32, st) for _ in range(2)]
            r_g, r_win, r_wk2, r_qg, r_xt, r_hT, r_rstd, r_pss = R(), R(), R(), R(), R(), R(), R(), R()
            r_stg, r_sqt, r_cs, r_sqq, r_rs, r_qn, r_t1, r_t2 = ([R(), R()] for _ in range(8))
            r_ob, r_of, r_psm = ([R(), R(), R()] for _ in range(3))
            r_vb, r_psa = [R(), R()], [R(), R()]
            rd = self.rd

            P.dma("sp", lambda e: e.dma_start(out=g_sb[:], in_=d["norm_mix_g"][l].rearrange("(kt p) -> p kt", p=128),
                                              allow_slow_non_contiguous=True), writes=[r_g])
            import os
            SKIP = os.environ.get("SKIP", "")
            for h2 in range(0 if "qg" in SKIP else 2):
                P.dma("sp", lambda e, h2=h2: e.dma_start(out=qg[h2 * 64:(h2 + 1) * 64, 0:1],
                                                         in_=d["q_norm_g"][l].rearrange("(p o) -> p o", o=1),
                                                         allow_slow_non_contiguous=True), writes=[r_qg])
                P.dma("sp", lambda e, h2=h2: e.dma_start(out=qg[h2 * 64:(h2 + 1) * 64, 1:2],
                                                         in_=d["k_norm_g"][l].rearrange("(p o) -> p o", o=1),
                                                         allow_slow_non_contiguous=True), writes=[r_qg])
            P.op("dve", lambda e: e.tensor_scalar(out=qg[:, 0:1], in0=qg[:, 0:1], scalar1=0.125, scalar2=None,
                                                  op0=ALU.mult), reads=[r_qg], writes=[r_qg])
            self.load_w(st, win, r_win, d["w_in"][l], NIN, scale=g_sb, rscale=r_g, stg=stg, rstg=r_stg)
            for kt in range(0 if "wk2" in SKIP else 8):
                for g in range(2):
                    for dup in range(2):
                        P.op("pool", lambda e, kt=kt, g=g, dup=dup: e.tensor_copy(
                            out=wk2[:, kt, g * 128 + dup * 64:g * 128 + dup * 64 + 64],
                            in_=win[:, kt, OKA + g * 64:OKA + g * 64 + 64]), reads=[r_win], writes=[r_wk2])
            for b in range(0 if "vb" in SKIP else 2):
                for g in range(2):
                    P.op("pool", lambda e, b=b, g=g: e.memset(vb[b][:, g * 65 + 64:g * 65 + 65], 1.0), writes=[r_vb[b]])
            xtiled = xin.rearrange("(kt p) n -> p kt n", p=128)
            cnt = {"m": 0, "a": 0, "o": 0, "f": 0, "q": 0}

            def fm_group(lhs_fn, kind, dst, rdst, row0, c0, ti):
                i = cnt["m"] % 3
                cnt["m"] += 1
                for kt in range(8):
                    P.op("pe", lambda e, kt=kt, i=i: e.matmul(ps_m[i][:], lhsT=lhs_fn(kt), rhs=hT[:, kt, :],
                                                              start=(kt == 0), stop=(kt == 7)),
                         reads=[r_win, r_wk2, r_hT], writes=[r_psm[i]])
                if kind == "f32":
                    j = cnt["f"] % 3
                    cnt["f"] += 1
                    P.op("act", lambda e, i=i, j=j: e.copy(out=of[j][:], in_=ps_m[i][:]), reads=[r_psm[i]],
                         writes=[r_of[j]])
                    P.dma("sp", lambda e, j=j: e.dma_start(out=dst[row0:row0 + 128, c0:c0 + 512], in_=of[j][:]),
                          reads=[r_of[j]], writes=[rdst])
                elif kind == "sig":
                    j = cnt["o"] % 3
                    cnt["o"] += 1
                    P.op("act", lambda e, i=i, j=j: e.activation(out=ob[j][:], in_=ps_m[i][:], func=AF.Sigmoid),
                         reads=[r_psm[i]], writes=[r_ob[j]])
                    P.dma("sp", lambda e, j=j: e.dma_start(out=dst[row0:row0 + 128, c0:c0 + 512], in_=ob[j][:]),
                          reads=[r_ob[j]], writes=[rdst])
                else:
                    q = cnt["q"] % 2
                    cnt["q"] += 1
                    a = cnt["a"] % 2
                    cnt["a"] += 1
                    cb = ti % 2
                    P.op("act", lambda e, i=i, q=q: e.activation(out=sqq[q][:], in_=ps_m[i][:], func=AF.Square),
                         reads=[r_psm[i]], writes=[r_sqq[q]])
                    P.op("pe", lambda e, a=a, q=q: e.matmul(ps_a[a][:], lhsT=self.blk_f[:], rhs=sqq[q][:],
                                                            start=True, stop=True),
                         reads=[r_sqq[q], self.r_const], writes=[r_psa[a]])
                    P.op("act", lambda e, a=a, q=q: e.activation(out=rs[q][:], in_=ps_a[a][:], func=AF.Sqrt,
                                                                 bias=self.eps_c[:], scale=1.0 / 64),
                         reads=[r_psa[a], self.r_const], writes=[r_rs[q]])
                    P.op("dve", lambda e, q=q: e.reciprocal(out=rs[q][:], in_=rs[q][:]), reads=[r_rs[q]], writes=[r_rs[q]])
                    P.op("dve", lambda e, i=i, q=q: e.scalar_tensor_tensor(
                        out=qn[q][:], in0=ps_m[i][:], scalar=qg[:, kind:kind + 1], in1=rs[q][:], op0=ALU.mult,
                        op1=ALU.mult), reads=[r_psm[i], r_qg, r_rs[q]], writes=[r_qn[q]])
                    a2 = cnt["a"] % 2
                    cnt["a"] += 1
                    P.op("pe", lambda e, a2=a2, q=q: e.matmul(ps_a[a2][:], lhsT=self.rot_f[:], rhs=qn[q][:],
                                                              start=True, stop=True),
                         reads=[r_qn[q], self.r_const], writes=[r_psa[a2]])
                    P.op("pool", lambda e, q=q, cb=cb: e.tensor_tensor(out=t1[q][:], in0=qn[q][:], in1=cs[cb][:, 0, :],
                                                                       op=ALU.mult),
                         reads=[r_qn[q], r_cs[cb]], writes=[r_t1[q]])
                    P.op("dve", lambda e, q=q, cb=cb, a2=a2: e.tensor_tensor(out=t2[q][:], in0=ps_a[a2][:],
                                                                             in1=cs[cb][:, 1, :], op=ALU.mult),
                         reads=[r_psa[a2], r_cs[cb]], writes=[r_t2[q]])
                    j = cnt["o"] % 3
                    cnt["o"] += 1
                    P.op("pool", lambda e, q=q, j=j: e.tensor_tensor(out=ob[j][:], in0=t1[q][:], in1=t2[q][:], op=ALU.add),
                         reads=[r_t1[q], r_t2[q]], writes=[r_ob[j]])
                    P.dma("sp", lambda e, j=j: e.dma_start(out=dst[row0:row0 + 128, c0:c0 + 512], in_=ob[j][:]),
                          reads=[r_ob[j]], writes=[rdst])

            for ti in range(NT):
                c0 = ti * 512
                P.dma("sp", lambda e, c0=c0: e.dma_start(out=xt[:], in_=xtiled[:, :, c0:c0 + 512]),
                      reads=[rx_in], writes=[r_xt])
                P.dma("sp", lambda e, c0=c0, ti=ti: e.dma_start(out=cs[ti % 2][:, 0, :], in_=d["ropec"][:, c0:c0 + 512]),
                      writes=[r_cs[ti % 2]])
                P.dma("sp", lambda e, c0=c0, ti=ti: e.dma_start(out=cs[ti % 2][:, 1, :], in_=d["ropes"][:, c0:c0 + 512]),
                      writes=[r_cs[ti % 2]])
                self.rms_tile(xt, r_xt, hT, r_hT, (sqt, r_sqt, ps_ss, r_pss, rstd, r_rstd))
                import os
                PARTS = os.environ.get("PROJ_PARTS", "qdgt")
                for m in range(4 if "q" in PARTS else 0):
                    fm_group(lambda kt, m=m: win[:, kt, OQA + m * 128:OQA + (m + 1) * 128], 0, d["qaT"], rd["qaT"],
                             m * 128, c0, ti)
                for g in range(2 if "q" in PARTS else 0):
                    fm_group(lambda kt, g=g: wk2[:, kt, g * 128:(g + 1) * 128], 1, d["kaT"], rd["kaT"], g * 128, c0, ti)
                for m in range(12 if "d" in PARTS else 0):
                    fm_group(lambda kt, m=m: win[:, kt, OQD + m * 128:OQD + (m + 1) * 128], "f32", d["dpre"], rd["dpre"],
                             m * 128, c0 + 2, ti)
                for m in range(16 if "g" in PARTS else 0):
                    fm_group(lambda kt, m=m: win[:, kt, OG + m * 128:OG + (m + 1) * 128], "sig", d["gT"], rd["gT"],
                             m * 128, c0, ti)
                for sub in range(4 if "t" in PARTS else 0):
                    r0 = c0 + sub * 128
                    i = cnt["m"] % 3
                    cnt["m"] += 1
                    for kt in range(8):
                        P.op("pe", lambda e, kt=kt, i=i, sub=sub: e.matmul(
                            ps_m[i][:, 0:128], lhsT=hT[:, kt, sub * 128:(sub + 1) * 128], rhs=win[:, kt, OVA:OVA + 128],
                            start=(kt == 0), stop=(kt == 7)), reads=[r_win, r_hT], writes=[r_psm[i]])
                    b = sub % 2
                    for g in range(2):
                        P.op("act", lambda e, i=i, b=b, g=g: e.copy(out=vb[b][:, g * 65:g * 65 + 64],
                                                                     in_=ps_m[i][:, g * 64:(g + 1) * 64]),
                             reads=[r_psm[i]], writes=[r_vb[b]])
                    P.dma("sp", lambda e, b=b, r0=r0: e.dma_start(out=d["va"][r0:r0 + 128, :], in_=vb[b][:]),
                          reads=[r_vb[b]], writes=[rd["va"]])
                    for (oc, ncol, dst, rdst) in ((OBD, 32, d["bd"], rd["bd"]), (OZ, 512, d["z"], rd["z"])):
                        i = cnt["m"] % 3
                        cnt["m"] += 1
                        for kt in range(8):
                            P.op("pe", lambda e, kt=kt, i=i, sub=sub, oc=oc, ncol=ncol: e.matmul(
                                ps_m[i][:, 0:ncol], lhsT=hT[:, kt, sub * 128:(sub + 1) * 128],
                                rhs=win[:, kt, oc:oc + ncol], start=(kt == 0), stop=(kt == 7)),
                                reads=[r_win, r_hT], writes=[r_psm[i]])
                        j = cnt["f"] % 3
                        cnt["f"] += 1
                        P.op("act", lambda e, i=i, j=j, ncol=ncol: e.copy(out=of[j][:, 0:ncol], in_=ps_m[i][:, 0:ncol]),
                             reads=[r_psm[i]], writes=[r_of[j]])
                        P.dma("sp", lambda e, j=j, r0=r0, ncol=ncol, dst=dst: e.dma_start(
                            out=dst[r0:r0 + 128, :], in_=of[j][:, 0:ncol]), reads=[r_of[j]], writes=[rdst])
            P.barrier()

    def attention(self, l):
        P, nc, T, d, rd = self.P, self.nc, self.T, self.d, self.rd
        R = P.res
        NKT = T // 128
        NQC = T // 512
        with contextlib.ExitStack() as st:
            kg = P.sbuf("kg", [128, T], BF16, st)
            vg = P.sbuf("vg", [128, NKT, 65], BF16, st)
            qt = [P.sbuf("qt", [128, 512], BF16, st) for _ in range(2)]
            pT = [P.sbuf("pT", [128, 512], BF16, st) for _ in range(4)]
            rsum = P.sbuf("rsum", [128, 512], F32, st)
            ocp = [P.sbuf("ocp", [64, 512], F32, st) for _ in range(2)]
            oo = [P.sbuf("oo", [64, 512], BF16, st) for _ in range(2)]
            ps_s = [P.psum("ps_s", [128, 512], F32, st) for _ in range(4)]
            ps_o = [P.psum("ps_o", [128, 512], F32, st) for _ in range(2)]
            ps_b = P.psum("ps_b", [128, 512], F32, st)
            r_kg, r_vg, r_rsum, r_psb = R(), R(), R(), R()
            r_qt, r_ocp, r_oo, r_pso = ([R(), R()] for _ in range(4))
            r_pT, r_pss = ([R(), R(), R(), R()] for _ in range(2))
            LA = 2
            for g in range(2):
                P.dma("sp", lambda e, g=g: e.dma_start(out=kg[:], in_=d["kaT"][g * 128:(g + 1) * 128, :]),
                      reads=[rd["kaT"]], writes=[r_kg])
                P.dma("sp", lambda e, g=g: e.dma_start(
                    out=vg[:], in_=d["va"][:, g * 65:(g + 1) * 65].rearrange("(kt p) c -> p kt c", p=128)),
                    reads=[rd["va"]], writes=[r_vg])
                tiles = []
                for qc in range(NQC):
                    for pair in range(2):
                        for h2 in range(2):
                            tiles.append((qc, pair, h2))
                items = [(ti_, kt) for ti_ in range(len(tiles)) for kt in range(NKT)]
                NI = len(items)

                def emit_S(idx):
                    ti_, kt = items[idx]
                    qc, pair, h2 = tiles[ti_]
                    qb = (qc * 2 + pair) % 2
                    if h2 == 0 and kt == 0:
                        mrow = (g * 2 + pair) * 128
                        P.dma("sp", lambda e, qb=qb, mrow=mrow, qc=qc: e.dma_start(
                            out=qt[qb][:], in_=d["qaT"][mrow:mrow + 128, qc * 512:(qc + 1) * 512]),
                            reads=[rd["qaT"]], writes=[r_qt[qb]])
                    pl, ph = h2 * 64, h2 * 64 + 64
                    s_ = idx % 4
                    P.op("pe", lambda e, s_=s_, kt=kt, qb=qb, pl=pl, ph=ph: e.matmul(
                        ps_s[s_][:], lhsT=kg[pl:ph, kt * 128:(kt + 1) * 128], rhs=qt[qb][pl:ph, :],
                        start=True, stop=True), reads=[r_kg, r_qt[qb]], writes=[r_pss[s_]])

                def emit_PV(idx):
                    ti_, kt = items[idx]
                    qc, pair, h2 = tiles[ti_]
                    head = g * 4 + pair * 2 + h2
                    ob_ = ti_ % 2
                    s_ = idx % 4
                    P.op("act", lambda e, s_=s_: e.activation(out=pT[s_][:], in_=ps_s[s_][:], func=AF.Exp),
                         reads=[r_pss[s_]], writes=[r_pT[s_]])
                    P.op("pe", lambda e, s_=s_, kt=kt, ob_=ob_: e.matmul(
                        ps_o[ob_][0:65, :], lhsT=vg[:, kt, :], rhs=pT[s_][:], start=(kt == 0),
                        stop=(kt == NKT - 1)), reads=[r_vg, r_pT[s_]], writes=[r_pso[ob_]])
                    if kt == NKT - 1:
                        P.op("dve", lambda e, ob_=ob_: e.reciprocal(out=rsum[64:65, :], in_=ps_o[ob_][64:65, :]),
                             reads=[r_pso[ob_]], writes=[r_rsum])
                        P.op("pe", lambda e: e.matmul(ps_b[0:64, :], lhsT=self.ones_f[64:65, 0:64], rhs=rsum[64:65, :],
                                                      start=True, stop=True),
                             reads=[r_rsum, self.r_const], writes=[r_psb])
                        P.op("act", lambda e, ob_=ob_: e.copy(out=ocp[ob_][:], in_=ps_o[ob_][0:64, :]),
                             reads=[r_pso[ob_]], writes=[r_ocp[ob_]])
                        P.op("dve", lambda e, ob_=ob_: e.tensor_tensor(out=oo[ob_][:], in0=ocp[ob_][:],
                                                                       in1=ps_b[0:64, :], op=ALU.mult),
                             reads=[r_ocp[ob_], r_psb], writes=[r_oo[ob_]])
                        P.dma("sp", lambda e, ob_=ob_, head=head, qc=qc: e.dma_start(
                            out=d["oaT"][head * 64:(head + 1) * 64, qc * 512:(qc + 1) * 512], in_=oo[ob_][:]),
                            reads=[r_oo[ob_]], writes=[rd["oaT"]])

                for idx in range(NI + LA):
                    if idx < NI:
                        emit_S(idx)
                    if idx - LA >= 0:
                        emit_PV(idx - LA)
            P.barrier()

    def merge(self, l, xin, xout, rx_in, rx_out):
        P, nc, T, d, rd = self.P, self.nc, self.T, self.d, self.rd
        R = P.res
        NT = T // 512
        with contextlib.ExitStack() as st:
            woa = P.sbuf("woa", [128, 4, D], BF16, st)
            wod = P.sbuf("wod", [128, 4, D], BF16, st)
            wo = P.sbuf("wo", [128, 8, D], BF16, st)
            stg = [P.sbuf("stg", [128, 1024], F32, st) for _ in range(2)]
            oa = [P.sbuf("oa", [128, 4, 512], BF16, st) for _ in range(2)]
            od = [P.sbuf("od", [128, 4, 512], BF16, st) for _ in range(2)]
            gt = [P.sbuf("gt", [128, 2, 512], BF16, st) for _ in range(2)]
            ta = [P.sbuf("ta", [128, 512], F32, st) for _ in range(2)]
            mx = P.sbuf("mx", [128, 8, 512], BF16, st)
            xr = [P.sbuf("xr", [128, 512], F32, st) for _ in range(2)]
            xo = [P.sbuf("xo", [128, 512], F32, st) for _ in range(2)]
            ps_a = [P.psum("ps_a", [128, 512], F32, st) for _ in range(2)]
            ps_d = [P.psum("ps_d", [128, 512], F32, st) for _ in range(2)]
            ps_y = [P.psum("ps_y", [128, 512], F32, st) for _ in range(2)]
            r_woa, r_wod, r_wo, r_mx = R(), R(), R(), R()
            r_stg, r_oa, r_od, r_gt, r_ta, r_xr, r_xo, r_psa, r_psd, r_psy = ([R(), R()] for _ in range(10))
            self.load_w(st, woa, r_woa, d["w_o_attn"][l], D, stg=stg, rstg=r_stg)
            self.load_w(st, wod, r_wod, d["w_o_dn"][l], D, stg=stg, rstg=r_stg)
            self.load_w(st, wo, r_wo, d["w_out"][l], D, stg=stg, rstg=r_stg)
            oaT = d["oaT"].rearrange("(kt p) n -> p kt n", p=128)
            odT = d["odT"].rearrange("(kt p) n -> p kt n", p=128)
            k = 0
            for ti in range(NT):
                c0 = ti * 512
                b = ti % 2
                P.dma("sp", lambda e, b=b, c0=c0: e.dma_start(out=oa[b][:], in_=oaT[:, :, c0:c0 + 512]),
                      reads=[rd["oaT"]], writes=[r_oa[b]])
                P.dma("sp", lambda e, b=b, c0=c0: e.dma_start(out=od[b][:], in_=odT[:, :, c0:c0 + 512]),
                      reads=[rd["odT"]], writes=[r_od[b]])
                for mo in range(8):
                    q = k % 2
                    k += 1
                    for br in range(2):
                        P.dma("sp", lambda e, q=q, br=br, mo=mo, c0=c0: e.dma_start(
                            out=gt[q][:, br, :], in_=d["gT"][br * D + mo * 128:br * D + (mo + 1) * 128, c0:c0 + 512]),
                            reads=[rd["gT"]], writes=[r_gt[q]])
                    for kt in range(4):
                        P.op("pe", lambda e, q=q, kt=kt, mo=mo, b=b: e.matmul(
                            ps_a[q][:], lhsT=woa[:, kt, mo * 128:(mo + 1) * 128], rhs=oa[b][:, kt, :],
                            start=(kt == 0), stop=(kt == 3)), reads=[r_woa, r_oa[b]], writes=[r_psa[q]])
                    for kt in range(4):
                        P.op("pe", lambda e, q=q, kt=kt, mo=mo, b=b: e.matmul(
                            ps_d[q][:], lhsT=wod[:, kt, mo * 128:(mo + 1) * 128], rhs=od[b][:, kt, :],
                            start=(kt == 0), stop=(kt == 3)), reads=[r_wod, r_od[b]], writes=[r_psd[q]])
                    P.op("dve", lambda e, q=q: e.tensor_tensor(out=ta[q][:], in0=ps_a[q][:], in1=gt[q][:, 0, :],
                                                               op=ALU.mult), reads=[r_psa[q], r_gt[q]], writes=[r_ta[q]])
                    P.op("dve", lambda e, q=q: e.tensor_tensor(out=xo[q][:], in0=ps_d[q][:], in1=gt[q][:, 1, :],
                                                               op=ALU.mult), reads=[r_psd[q], r_gt[q]], writes=[r_xo[q]])
                    P.op("pool", lambda e, q=q, mo=mo: e.tensor_tensor(out=mx[:, mo, :], in0=ta[q][:], in1=xo[q][:],
                                                                       op=ALU.add),
                         reads=[r_ta[q], r_xo[q]], writes=[r_mx])
                for mo in range(8):
                    q = k % 2
                    k += 1
                    P.dma("sp", lambda e, q=q, mo=mo, c0=c0: e.dma_start(
                        out=xr[q][:], in_=xin[mo * 128:(mo + 1) * 128, c0:c0 + 512]), reads=[rx_in], writes=[r_xr[q]])
                    for kt in range(8):
                        P.op("pe", lambda e, q=q, kt=kt, mo=mo: e.matmul(
                            ps_y[q][:], lhsT=wo[:, kt, mo * 128:(mo + 1) * 128], rhs=mx[:, kt, :],
                            start=(kt == 0), stop=(kt == 7)), reads=[r_wo, r_mx], writes=[r_psy[q]])
                    P.op("dve", lambda e, q=q: e.tensor_tensor(out=xo[q][:], in0=ps_y[q][:], in1=xr[q][:], op=ALU.add),
                         reads=[r_psy[q], r_xr[q]], writes=[r_xo[q]])
                    P.dma("sp", lambda e, q=q, mo=mo, c0=c0: e.dma_start(
                        out=xout[mo * 128:(mo + 1) * 128, c0:c0 + 512], in_=xo[q][:]), reads=[r_xo[q]], writes=[rx_out])
            P.barrier()

    def zero_dram(self, ap, rres, bf=False):
        P = self.P
        z = self.zero_b if bf else self.zero_f
        rows, cols = ap.shape
        for r0 in range(0, rows, 128):
            rn = min(128, rows - r0)
            for c0 in range(0, cols, 512):
                cn = min(512, cols - c0)
                P.dma("sp", lambda e, r0=r0, rn=rn, c0=c0, cn=cn: e.dma_start(
                    out=ap[r0:r0 + rn, c0:c0 + cn], in_=z[0:rn, 0:cn]), reads=[self.r_const], writes=[rres])


    def deltanet(self, l):
        P, nc, T, d, rd = self.P, self.nc, self.T, self.d, self.rd
        R = P.res
        NT = T // 512
        NCP = T // 128
        self.zero_dram(d["dpre"][:, 0:2], rd["dpre"])
        self.zero_dram(d["dpre"][:, T + 2:T + 4], rd["dpre"])
        with contextlib.ExitStack() as st:
            cwd = P.sbuf("cwd", [128, 5, 12], F32, st)
            xin = [P.sbuf("xin", [128, 516], F32, st) for _ in range(2)]
            acc = [P.sbuf("acc", [128, 512], F32, st) for _ in range(2)]
            sl = [P.sbuf("sl", [128, 512], F32, st) for _ in range(2)]
            sq = [P.sbuf("sq", [128, 512], F32, st) for _ in range(2)]
            rs = [P.sbuf("rs", [128, 512], F32, st) for _ in range(2)]
            ob = [P.sbuf("ob", [128, 512], BF16, st) for _ in range(2)]
            tmo = [P.sbuf("tmo", [128, 4, 128], BF16, st) for _ in range(2)]
            ps = [P.psum("ps", [128, 512], F32, st) for _ in range(2)]
            pst = [P.psum("pst", [128, 4, 128], BF16, st) for _ in range(2)]
            r_cwd = R()
            r_xin, r_acc, r_sl, r_sq, r_rs, r_ob, r_tmo, r_ps, r_pst = ([R(), R()] for _ in range(9))
            for tap in range(5):
                P.dma("sp", lambda e, tap=tap: e.dma_start(
                    out=cwd[:, tap, :], in_=d["dn_conv_w"][l][tap, :].rearrange("(m p) -> p m", p=128),
                    allow_slow_non_contiguous=True), writes=[r_cwd])
            it = 0
            for m in range(12):
                for ti in range(NT):
                    b = it % 2
                    it += 1
                    c0 = ti * 512
                    P.dma("sp", lambda e, b=b, m=m, c0=c0: e.dma_start(
                        out=xin[b][:], in_=d["dpre"][m * 128:(m + 1) * 128, c0:c0 + 516]),
                        reads=[rd["dpre"]], writes=[r_xin[b]])
                    P.op("act", lambda e, b=b, m=m: e.activation(out=acc[b][:], in_=xin[b][:, 0:512], func=AF.Copy,
                                                                 scale=cwd[:, 0, m:m + 1]),
                         reads=[r_xin[b], r_cwd], writes=[r_acc[b]])
                    for tap in range(1, 5):
                        P.op("dve", lambda e, b=b, m=m, tap=tap: e.scalar_tensor_tensor(
                            out=acc[b][:], in0=xin[b][:, tap:tap + 512], scalar=cwd[:, tap, m:m + 1], in1=acc[b][:],
                            op0=ALU.mult, op1=ALU.add), reads=[r_xin[b], r_cwd, r_acc[b]], writes=[r_acc[b]])
                    P.op("act", lambda e, b=b: e.activation(out=sl[b][:], in_=acc[b][:], func=AF.Silu),
                         reads=[r_acc[b]], writes=[r_sl[b]])
                    if m < 8:
                        P.op("act", lambda e, b=b: e.activation(out=sq[b][:], in_=sl[b][:], func=AF.Square),
                             reads=[r_sl[b]], writes=[r_sq[b]])
                        P.op("pe", lambda e, b=b: e.matmul(ps[b][:], lhsT=self.blk_f, rhs=sq[b][:], start=True, stop=True),
                             reads=[r_sq[b], self.r_const], writes=[r_ps[b]])
                        P.op("act", lambda e, b=b: e.activation(out=rs[b][:], in_=ps[b][:], func=AF.Sqrt,
                                                                bias=self.eps_c[:], scale=1.0),
                             reads=[r_ps[b], self.r_const], writes=[r_rs[b]])
                        P.op("dve", lambda e, b=b: e.reciprocal(out=rs[b][:], in_=rs[b][:]), reads=[r_rs[b]], writes=[r_rs[b]])
                        scl = 0.125 if m < 4 else 1.0
                        P.op("dve", lambda e, b=b, scl=scl: e.scalar_tensor_tensor(
                            out=ob[b][:], in0=sl[b][:], scalar=scl, in1=rs[b][:], op0=ALU.mult, op1=ALU.mult),
                            reads=[r_sl[b], r_rs[b]], writes=[r_ob[b]])
                        dst, rdst = (d["dqT"], rd["dqT"]) if m < 4 else (d["dkT"], rd["dkT"])
                        P.dma("sp", lambda e, b=b, m=m, c0=c0, dst=dst: e.dma_start(
                            out=dst[(m % 4) * 128:(m % 4 + 1) * 128, c0:c0 + 512], in_=ob[b][:]),
                            reads=[r_ob[b]], writes=[rdst])
                    else:
                        P.op("pool", lambda e, b=b: e.tensor_copy(out=ob[b][:], in_=sl[b][:]), reads=[r_sl[b]],
                             writes=[r_ob[b]])
                    if m >= 4:
                        for sub in range(4):
                            P.op("pe", lambda e, b=b, sub=sub: e.transpose(
                                out=pst[b][:, sub, :], in_=ob[b][:, sub * 128:(sub + 1) * 128], identity=self.identb[:]),
                                reads=[r_ob[b], self.r_const], writes=[r_pst[b]])
                        P.op("act", lambda e, b=b: e.copy(out=tmo[b][:], in_=pst[b][:]), reads=[r_pst[b]], writes=[r_tmo[b]])
                        dst, rdst = (d["dk_tm"], rd["dk_tm"]) if m < 8 else (d["dv_tm"], rd["dv_tm"])
                        mc = (m % 4) * 128
                        P.dma("sp", lambda e, b=b, mc=mc, c0=c0, dst=dst: e.dma_start(
                            out=dst[c0:c0 + 512, mc:mc + 128].rearrange("(s p) c -> p s c", p=128), in_=tmo[b][:]),
                            reads=[r_tmo[b]], writes=[rdst])
            P.barrier()
        import os
        DN_STOP = os.environ.get("DN_STOP", "")
        if DN_STOP == "A":
            return
        with contextlib.ExitStack() as st:
            bdl = P.sbuf("bdl", [128, NCP, 32], F32, st)
            tmp16 = P.sbuf("tmp16", [128, NCP, 16], F32, st)
            dtb = P.sbuf("dtb", [128, 16], F32, st)
            nea = P.sbuf("nea", [128, 16], F32, st)
            names = ("beta", "nbeta", "g", "gc", "ngc", "be", "e2")
            ga = {n: P.sbuf(n, [128, 2, NCP, 8], F32, st) for n in names}
            r_gate = R()
            pq = [P.psum("pq", [128, 4, 128], F32, st) for _ in range(5)]
            psg = [pq[i][:].rearrange("p a d -> p (a d)") for i in range(2)]
            r_psg = [R(), R()]
            P.dma("sp", lambda e: e.dma_start(out=bdl[:], in_=d["bd"].rearrange("(cp p) c -> p cp c", p=128)),
                  reads=[rd["bd"]], writes=[r_gate])
            P.dma("sp", lambda e: e.dma_start(
                out=dtb[:], in_=d["dn_dt_bias"][l].rearrange("a h -> (a h)").partition_broadcast(128)), writes=[r_gate])
            P.dma("sp", lambda e: e.dma_start(
                out=nea[:], in_=d["dn_a_log"][l].rearrange("a h -> (a h)").partition_broadcast(128)), writes=[r_gate])
            G = [r_gate]
            P.op("act", lambda e: e.activation(out=nea[:], in_=nea[:], func=AF.Exp), reads=G, writes=G)
            P.op("dve", lambda e: e.tensor_scalar(out=nea[:], in0=nea[:], scalar1=-1.0, scalar2=None, op0=ALU.mult),
                 reads=G, writes=G)
            P.op("dve", lambda e: e.tensor_tensor(out=tmp16[:], in0=bdl[:, :, 16:32],
                                                  in1=dtb[:].unsqueeze(1).broadcast_to([128, NCP, 16]), op=ALU.add),
                 reads=G, writes=G)
            P.op("act", lambda e: e.activation(out=tmp16[:], in_=tmp16[:], func=AF.Exp), reads=G, writes=G)
            P.op("act", lambda e: e.activation(out=tmp16[:], in_=tmp16[:], func=AF.Ln, bias=self.ones_f[:, 0:1], scale=1.0),
                 reads=G + [self.r_const], writes=G)
            for dr in range(2):
                P.op("dve", lambda e, dr=dr: e.tensor_tensor(
                    out=ga["g"][:, dr], in0=tmp16[:, :, dr * 8:(dr + 1) * 8],
                    in1=nea[:, dr * 8:(dr + 1) * 8].unsqueeze(1).broadcast_to([128, NCP, 8]), op=ALU.mult),
                    reads=G, writes=G)
                P.op("act", lambda e, dr=dr: e.activation(out=ga["beta"][:, dr], in_=bdl[:, :, dr * 8:(dr + 1) * 8],
                                                          func=AF.Sigmoid), reads=G, writes=G)
            P.op("dve", lambda e: e.tensor_scalar(out=ga["nbeta"][:], in0=ga["beta"][:], scalar1=-1.0, scalar2=None,
                                                  op0=ALU.mult), reads=G, writes=G)
            NB = NCP * 8
            for dr in range(2):
                gflat = ga["g"][:, dr].rearrange("p c h -> p (c h)")
                gcflat = ga["gc"][:, dr].rearrange("p c h -> p (c h)")
                e2flat = ga["e2"][:, dr].rearrange("p c h -> p (c h)")
                for c0 in range(0, NB, 512):
                    cn = min(512, NB - c0)
                    P.op("pe", lambda e, dr=dr, c0=c0, cn=cn, gflat=gflat: e.matmul(
                        psg[0][:, 0:cn], lhsT=self.tri[dr], rhs=gflat[:, c0:c0 + cn], start=True, stop=True),
                        reads=G + [self.r_const], writes=[r_psg[0]])
                    P.op("act", lambda e, c0=c0, cn=cn, gcflat=gcflat: e.copy(out=gcflat[:, c0:c0 + cn], in_=psg[0][:, 0:cn]),
                         reads=[r_psg[0]], writes=G)
                    P.op("pe", lambda e, c0=c0, cn=cn, gflat=gflat: e.matmul(
                        psg[1][:, 0:cn], lhsT=self.blk_f, rhs=gflat[:, c0:c0 + cn], start=True, stop=True),
                        reads=G + [self.r_const], writes=[r_psg[1]])
                    P.op("dve", lambda e, c0=c0, cn=cn, gcflat=gcflat, e2flat=e2flat: e.tensor_tensor(
                        out=e2flat[:, c0:c0 + cn], in0=psg[1][:, 0:cn], in1=gcflat[:, c0:c0 + cn], op=ALU.subtract),
                        reads=[r_psg[1]] + G, writes=G)
            P.op("act", lambda e: e.activation(out=ga["e2"][:], in_=ga["e2"][:], func=AF.Exp), reads=G, writes=G)
            P.op("dve", lambda e: e.tensor_scalar(out=ga["ngc"][:], in0=ga["gc"][:], scalar1=-1.0, scalar2=None,
                                                  op0=ALU.mult), reads=G, writes=G)
            P.op("act", lambda e: e.activation(out=ga["be"][:], in_=ga["gc"][:], func=AF.Exp), reads=G, writes=G)
            P.op("pool", lambda e: e.tensor_tensor(out=ga["be"][:], in0=ga["be"][:], in1=ga["beta"][:], op=ALU.mult),
                 reads=G, writes=G)

            def sb2(name, shape, dt, n=2):
                return [P.sbuf(name, shape, dt, st) for _ in range(n)], [R() for _ in range(n)]
            kq, r_kq = sb2("kq", [128, 2, 128], BF16, 3)
            ktm, r_ktm = sb2("ktm", [128, 2, 64], BF16, 3)
            vtm, r_vtm = sb2("vtm", [128, 2, 64], BF16, 3)
            G2, r_G2 = sb2("G2", [128, 2, 64], F32)
            Eg, r_Eg = sb2("Eg", [128, 128], F32)
            qdec, r_qdec = sb2("qdec", [128, 128], BF16)
            gtc, r_gtc = sb2("gtc", [128, 2], F32)
            vb_, r_vb = sb2("vb", [128, 2, 64], BF16)
            kbe, r_kbe = sb2("kbe", [128, 2, 64], BF16)
            kdec, r_kdec = sb2("kdec", [128, 2, 64], BF16)
            G1, r_G1 = sb2("G1", [128, 128], F32)
            G1n, r_G1n = sb2("G1n", [128, 128], F32)
            bcol, r_bcol = sb2("bcol", [128, 16], F32)
            Dm, r_Dm = sb2("Dm", [128, 128], F32)
            DmT, r_DmT = sb2("DmT", [128, 128], F32)
            Nn, r_Nn = sb2("Nn", [128, 128], BF16, 4)
            Mm, r_Mm = sb2("Mm", [128, 128], BF16, 4)
            TT, r_TT = sb2("TT", [128, 128], BF16)
            qkT, r_qkT = sb2("qkT", [128, 128], BF16, 4)
            u_, r_u = sb2("u", [128, 128], F32)
            wT2, r_wT2 = sb2("wT2", [128, 128], BF16)
            vn, r_vn = sb2("vn", [128, 128], BF16)
            osb, r_osb = sb2("osb", [128, 128], F32, 3)
            tS, r_tS = sb2("tS", [128, 128], F32)
            Sf = P.sbuf("Sf", [128, 128], F32, st)
            Sb = P.sbuf("Sb", [128, 128], BF16, st)
            r_S, r_Sb = R(), R()
            P.barrier()
            r_pq = [[R() for _ in range(4)] for _ in range(5)]
            ptb = P.psum("ptb", [128, 4, 128], BF16, st)
            r_ptb = [R() for _ in range(4)]
            psc = P.psum("psc", [128, 4, 128], F32, st)
            r_psc = [R() for _ in range(4)]
            cq = {}

            def scan_ps(kind):
                i = {"V": 0, "O": 1, "S": 2}[kind]
                return psc[:, i, :], r_psc[0]

            def PQ(kind, nb, bank):
                i = cq.get(kind, 0)
                cq[kind] = i + 1
                qn_ = nb[i % len(nb)]
                return pq[bank][:, qn_, :], r_pq[bank][0]

            cnt = {}

            def rot(name, n):
                i = cnt.get(name, 0)
                cnt[name] = i + 1
                return i % n

            CUT = os.environ.get("DN_CUT", "")
            ACTV = os.environ.get("ACTV", "")

            def ACTKW(b):
                if ACTV == "v1":
                    return dict()
                if ACTV == "v2":
                    return dict(bias=b)
                return dict(bias=b)

            class _Stop(Exception):
                pass

            def cut(tag):
                if CUT == tag:
                    raise _Stop()

            for hp in range(0 if DN_STOP == "B" else 4):
              try:
                  for dr in range(2):
                      P.op("pool", lambda e: e.memset(Sf[:], 0.0), writes=[r_S])
                      P.op("pool", lambda e: e.memset(Sb[:], 0.0), writes=[r_Sb])
                      odst, rodst = (d["o_f"], rd["o_f"]) if dr == 0 else (d["o_b"], rd["o_b"])
                      for step in range(NCP):
                          cp = step if dr == 0 else NCP - 1 - step
                          t0 = cp * 128
                          i3 = rot("ld", 3)
                          P.dma("sp", lambda e, i3=i3, hp=hp, t0=t0: e.dma_start(
                              out=kq[i3][:, 0, :], in_=d["dkT"][hp * 128:(hp + 1) * 128, t0:t0 + 128]),
                              reads=[rd["dkT"]], writes=[r_kq[i3]])
                          P.dma("sp", lambda e, i3=i3, hp=hp, t0=t0: e.dma_start(
                              out=kq[i3][:, 1, :], in_=d["dqT"][hp * 128:(hp + 1) * 128, t0:t0 + 128]),
                              reads=[rd["dqT"]], writes=[r_kq[i3]])
                          P.dma("sp", lambda e, i3=i3, hp=hp, t0=t0: e.dma_start(
                              out=ktm[i3][:].rearrange("p a d -> p (a d)"), in_=d["dk_tm"][t0:t0 + 128, hp * 128:(hp + 1) * 128]),
                              reads=[rd["dk_tm"]], writes=[r_ktm[i3]])
                          P.dma("sp", lambda e, i3=i3, hp=hp, t0=t0: e.dma_start(
                              out=vtm[i3][:].rearrange("p a d -> p (a d)"), in_=d["dv_tm"][t0:t0 + 128, hp * 128:(hp + 1) * 128]),
                              reads=[rd["dv_tm"]], writes=[r_vtm[i3]])
                          kT2 = kq[i3][:, 0, :]
                          qT2 = kq[i3][:, 1, :]
                          i2 = rot("u", 2)

                          def col(name, h0, n=2, dr=dr, cp=cp):
                              return ga[name][:, dr, cp, h0:h0 + n]
                          P.op("dve", lambda e, i2=i2, c=col("g", 2 * hp): e.tensor_copy(
                              out=G2[i2][:], in_=c.unsqueeze(2).broadcast_to([128, 2, 64])), reads=G, writes=[r_G2[i2]])
                          G2f = G2[i2][:].rearrange("p a d -> p (a d)")
                          pE, rpE = PQ("E", [0, 1], 0)
                          P.op("pe", lambda e, pE=pE, G2f=G2f, dr=dr: e.matmul(pE, lhsT=G2f, rhs=self.tri[dr], start=True, stop=True),
                               reads=[r_G2[i2], self.r_const], writes=[rpE])
                          P.op("act", lambda e, pE=pE, i2=i2: e.activation(out=Eg[i2][:], in_=pE, func=AF.Exp),
                               reads=[rpE], writes=[r_Eg[i2]])
                          P.op("dve", lambda e, i2=i2, qT2=qT2: e.tensor_tensor(out=qdec[i2][:], in0=qT2, in1=Eg[i2][:], op=ALU.mult),
                               reads=[r_kq[i3], r_Eg[i2]], writes=[r_qdec[i2]])
                          pG, rpG = PQ("G", [2, 3], 0)
                          P.op("pe", lambda e, pG=pG, G2f=G2f: e.matmul(pG[:, 0:2], lhsT=G2f, rhs=self.selc, start=True, stop=True),
                               reads=[r_G2[i2], self.r_const], writes=[rpG])
                          P.op("act", lambda e, pG=pG, i2=i2: e.activation(out=gtc[i2][:], in_=pG[:, 0:2], func=AF.Exp),
                               reads=[rpG], writes=[r_gtc[i2]])
                          for (dst_, rdst_, src_, rsrc_, cname) in ((vb_, r_vb, vtm, r_vtm, "beta"), (kbe, r_kbe, ktm, r_ktm, "be"),
                                                                    (kdec, r_kdec, ktm, r_ktm, "e2")):
                              P.op("dve", lambda e, i2=i2, i3=i3, dst_=dst_, src_=src_, c=col(cname, 2 * hp): e.tensor_tensor(
                                  out=dst_[i2][:], in0=src_[i3][:], in1=c.unsqueeze(2).broadcast_to([128, 2, 64]), op=ALU.mult),
                                  reads=[rsrc_[i3]] + G, writes=[rdst_[i2]])
                          cut("C1")
                          pU, rpU = PQ("U", [0, 1], 3)
                          qk_idx = []
                          for a in range(2):
                              h = 2 * hp + a
                              pa = slice(64 * a, 64 * a + 64)
                              ia = rot("a", 2)
                              P.op("dve", lambda e, ia=ia, c=col("g", h, 1): e.tensor_copy(
                                  out=G1[ia][:], in_=c.broadcast_to([128, 128])), reads=G, writes=[r_G1[ia]])
                              P.op("dve", lambda e, ia=ia, c=col("gc", h, 1): e.tensor_copy(out=bcol[ia][:, 0:1], in_=c),
                                   reads=G, writes=[r_bcol[ia]])
                              P.op("dve", lambda e, ia=ia, c=col("ngc", h, 1): e.tensor_copy(out=bcol[ia][:, 8:9], in_=c),
                                   reads=G, writes=[r_bcol[ia]])
                              cut("C1a")
                              pA, rpA = PQ("A", [0, 1], 1)
                              P.op("dve", lambda e, ia=ia, c=col("g", h, 1): e.tensor_scalar(
                                  out=G1n[ia][:], in0=c.broadcast_to([128, 128]), scalar1=-1.0, scalar2=None, op0=ALU.mult),
                                  reads=G, writes=[r_G1n[ia]])
                              P.op("pe", lambda e, pA=pA, ia=ia, dr=dr: e.matmul(pA, lhsT=G1n[ia][:], rhs=self.tri[dr], start=True, stop=False),
                                   reads=[r_G1n[ia], self.r_const], writes=[rpA])
                              P.op("pe", lambda e, pA=pA, dr=dr: e.matmul(pA, lhsT=self.identf, rhs=self.maskS[dr], start=False, stop=True),
                                   reads=[self.r_const], writes=[rpA])
                              cut("C1b")
                              P.op("act", lambda e, pA=pA, ia=ia, c=col("gc", h, 1): e.activation(
                                  out=Dm[ia][:], in_=pA, func=AF.Exp, **ACTKW(bcol[ia][:, 0:1])), reads=[rpA, r_bcol[ia]], writes=[r_Dm[ia]])
                              cut("C1c")
                              pB, rpB = PQ("B", [2, 3], 1)
                              P.op("pe", lambda e, pB=pB, ia=ia, dr=dr: e.matmul(pB, lhsT=(G1n if os.environ.get("BX") == "1" else G1)[ia][:], rhs=self.tri[dr], start=True, stop=False),
                                   reads=[r_G1[ia], r_G1n[ia], self.r_const], writes=[rpB])
                              P.op("pe", lambda e, pB=pB, dr=dr: e.matmul(pB, lhsT=self.identf, rhs=self.maskI[dr], start=False, stop=True),
                                   reads=[self.r_const], writes=[rpB])
                              cut("C1c2")
                              P.op("act", lambda e, pB=pB, ia=ia, c=col("ngc", h, 1): e.activation(
                                  out=DmT[ia][:], in_=pB, func=AF.Exp, bias=bcol[ia][:, 8:9]), reads=[rpB, r_bcol[ia]], writes=[r_DmT[ia]])
                              cut("C1d")
                              pK, rpK = PQ("K", [0, 1], 2)
                              P.op("pe", lambda e, pK=pK, kT2=kT2, pa=pa: e.matmul(pK, lhsT=kT2[pa, :], rhs=kT2[pa, :], start=True, stop=True),
                                   reads=[r_kq[i3]], writes=[rpK])
                              n0 = rot("N", 4)
                              P.op("dve", lambda e, pK=pK, n0=n0, ia=ia, c=col("nbeta", h, 1): e.scalar_tensor_tensor(
                                  out=Nn[n0][:], in0=pK, scalar=c, in1=Dm[ia][:], op0=ALU.mult, op1=ALU.mult),
                                  reads=[rpK, r_Dm[ia]] + G, writes=[r_Nn[n0]])
                              cut("C1e")
                              pQ_, rpQ = PQ("Q", [2, 3], 2)
                              P.op("pe", lambda e, pQ_=pQ_, kT2=kT2, qT2=qT2, pa=pa: e.matmul(pQ_, lhsT=kT2[pa, :], rhs=qT2[pa, :],
                                                                                            start=True, stop=True),
                                   reads=[r_kq[i3]], writes=[rpQ])
                              iq = rot("qk", 4)
                              qk_idx.append(iq)
                              P.op("dve", lambda e, pQ_=pQ_, iq=iq, ia=ia: e.tensor_tensor(out=qkT[iq][:], in0=pQ_, in1=DmT[ia][:], op=ALU.mult),
                                   reads=[rpQ, r_DmT[ia]], writes=[r_qkT[iq]])
                              cut("C2")
                              tq = rot("tb", 4)
                              P.op("pe", lambda e, tq=tq, n0=n0: e.transpose(out=ptb[:, tq, :], in_=Nn[n0][:], identity=self.identb[:]),
                                   reads=[r_Nn[n0], self.r_const], writes=[r_ptb[0]])
                              m0 = rot("M", 4)
                              P.op("act", lambda e, tq=tq, m0=m0: e.copy(out=Mm[m0][:], in_=ptb[:, tq, :]),
                                   reads=[r_ptb[0]], writes=[r_Mm[m0]])
                              P.op("pool", lambda e, ia=ia, m0=m0: e.tensor_tensor(out=TT[ia][:], in0=Mm[m0][:], in1=self.identb[:], op=ALU.add),
                                   reads=[r_Mm[m0], self.r_const], writes=[r_TT[ia]])
                              nk, mk = n0, m0
                              for k in range(1, 6):
                                  pN, rpN = PQ("N", [0, 1], 4)
                                  P.op("pe", lambda e, pN=pN, nk=nk, mk=mk: e.matmul(pN, lhsT=Mm[mk][:], rhs=Nn[nk][:], start=True, stop=True),
                                       reads=[r_Mm[mk], r_Nn[nk]], writes=[rpN])
                                  n1 = rot("N", 4)
                                  P.op("act", lambda e, pN=pN, n1=n1: e.copy(out=Nn[n1][:], in_=pN), reads=[rpN], writes=[r_Nn[n1]])
                                  m1 = mk
                                  if k < 5:
                                      pM, rpM = PQ("Mq", [2], 4)
                                      P.op("pe", lambda e, pM=pM, nk=nk, mk=mk: e.matmul(pM, lhsT=Nn[nk][:], rhs=Mm[mk][:], start=True, stop=True),
                                           reads=[r_Mm[mk], r_Nn[nk]], writes=[rpM])
                                      m1 = rot("M", 4)
                                      P.op("dve", lambda e, pM=pM, m1=m1: e.tensor_copy(out=Mm[m1][:], in_=pM), reads=[rpM], writes=[r_Mm[m1]])
                                  pT_, rpT = PQ("T", [3], 4)
                                  P.op("pe", lambda e, pT_=pT_, n1=n1, ia=ia: e.matmul(pT_, lhsT=Nn[n1][:], rhs=TT[ia][:], start=True, stop=True),
                                       reads=[r_Nn[n1], r_TT[ia]], writes=[rpT])
                                  P.op("dve", lambda e, pT_=pT_, ia=ia: e.tensor_tensor(out=TT[ia][:], in0=pT_, in1=TT[ia][:], op=ALU.add),
                                       reads=[rpT, r_TT[ia]], writes=[r_TT[ia]])
                                  nk, mk = n1, m1
                              cut("C3")
                              P.op("pe", lambda e, pU=pU, ia=ia, i2=i2, a=a: e.matmul(pU[:, 64 * a:64 * a + 64], lhsT=TT[ia][:], rhs=vb_[i2][:, a, :],
                                                                                     start=True, stop=True),
                                   reads=[r_TT[ia], r_vb[i2]], writes=[rpU])
                              pW, rpW = PQ("W", [2, 3], 3)
                              P.op("pe", lambda e, pW=pW, ia=ia, i2=i2: e.matmul(pW, lhsT=kbe[i2][:].rearrange("p a d -> p (a d)"), rhs=TT[ia][:],
                                                                                start=True, stop=True),
                                   reads=[r_TT[ia], r_kbe[i2]], writes=[rpW])
                              P.op("act", lambda e, pW=pW, i2=i2, pa=pa: e.copy(out=wT2[i2][pa, :], in_=pW[pa, :]),
                                   reads=[rpW], writes=[r_wT2[i2]])
                          P.op("act", lambda e, pU=pU, i2=i2: e.copy(out=u_[i2][:], in_=pU), reads=[rpU], writes=[r_u[i2]])
                          cut("C4")
                          for c2 in ((0, 1) if dr == 0 else (1, 0)):
                              ch = slice(64 * c2, 64 * c2 + 64)
                              pV, rpV = PQ("sV", [0, 1], 2) if False else scan_ps("V")
                              P.op("pe", lambda e, pV=pV, i2=i2: e.matmul(pV, lhsT=wT2[i2][:], rhs=Sb[:], start=True, stop=True),
                                   reads=[r_wT2[i2], r_Sb], writes=[rpV])
                              iv = rot("vn", 2)
                              P.op("dve", lambda e, pV=pV, i2=i2, iv=iv, ch=ch: e.tensor_tensor(out=vn[iv][ch, :], in0=u_[i2][ch, :], in1=pV[ch, :],
                                                                                             op=ALU.subtract),
                                   reads=[rpV, r_u[i2]], writes=[r_vn[iv]])
                              pO, rpO = scan_ps("O")
                              P.op("pe", lambda e, pO=pO, i2=i2: e.matmul(pO, lhsT=qdec[i2][:], rhs=Sb[:], start=True, stop=False),
                                   reads=[r_qdec[i2], r_Sb], writes=[rpO])
                              for a in range(2):
                                  P.op("pe", lambda e, pO=pO, a=a, iv=iv, ch=ch, iq=qk_idx[a]: e.matmul(
                                      pO[:, 64 * a:64 * a + 64], lhsT=qkT[iq][ch, :], rhs=vn[iv][ch, 64 * a:64 * a + 64],
                                      start=False, stop=(a == 1)), reads=[r_qkT[qk_idx[a]], r_vn[iv]], writes=[rpO])
                              io = rot("o", 3)
                              P.op("act", lambda e, pO=pO, io=io, ch=ch: e.copy(out=osb[io][ch, :], in_=pO[ch, :]),
                                   reads=[rpO], writes=[r_osb[io]])
                              P.dma("sp", lambda e, io=io, ch=ch, t0=t0, c2=c2, hp=hp, odst=odst: e.dma_start(
                                  out=odst[t0 + 64 * c2:t0 + 64 * c2 + 64, hp * 128:(hp + 1) * 128], in_=osb[io][ch, :]),
                                  reads=[r_osb[io]], writes=[rodst])
                              pS, rpS = scan_ps("S")
                              P.op("pe", lambda e, pS=pS, i2=i2, iv=iv, ch=ch: e.matmul(
                                  pS, lhsT=kdec[i2][ch].rearrange("p a d -> p (a d)"), rhs=vn[iv][ch, :], start=True, stop=True),
                                  reads=[r_kdec[i2], r_vn[iv]], writes=[rpS])
                              its = rot("tS", 2)
                              P.op("dve", lambda e, pS=pS, its=its: e.tensor_tensor(out=tS[its][:], in0=pS, in1=self.blk_f, op=ALU.mult),
                                   reads=[rpS, self.r_const], writes=[r_tS[its]])
                              P.op("dve", lambda e, its=its, i2=i2, c2=c2: e.scalar_tensor_tensor(
                                  out=Sf[:], in0=Sf[:], scalar=gtc[i2][:, c2:c2 + 1], in1=tS[its][:], op0=ALU.mult, op1=ALU.add),
                                  reads=[r_S, r_gtc[i2], r_tS[its]], writes=[r_S])
                              P.op("pool", lambda e: e.tensor_copy(out=Sb[:], in_=Sf[:]), reads=[r_S], writes=[r_Sb])
              except _Stop:
                break
            P.barrier()
        if DN_STOP in ("B", "C"):
            return
        with contextlib.ExitStack() as st:
            gng = P.sbuf("gng", [128, 64], F32, st)
            of_ = [P.sbuf("of", [128, 512], F32, st) for _ in range(2)]
            ob_ = [P.sbuf("obk", [128, 512], F32, st) for _ in range(2)]
            zt = [P.sbuf("zt", [128, 512], F32, st) for _ in range(2)]
            sq = [P.sbuf("sqd", [128, 512], F32, st) for _ in range(2)]
            ss = [P.sbuf("ss", [128, 8], F32, st) for _ in range(2)]
            obf = [P.sbuf("obf", [128, 512], BF16, st) for _ in range(2)]
            oT = [P.sbuf("oT", [128, 4, 128], BF16, st) for _ in range(2)]
            pst = [P.psum("pstd", [128, 4, 128], BF16, st) for _ in range(2)]
            r_gng = R()
            r_of, r_ob, r_zt, r_sq, r_ss, r_obf, r_oT, r_pst = ([R(), R()] for _ in range(8))
            P.dma("sp", lambda e: e.dma_start(out=gng[:], in_=d["dn_out_norm_g"][l].partition_broadcast(128)), writes=[r_gng])
            odT = d["odT"].rearrange("(kt p) n -> p kt n", p=128)
            for tt in range(NCP):
                b = tt % 2
                t0 = tt * 128
                P.dma("sp", lambda e, b=b, t0=t0: e.dma_start(out=of_[b][:], in_=d["o_f"][t0:t0 + 128, :]), reads=[rd["o_f"]], writes=[r_of[b]])
                P.dma("sp", lambda e, b=b, t0=t0: e.dma_start(out=ob_[b][:], in_=d["o_b"][t0:t0 + 128, :]), reads=[rd["o_b"]], writes=[r_ob[b]])
                P.dma("sp", lambda e, b=b, t0=t0: e.dma_start(out=zt[b][:], in_=d["z"][t0:t0 + 128, :]), reads=[rd["z"]], writes=[r_zt[b]])
                P.op("pool", lambda e, b=b: e.tensor_tensor(out=of_[b][:], in0=of_[b][:], in1=ob_[b][:], op=ALU.add),
                     reads=[r_of[b], r_ob[b]], writes=[r_of[b]])
                P.op("act", lambda e, b=b: e.activation(out=sq[b][:], in_=of_[b][:], func=AF.Square), reads=[r_of[b]], writes=[r_sq[b]])
                P.op("dve", lambda e, b=b: e.tensor_reduce(out=ss[b][:], in_=sq[b][:].rearrange("p (h d) -> p h d", h=8),
                                                           axis=mybir.AxisListType.X, op=ALU.add), reads=[r_sq[b]], writes=[r_ss[b]])
                P.op("act", lambda e, b=b: e.activation(out=ss[b][:], in_=ss[b][:], func=AF.Sqrt, bias=self.eps_c[:], scale=1.0 / 64),
                     reads=[r_ss[b], self.r_const], writes=[r_ss[b]])
                P.op("dve", lambda e, b=b: e.reciprocal(out=ss[b][:], in_=ss[b][:]), reads=[r_ss[b]], writes=[r_ss[b]])
                P.op("dve", lambda e, b=b: e.tensor_tensor(
                    out=of_[b][:].rearrange("p (h d) -> p h d", h=8), in0=of_[b][:].rearrange("p (h d) -> p h d", h=8),
                    in1=ss[b][:].unsqueeze(2).broadcast_to([128, 8, 64]), op=ALU.mult), reads=[r_of[b], r_ss[b]], writes=[r_of[b]])
                P.op("dve", lambda e, b=b: e.tensor_tensor(
                    out=of_[b][:].rearrange("p (h d) -> p h d", h=8), in0=of_[b][:].rearrange("p (h d) -> p h d", h=8),
                    in1=gng[:].unsqueeze(1).broadcast_to([128, 8, 64]), op=ALU.mult), reads=[r_of[b], r_gng], writes=[r_of[b]])
                P.op("act", lambda e, b=b: e.activation(out=zt[b][:], in_=zt[b][:], func=AF.Silu), reads=[r_zt[b]], writes=[r_zt[b]])
                P.op("dve", lambda e, b=b: e.tensor_tensor(out=obf[b][:], in0=of_[b][:], in1=zt[b][:], op=ALU.mult),
                     reads=[r_of[b], r_zt[b]], writes=[r_obf[b]])
                for kt in range(4):
                    P.op("pe", lambda e, b=b, kt=kt: e.transpose(out=pst[b][:, kt, :], in_=obf[b][:, kt * 128:(kt + 1) * 128],
                                                                 identity=self.identb[:]),
                         reads=[r_obf[b], self.r_const], writes=[r_pst[b]])
                P.op("act", lambda e, b=b: e.copy(out=oT[b][:], in_=pst[b][:]), reads=[r_pst[b]], writes=[r_oT[b]])
                P.dma("sp", lambda e, b=b, t0=t0: e.dma_start(out=odT[:, :, t0:t0 + 128], in_=oT[b][:]),
                      reads=[r_oT[b]], writes=[rd["odT"]])
            P.barrier()


WEIGHT_SPECS = [
    ("norm_mix_g", (DEPTH, D)), ("w_in", (DEPTH, D, NIN)), ("q_norm_g", (DEPTH, 64)), ("k_norm_g", (DEPTH, 64)),
    ("dn_conv_w", (DEPTH, 5, 1536)), ("dn_a_log", (DEPTH, 2, 8)), ("dn_dt_bias", (DEPTH, 2, 8)),
    ("dn_out_norm_g", (DEPTH, 64)), ("w_o_attn", (DEPTH, 512, D)), ("w_o_dn", (DEPTH, 512, D)),
    ("w_out", (DEPTH, D, D)), ("norm_ffn_g", (DEPTH, D)), ("w_up", (DEPTH, D, 2 * DFF)),
    ("ffn_conv_w", (DEPTH, 3, 2 * DFF)), ("w_down", (DEPTH, DFF, D)),
]


def host_consts(T):
    blk = np.zeros((128, 128), np.float32)
    blk[:64, :64] = 1
    blk[64:, 64:] = 1
    rot = np.zeros((128, 128), np.float32)
    for h2 in range(2):
        for ax in range(2):
            for f in range(16):
                m0 = h2 * 64 + ax * 32 + f
                m1 = m0 + 16
                rot[m1, m0] = -1.0
                rot[m0, m1] = 1.0
    ident = np.eye(128, dtype=np.float32)
    p = np.arange(128)[:, None]
    f = np.arange(128)[None, :]
    same = (p // 64) == (f // 64)
    triF = (same & (p <= f)).astype(np.float32)
    triB = (same & (p >= f)).astype(np.float32)
    BIG = 30000.0
    mSf = np.where(same & (p > f), 0.0, -BIG).astype(np.float32)
    mSb = np.where(same & (p < f), 0.0, -BIG).astype(np.float32)
    mIf = np.where(same & (f >= p), 0.0, -BIG).astype(np.float32)
    mIb = np.where(same & (f <= p), 0.0, -BIG).astype(np.float32)
    z = np.zeros((128, 128), np.float32)
    selc = np.zeros((128, 2), np.float32)
    selc[:64, 0] = 1
    selc[64:, 1] = 1
    cf = np.concatenate([blk, rot, ident, triF, triB, mSf, mSb, mIf, mIb, -triF, -triB, z, z, selc], axis=1)
    t = np.arange(T)
    row = (t // 64).astype(np.float32)
    col = (t % 64).astype(np.float32)
    inv = (np.float32(10000.0) ** (-np.arange(16, dtype=np.float32) / np.float32(16))).astype(np.float32)
    ang = np.stack([row[:, None] * inv, col[:, None] * inv], axis=1)
    c = np.cos(ang).astype(np.float32)
    sn = np.sin(ang).astype(np.float32)
    C = np.zeros((128, T), np.float32)
    S = np.zeros((128, T), np.float32)
    for h2 in range(2):
        for ax in range(2):
            for half in range(2):
                r0 = h2 * 64 + ax * 32 + half * 16
                C[r0:r0 + 16] = c[:, ax, :].T
                S[r0:r0 + 16] = sn[:, ax, :].T
    return dict(cf32=cf, ropec=C, ropes=S)


def build(T=SEQ, depth=DEPTH, with_dn=True, with_ffn=True, debug=False, stages="padm"):
    nc = bass.Bass("TRN2", target_bir_lowering=False)
    B = Builder(nc, T=T)
    B.debug = debug
    P = B.P
    x0 = B.dram_in("xT", [D, T], F32)
    for name, shp in WEIGHT_SPECS:
        B.dram_in(name, list(shp), F32)
    B.dram_in("cf32", [128, NCF], F32)
    B.dram_in("ropec", [128, T], F32)
    B.dram_in("ropes", [128, T], F32)
    out = B.dram_out("yT", [D, T], F32)
    xa = B.dram_tmp("xa", [D, T], F32)
    xb = B.dram_tmp("xb", [D, T], F32)
    B.dram_tmp("qaT", [512, T], BF16)
    B.dram_tmp("kaT", [256, T], BF16)
    B.dram_tmp("va", [T, 130], BF16)
    B.dram_tmp("dpre", [1536, T + 4], F32)
    B.dram_tmp("bd", [T, 32], F32)
    B.dram_tmp("z", [T, 512], F32)
    B.dram_tmp("gT", [2048, T], BF16)
    B.dram_tmp("oaT", [512, T], BF16)
    B.dram_tmp("odT", [512, T], BF16)
    B.dram_tmp("dqT", [512, T], BF16)
    B.dram_tmp("dkT", [512, T], BF16)
    B.dram_tmp("dk_tm", [T, 512], BF16)
    B.dram_tmp("dv_tm", [T, 512], BF16)
    B.dram_tmp("o_f", [T, 512], F32)
    B.dram_tmp("o_b", [T, 512], F32)
    B.rd = {k: P.res(k) for k in ("qaT", "kaT", "va", "dpre", "bd", "z", "gT", "oaT", "odT", "dqT", "dkT", "dk_tm", "dv_tm",
                                    "o_f", "o_b")}
    rx = {"x0": P.res(), "xa": P.res(), "xb": P.res(), "out": P.res()}
    with P.stack:
        B.load_consts()
        tch = P.sbuf("touch", [1, 16], F32)
        rt = P.res()
        for name, shp in WEIGHT_SPECS:
            ap = B.d[name]
            idx = tuple([0] * (len(shp) - 1))
            P.dma("sp", lambda e, ap=ap, idx=idx: e.dma_start(out=tch[0:1, 0:8], in_=ap[idx][0:8].rearrange("(o n) -> o n", o=1)),
                  writes=[rt])
        import os
        if not with_dn and "zero" not in os.environ.get("SKIP", ""):
            B.zero_dram(B.d["odT"], B.rd["odT"], bf=True)
        P.barrier()
        for l in range(depth):
            src, rsrc = (x0, rx["x0"]) if l == 0 else (xa, rx["xa"])
            last = (l == depth - 1)
            if "p" in stages:
                B.proj(l, src, rsrc)
            if "a" in stages:
                B.attention(l)
            if with_dn and "d" in stages:
                B.deltanet(l)
            if "m" not in stages:
                continue
            if with_ffn:
                B.merge(l, src, xb, rsrc, rx["xb"])
                dst, rdst = (out, rx["out"]) if last else (xa, rx["xa"])
                B.ffn(l, xb, dst, rx["xb"], rdst)
            else:
                B.merge(l, src, out, rsrc, rx["out"])
        P.emit()
        B.nc_stats = P.stats
    return nc, B


_CACHE = {}


def kernel(**inputs):
    T = SEQ
    if "nc" not in _CACHE:
        _CACHE["nc"] = build(T=T, depth=DEPTH)[0]
        _CACHE["consts"] = host_consts(T)
    nc = _CACHE["nc"]
    x = np.asarray(inputs["x"], dtype=np.float32)
    nb = x.shape[0]
    base = {k: np.ascontiguousarray(np.asarray(inputs[k], dtype=np.float32)) for k, _ in WEIGHT_SPECS}
    base.update(_CACHE["consts"])
    in_maps = []
    for c in range(8):
        m = dict(base)
        m["xT"] = np.ascontiguousarray(x[c % nb].T)
        in_maps.append(m)
    res = run_bass_kernel_spmd(nc, in_maps, core_ids=list(range(8)))
    out = np.stack([np.ascontiguousarray(res.results[b]["yT"].T) for b in range(nb)], axis=0)
    return out.astype(np.float32)
```

```python
import contextlib
import numpy as np
import ml_dtypes
import concourse.bass as bass
import concourse.mybir as mybir
from concourse.bass_utils import run_bass_kernel_spmd

F32 = mybir.dt.float32
BF16 = mybir.dt.bfloat16
ALU = mybir.AluOpType
AF = mybir.ActivationFunctionType

D = 1024
SEQ = 8192
DEPTH = 4
NIN = 4896
DFF = 2816
EPS = 1e-6

SEM_WRAP = 6000
DMA_RING = 8
NCF = 13 * 128 + 2


class Res:
    __slots__ = ("name", "last_w", "readers")

    def __init__(self, name):
        self.name = name
        self.last_w = None
        self.readers = []


class Op:
    __slots__ = ("eng", "fn", "deps", "is_dma", "has_dep", "sem", "val", "slot_prev", "barrier")

    def __init__(self, eng, fn, is_dma):
        self.eng = eng
        self.fn = fn
        self.deps = []
        self.is_dma = is_dma
        self.has_dep = False
        self.sem = None
        self.val = 0
        self.slot_prev = None
        self.barrier = False


class Prog:
    ENGS = ("pe", "act", "dve", "pool", "sp")

    def __init__(self, nc):
        self.nc = nc
        self.ops = []
        self.stack = contextlib.ExitStack()
        self.res_all = []
        self.n = 0
        self.last_op = {e: None for e in self.ENGS}
        self.dma_last = {e: {} for e in self.ENGS}
        self.dma_cnt = {e: 0 for e in self.ENGS}
        self.bar_deps = {e: None for e in self.ENGS}

    def sbuf(self, name, shape, dt, stack=None):
        self.n += 1
        t = (stack or self.stack).enter_context(self.nc.sbuf_tensor(f"{name}_{self.n}", list(shape), dt))
        return t

    def psum(self, name, shape, dt, stack=None):
        self.n += 1
        t = (stack or self.stack).enter_context(self.nc.psum_tensor(f"{name}_{self.n}", list(shape), dt))
        return t

    def res(self, name="r"):
        r = Res(name)
        self.res_all.append(r)
        return r

    def _add(self, eng, fn, reads, writes, is_dma):
        o = Op(eng, fn, is_dma)
        deps = []
        seen = set()

        def add(d):
            if d is not None and id(d) not in seen:
                seen.add(id(d))
                deps.append(d)

        for r in reads:
            add(r.last_w)
        for w in writes:
            add(w.last_w)
            for rd in w.readers:
                add(rd)
        for r in reads:
            if not is_dma:
                r.readers = [x for x in r.readers if x.is_dma or x.eng != eng]
            r.readers.append(o)
        for w in writes:
            w.last_w = o
            w.readers = []
        if self.bar_deps[eng] is not None:
            for d in self.bar_deps[eng]:
                add(d)
            self.bar_deps[eng] = None
        if eng == "pe" and not is_dma:
            deps = [d for d in deps if d.is_dma or d.eng != "pe"]
        o.deps = deps
        for d in deps:
            d.has_dep = True
        if is_dma:
            i = self.dma_cnt[eng]
            self.dma_cnt[eng] += 1
            slot = i % DMA_RING
            o.val = 16 * (i // DMA_RING + 1)
            o.sem = (eng, slot)
            o.slot_prev = self.dma_last[eng].get(slot)
            self.dma_last[eng][slot] = o
        else:
            self.last_op[eng] = o
        self.ops.append(o)
        return o

    def op(self, eng, fn, reads=(), writes=()):
        return self._add(eng, fn, reads, writes, False)

    def dma(self, eng, fn, reads=(), writes=()):
        return self._add(eng, fn, reads, writes, True)

    def _tails(self):
        b = [o for o in self.last_op.values() if o is not None]
        for q in self.dma_last.values():
            b.extend(q.values())
        for o in b:
            o.has_dep = True
        return b

    def barrier(self):
        b = self._tails()
        for e in self.ENGS:
            self.bar_deps[e] = list(b)

    def emit(self):
        nc = self.nc
        ops = self.ops
        final = self._tails()
        per_eng = {e: [] for e in self.ENGS}
        cnt = {e: 0 for e in self.ENGS}
        semlist = {e: [] for e in self.ENGS}
        dma_ring = {}
        stack = self.stack

        def new_sem(name):
            return stack.enter_context(nc.semaphore(name))

        for o in ops:
            e = o.eng
            if o.is_dma:
                if o.sem not in dma_ring:
                    dma_ring[o.sem] = new_sem(f"dq_{o.sem[0]}_{o.sem[1]}")
                o.sem = dma_ring[o.sem]
            elif o.has_dep:
                c = cnt[e]
                cnt[e] += 1
                si = c // SEM_WRAP
                if len(semlist[e]) <= si:
                    semlist[e].append(new_sem(f"s_{e}_{si}"))
                o.sem = semlist[e][si]
                o.val = c % SEM_WRAP + 1
            per_eng[e].append(o)
        self.stats = dict(cnt=cnt, dma=dict(self.dma_cnt), nops=len(ops),
                          nsem=sum(len(v) for v in semlist.values()) + len(dma_ring))

        engmap = {"pe": "tensor", "act": "scalar", "dve": "vector", "pool": "gpsimd", "sp": "sync"}

        def run_engine(ename, eng):
            waited = {}

            def wait(sem, val):
                k = id(sem)
                if waited.get(k, 0) >= val:
                    return
                waited[k] = val
                eng.wait_ge(sem, val)

            for o in per_eng[ename]:
                for d in o.deps:
                    if d.sem is None:
                        continue
                    if (not d.is_dma) and d.eng == ename and ename == "pe":
                        continue
                    wait(d.sem, d.val)
                if o.is_dma:
                    if o.slot_prev is not None:
                        wait(o.slot_prev.sem, o.slot_prev.val)
                    o.fn(eng).then_inc(o.sem, 16)
                else:
                    ins = o.fn(eng)
                    if o.sem is not None:
                        ins.then_inc(o.sem, 1)
            for d in final:
                wait(d.sem, d.val)

        with nc.Block() as block:
            for ename in self.ENGS:
                getattr(block, engmap[ename])(lambda eng, ename=ename: run_engine(ename, eng))


def _bf(a):
    return np.ascontiguousarray(a).astype(ml_dtypes.bfloat16)


class Builder:
    def __init__(self, nc, T=SEQ):
        self.nc = nc
        self.T = T
        self.P = Prog(nc)
        self.d = {}
        self.debug = False

    def dram_in(self, name, shape, dt):
        self.d[name] = self.nc.dram_tensor(name, list(shape), dt, kind="ExternalInput").ap()
        return self.d[name]

    def dram_out(self, name, shape, dt):
        self.d[name] = self.nc.dram_tensor(name, list(shape), dt, kind="ExternalOutput").ap()
        return self.d[name]

    def dram_tmp(self, name, shape, dt):
        self.d[name] = self.nc.dram_tensor(name, list(shape), dt,
                                           kind="ExternalOutput" if self.debug else "Internal").ap()
        return self.d[name]

    def load_consts(self):
        P, nc = self.P, self.nc
        self.ones_f = P.sbuf("ones_f", [128, 128], F32)
        r = self.r_const = P.res("const")
        P.op("pool", lambda e: e.memset(self.ones_f[:], 1.0), writes=[r])
        self.eps_c = P.sbuf("eps_c", [128, 1], F32)
        P.op("pool", lambda e: e.memset(self.eps_c[:], EPS), writes=[r])
        self.cf = P.sbuf("cf", [128, NCF], F32)
        P.dma("sp", lambda e: e.dma_start(out=self.cf[:], in_=self.d["cf32"][:, :]), writes=[r])
        self.blk_f = self.cf[:, 0:128]
        self.rot_f = self.cf[:, 128:256]
        c = lambda i: self.cf[:, i * 128:(i + 1) * 128]
        self.identf = c(2)
        self.tri = [c(3), c(4)]
        self.maskS = [c(5), c(6)]
        self.maskI = [c(7), c(8)]
        self.ntri = [c(9), c(10)]
        self.selc = self.cf[:, 13 * 128:13 * 128 + 2]
        self.identb = P.sbuf("identb", [128, 128], BF16)
        P.op("dve", lambda e: e.tensor_copy(out=self.identb[:], in_=self.identf), reads=[r], writes=[r])
        self.zero_f = P.sbuf("zero_f", [128, 512], F32)
        P.op("pool", lambda e: e.memset(self.zero_f[:], 0.0), writes=[r])
        self.zero_b = P.sbuf("zero_b", [128, 512], BF16)
        P.op("pool", lambda e: e.memset(self.zero_b[:], 0.0), writes=[r])

    def ffn(self, l, xin, xout, rx_in, rx_out):
        P, nc, T = self.P, self.nc, self.T
        d = self.d
        W = 510
        ntile = (T + W - 1) // W
        HC = DFF // 2
        NJ = HC // 128
        with contextlib.ExitStack() as st:
            g_sb = P.sbuf("ffn_g", [128, 8], F32, st)
            cw = P.sbuf("ffn_cw", [128, 3, 2 * NJ], F32, st)
            wup = P.sbuf("wup", [128, 8, 2 * HC], BF16, st)
            wdn = P.sbuf("wdn", [128, NJ, D], BF16, st)
            stg = [P.sbuf("stg", [128, HC], F32, st) for _ in range(2)]
            xt = [P.sbuf("xt", [128, 8, 512], F32, st) for _ in range(2)]
            xr = [P.sbuf("xr", [128, 512], F32, st) for _ in range(2)]
            sq = P.sbuf("sq", [128, 8, 512], F32, st)
            rstd = P.sbuf("rstd", [128, 512], F32, st)
            hT = P.sbuf("hT", [128, 8, 512], BF16, st)
            act = P.sbuf("act", [128, NJ, 512], BF16, st)
            tg = [P.sbuf("tg", [128, 512], F32, st) for _ in range(2)]
            tv = [P.sbuf("tv", [128, 512], F32, st) for _ in range(2)]
            sg = [P.sbuf("sg", [128, 512], F32, st) for _ in range(2)]
            xo = [P.sbuf("xo", [128, 512], F32, st) for _ in range(2)]
            ps_ss = P.psum("ps_ss", [128, 512], F32, st)
            ps_g = [P.psum("ps_g", [128, 512], F32, st) for _ in range(2)]
            ps_v = [P.psum("ps_v", [128, 512], F32, st) for _ in range(2)]
            ps_y = [P.psum("ps_y", [128, 512], F32, st) for _ in range(2)]
            R = P.res
            r_g, r_cw, r_wup, r_wdn = R(), R(), R(), R()
            r_stg = [R(), R()]
            r_xt = [R(), R()]
            r_xr = [R(), R()]
            r_sq, r_rstd, r_hT, r_act = R(), R(), R(), R()
            r_tg, r_tv, r_sg, r_xo = [R(), R()], [R(), R()], [R(), R()], [R(), R()]
            r_pss = R()
            r_psg, r_psv, r_psy = [R(), R()], [R(), R()], [R(), R()]

            P.dma("sp", lambda e: e.dma_start(out=g_sb[:], in_=d["norm_ffn_g"][l].rearrange("(kt p) -> p kt", p=128),
                                              allow_slow_non_contiguous=True), writes=[r_g])
            xtiled_in = xin.rearrange("(kt p) n -> p kt n", p=128)
            xtiled_out = xout.rearrange("(kt p) n -> p kt n", p=128)
            for ph in range(2):
                c0 = ph * HC
                for tap in range(3):
                    for half in range(2):
                        P.dma("sp", lambda e, tap=tap, half=half, c0=c0: e.dma_start(
                            out=cw[:, tap, half * NJ:(half + 1) * NJ],
                            in_=d["ffn_conv_w"][l][tap, half * DFF + c0:half * DFF + c0 + HC].rearrange("(m p) -> p m", p=128),
                            allow_slow_non_contiguous=True), writes=[r_cw])
                k = 0
                for kt in range(8):
                    for half in range(2):
                        col = half * DFF + c0
                        s = k % 2
                        k += 1
                        P.dma("sp", lambda e, s=s, kt=kt, col=col: e.dma_start(
                            out=stg[s][:], in_=d["w_up"][l][kt * 128:(kt + 1) * 128, col:col + HC]),
                            writes=[r_stg[s]])
                        if half == 0:
                            P.op("dve", lambda e, s=s, kt=kt, half=half: e.tensor_scalar(
                                out=wup[:, kt, half * HC:(half + 1) * HC], in0=stg[s][:], scalar1=g_sb[:, kt:kt + 1],
                                scalar2=None, op0=ALU.mult), reads=[r_stg[s], r_g], writes=[r_wup])
                        else:
                            P.op("act", lambda e, s=s, kt=kt, half=half: e.activation(
                                out=wup[:, kt, half * HC:(half + 1) * HC], in_=stg[s][:], func=AF.Copy,
                                scale=g_sb[:, kt:kt + 1]), reads=[r_stg[s], r_g], writes=[r_wup])
                for j in range(NJ):
                    s = k % 2
                    k += 1
                    P.dma("sp", lambda e, s=s, j=j, c0=c0: e.dma_start(
                        out=stg[s][:, 0:D], in_=d["w_down"][l][c0 + j * 128:c0 + (j + 1) * 128, :]),
                        writes=[r_stg[s]])
                    if j % 2 == 0:
                        P.op("dve", lambda e, s=s, j=j: e.tensor_copy(out=wdn[:, j, :], in_=stg[s][:, 0:D]),
                             reads=[r_stg[s]], writes=[r_wdn])
                    else:
                        P.op("act", lambda e, s=s, j=j: e.copy(out=wdn[:, j, :], in_=stg[s][:, 0:D]),
                             reads=[r_stg[s]], writes=[r_wdn])
                for ti in range(ntile):
                    b = ti % 2
                    s0 = ti * W
                    nout = min(W, T - s0)
                    lo = max(s0 - 1, 0)
                    hi = min(s0 + nout + 1, T)
                    off = lo - (s0 - 1)
                    full = (off == 0 and hi - lo == 512)
                    if not full:
                        P.op("pool", lambda e, b=b: e.memset(xt[b][:], 0.0), writes=[r_xt[b]])
                    P.dma("sp", lambda e, b=b, lo=lo, hi=hi, off=off: e.dma_start(
                        out=xt[b][:, :, off:off + hi - lo], in_=xtiled_in[:, :, lo:hi]), reads=[rx_in], writes=[r_xt[b]])
                    P.op("act", lambda e, b=b: e.activation(out=sq[:], in_=xt[b][:], func=AF.Square),
                         reads=[r_xt[b]], writes=[r_sq])
                    for kt in range(8):
                        P.op("pe", lambda e, kt=kt: e.matmul(ps_ss[:], lhsT=self.ones_f[:], rhs=sq[:, kt, :],
                                                             start=(kt == 0), stop=(kt == 7)),
                             reads=[r_sq, self.r_const], writes=[r_pss])
                    P.op("act", lambda e: e.activation(out=rstd[:], in_=ps_ss[:], func=AF.Sqrt, bias=self.eps_c[:],
                                                       scale=1.0 / D), reads=[r_pss, self.r_const], writes=[r_rstd])
                    P.op("dve", lambda e: e.reciprocal(out=rstd[:], in_=rstd[:]), reads=[r_rstd], writes=[r_rstd])
                    for kt in range(8):
                        P.op("dve", lambda e, b=b, kt=kt: e.scalar_tensor_tensor(
                            out=hT[:, kt, :], in0=xt[b][:, kt, :], scalar=1.0, in1=rstd[:], op0=ALU.mult,
                            op1=ALU.mult), reads=[r_xt[b], r_rstd], writes=[r_hT])
                    for j in range(NJ):
                        q = j % 2
                        for kt in range(8):
                            P.op("pe", lambda e, q=q, kt=kt, j=j: e.matmul(
                                ps_g[q][:], lhsT=wup[:, kt, j * 128:(j + 1) * 128], rhs=hT[:, kt, :],
                                start=(kt == 0), stop=(kt == 7)), reads=[r_wup, r_hT], writes=[r_psg[q]])
                        for kt in range(8):
                            P.op("pe", lambda e, q=q, kt=kt, j=j: e.matmul(
                                ps_v[q][:], lhsT=wup[:, kt, HC + j * 128:HC + (j + 1) * 128], rhs=hT[:, kt, :],
                                start=(kt == 0), stop=(kt == 7)), reads=[r_wup, r_hT], writes=[r_psv[q]])
                        for (ps, rps, t, rt, jj) in ((ps_g[q], r_psg[q], tg[q], r_tg[q], j),
                                                     (ps_v[q], r_psv[q], tv[q], r_tv[q], NJ + j)):
                            P.op("act", lambda e, ps=ps, t=t, jj=jj: e.activation(
                                out=t[:, 1:511], in_=ps[:, 0:510], func=AF.Copy, scale=cw[:, 0, jj:jj + 1]),
                                reads=[rps, r_cw], writes=[rt])
                            for tap in (1, 2):
                                P.op("dve", lambda e, ps=ps, t=t, jj=jj, tap=tap: e.scalar_tensor_tensor(
                                    out=t[:, 1:511], in0=ps[:, tap:tap + 510], scalar=cw[:, tap, jj:jj + 1],
                                    in1=t[:, 1:511], op0=ALU.mult, op1=ALU.add),
                                    reads=[rps, r_cw, rt], writes=[rt])
                        P.op("act", lambda e, q=q: e.activation(out=sg[q][:, 1:511], in_=tg[q][:, 1:511], func=AF.Silu),
                             reads=[r_tg[q]], writes=[r_sg[q]])
                        P.op("pool", lambda e, q=q, j=j: e.tensor_tensor(
                            out=act[:, j, 1:511], in0=sg[q][:, 1:511], in1=tv[q][:, 1:511], op=ALU.mult),
                            reads=[r_sg[q], r_tv[q]], writes=[r_act])
                    for mo in range(8):
                        q = mo % 2
                        if ph == 1:
                            P.dma("sp", lambda e, q=q, mo=mo, s0=s0, nout=nout: e.dma_start(
                                out=xr[q][:, 1:1 + nout], in_=xout[mo * 128:(mo + 1) * 128, s0:s0 + nout]),
                                reads=[rx_out], writes=[r_xr[q]])
                        for j in range(NJ):
                            P.op("pe", lambda e, q=q, j=j, mo=mo: e.matmul(
                                ps_y[q][:, 1:511], lhsT=wdn[:, j, mo * 128:(mo + 1) * 128], rhs=act[:, j, 1:511],
                                start=(j == 0), stop=(j == NJ - 1)), reads=[r_wdn, r_act], writes=[r_psy[q]])
                        if ph == 1:
                            P.op("dve", lambda e, q=q: e.tensor_tensor(
                                out=xo[q][:, 1:511], in0=ps_y[q][:, 1:511], in1=xr[q][:, 1:511], op=ALU.add),
                                reads=[r_psy[q], r_xr[q]], writes=[r_xo[q]])
                        else:
                            P.op("dve", lambda e, q=q, mo=mo, b=b: e.tensor_tensor(
                                out=xo[q][:, 1:511], in0=ps_y[q][:, 1:511], in1=xt[b][:, mo, 1:511], op=ALU.add),
                                reads=[r_psy[q], r_xt[b]], writes=[r_xo[q]])
                        P.dma("sp", lambda e, q=q, mo=mo, s0=s0, nout=nout: e.dma_start(
                            out=xout[mo * 128:(mo + 1) * 128, s0:s0 + nout], in_=xo[q][:, 1:1 + nout]),
                            reads=[r_xo[q]], writes=[rx_out])
            P.barrier()

    def load_w(self, st, dst, rdst, src, ncols, scale=None, rscale=None, stg=None, rstg=None, eng="pool"):
        P = self.P
        nk = src.shape[0] // 128
        CH = stg[0].shape[1]
        k = 0
        for kt in range(nk):
            for c0 in range(0, ncols, CH):
                cn = min(CH, ncols - c0)
                s = k % 2
                k += 1
                P.dma("sp", lambda e, s=s, kt=kt, c0=c0, cn=cn: e.dma_start(
                    out=stg[s][:, 0:cn], in_=src[kt * 128:(kt + 1) * 128, c0:c0 + cn]), writes=[rstg[s]])
                if k % 2 == 0:
                    if scale is not None:
                        P.op("dve", lambda e, s=s, kt=kt, c0=c0, cn=cn: e.tensor_scalar(
                            out=dst[:, kt, c0:c0 + cn], in0=stg[s][:, 0:cn], scalar1=scale[:, kt:kt + 1], scalar2=None,
                            op0=ALU.mult), reads=[rstg[s], rscale], writes=[rdst])
                    else:
                        P.op("dve", lambda e, s=s, kt=kt, c0=c0, cn=cn: e.tensor_copy(
                            out=dst[:, kt, c0:c0 + cn], in_=stg[s][:, 0:cn]), reads=[rstg[s]], writes=[rdst])
                else:
                    if scale is not None:
                        P.op("act", lambda e, s=s, kt=kt, c0=c0, cn=cn: e.activation(
                            out=dst[:, kt, c0:c0 + cn], in_=stg[s][:, 0:cn], func=AF.Copy, scale=scale[:, kt:kt + 1]),
                            reads=[rstg[s], rscale], writes=[rdst])
                    else:
                        P.op("act", lambda e, s=s, kt=kt, c0=c0, cn=cn: e.copy(
                            out=dst[:, kt, c0:c0 + cn], in_=stg[s][:, 0:cn]), reads=[rstg[s]], writes=[rdst])

    def rms_tile(self, xt, r_xt, hT, r_hT, tmp):
        P = self.P
        sqt, r_sqt, ps_ss, r_pss, rstd, r_rstd = tmp
        for kt in range(8):
            s = kt % 2
            P.op("act", lambda e, s=s, kt=kt: e.activation(out=sqt[s][:], in_=xt[:, kt, :], func=AF.Square),
                 reads=[r_xt], writes=[r_sqt[s]])
            P.op("pe", lambda e, s=s, kt=kt: e.matmul(ps_ss[:], lhsT=self.ones_f[:], rhs=sqt[s][:],
                                                      start=(kt == 0), stop=(kt == 7)),
                 reads=[r_sqt[s], self.r_const], writes=[r_pss])
        P.op("act", lambda e: e.activation(out=rstd[:], in_=ps_ss[:], func=AF.Sqrt, bias=self.eps_c[:],
                                           scale=1.0 / D), reads=[r_pss, self.r_const], writes=[r_rstd])
        P.op("dve", lambda e: e.reciprocal(out=rstd[:], in_=rstd[:]), reads=[r_rstd], writes=[r_rstd])
        for kt in range(8):
            P.op("dve" if kt % 2 == 0 else "pool", lambda e, kt=kt: e.tensor_tensor(
                out=hT[:, kt, :], in0=xt[:, kt, :], in1=rstd[:], op=ALU.mult),
                reads=[r_xt, r_rstd], writes=[r_hT])

    def proj(self, l, xin, rx_in):
        P, nc, T, d = self.P, self.nc, self.T, self.d
        R = P.res
        NT = T // 512
        OQA, OKA, OVA, OQD, OBD, OZ, OG = 0, 512, 640, 768, 2304, 2336, 2848
        with contextlib.ExitStack() as st:
            g_sb = P.sbuf("mix_g", [128, 8], F32, st)
            win = P.sbuf("win", [128, 8, NIN], BF16, st)
            wk2 = P.sbuf("wk2", [128, 8, 256], BF16, st)
            stg = [P.sbuf("stg", [128, 1224], F32, st) for _ in range(2)]
            qg = P.sbuf("qg", [128, 2], F32, st)
            xt = P.sbuf("xt", [128, 8, 512], F32, st)
            hT = P.sbuf("hT", [128, 8, 512], BF16, st)
            sqt = [P.sbuf("sqt", [128, 512], F32, st) for _ in range(2)]
            rstd = P.sbuf("rstd", [128, 512], F32, st)
            cs = [P.sbuf("cs", [128, 2, 512], F32, st) for _ in range(2)]
            sqq = [P.sbuf("sqq", [128, 512], F32, st) for _ in range(2)]
            rs = [P.sbuf("rs", [128, 512], F32, st) for _ in range(2)]
            qn = [P.sbuf("qn", [128, 512], F32, st) for _ in range(2)]
            t1 = [P.sbuf("t1", [128, 512], F32, st) for _ in range(2)]
            t2 = [P.sbuf("t2", [128, 512], F32, st) for _ in range(2)]
            ob = [P.sbuf("ob", [128, 512], BF16, st) for _ in range(3)]
            of = [P.sbuf("of", [128, 512], F32, st) for _ in range(3)]
            vb = [P.sbuf("vb", [128, 130], BF16, st) for _ in range(2)]
            ps_ss = P.psum("ps_ss", [128, 512], F32, st)
            ps_m = [P.psum("ps_m", [128, 512], F32, st) for _ in range(3)]
            ps_a = [P.psum("ps_a", [128, 512], F32, st) for _ in range(2)]
            r_g, r_win, r_wk2, r_qg, r_xt, r_hT, r_rstd, r_pss = R(), R(), R(), R(), R(), R(), R(), R()
            r_stg, r_sqt, r_cs, r_sqq, r_rs, r_qn, r_t1, r_t2 = ([R(), R()] for _ in range(8))
            r_ob, r_of, r_psm = ([R(), R(), R()] for _ in range(3))
            r_vb, r_psa = [R(), R()], [R(), R()]
            rd = self.rd

            P.dma("sp", lambda e: e.dma_start(out=g_sb[:], in_=d["norm_mix_g"][l].rearrange("(kt p) -> p kt", p=128),
                                              allow_slow_non_contiguous=True), writes=[r_g])
            import os
            SKIP = os.environ.get("SKIP", "")
            for h2 in range(0 if "qg" in SKIP else 2):
                P.dma("sp", lambda e, h2=h2: e.dma_start(out=qg[h2 * 64:(h2 + 1) * 64, 0:1],
                                                         in_=d["q_norm_g"][l].rearrange("(p o) -> p o", o=1),
                                                         allow_slow_non_contiguous=True), writes=[r_qg])
                P.dma("sp", lambda e, h2=h2: e.dma_start(out=qg[h2 * 64:(h2 + 1) * 64, 1:2],
                                                         in_=d["k_norm_g"][l].rearrange("(p o) -> p o", o=1),
                                                         allow_slow_non_contiguous=True), writes=[r_qg])
            P.op("dve", lambda e: e.tensor_scalar(out=qg[:, 0:1], in0=qg[:, 0:1], scalar1=0.125, scalar2=None,
                                                  op0=ALU.mult), reads=[r_qg], writes=[r_qg])
            self.load_w(st, win, r_win, d["w_in"][l], NIN, scale=g_sb, rscale=r_g, stg=stg, rstg=r_stg)
            for kt in range(0 if "wk2" in SKIP else 8):
                for g in range(2):
                    for dup in range(2):
                        P.op("pool", lambda e, kt=kt, g=g, dup=dup: e.tensor_copy(
                            out=wk2[:, kt, g * 128 + dup * 64:g * 128 + dup * 64 + 64],
                            in_=win[:, kt, OKA + g * 64:OKA + g * 64 + 64]), reads=[r_win], writes=[r_wk2])
            for b in range(0 if "vb" in SKIP else 2):
                for g in range(2):
                    P.op("pool", lambda e, b=b, g=g: e.memset(vb[b][:, g * 65 + 64:g * 65 + 65], 1.0), writes=[r_vb[b]])
            xtiled = xin.rearrange("(kt p) n -> p kt n", p=128)
            cnt = {"m": 0, "a": 0, "o": 0, "f": 0, "q": 0}

            def fm_group(lhs_fn, kind, dst, rdst, row0, c0, ti):
                i = cnt["m"] % 3
                cnt["m"] += 1
                for kt in range(8):
                    P.op("pe", lambda e, kt=kt, i=i: e.matmul(ps_m[i][:], lhsT=lhs_fn(kt), rhs=hT[:, kt, :],
                                                              start=(kt == 0), stop=(kt == 7)),
                         reads=[r_win, r_wk2, r_hT], writes=[r_psm[i]])
                if kind == "f32":
                    j = cnt["f"] % 3
                    cnt["f"] += 1
                    P.op("act", lambda e, i=i, j=j: e.copy(out=of[j][:], in_=ps_m[i][:]), reads=[r_psm[i]],
                         writes=[r_of[j]])
                    P.dma("sp", lambda e, j=j: e.dma_start(out=dst[row0:row0 + 128, c0:c0 + 512], in_=of[j][:]),
                          reads=[r_of[j]], writes=[rdst])
                elif kind == "sig":
                    j = cnt["o"] % 3
                    cnt["o"] += 1
                    P.op("act", lambda e, i=i, j=j: e.activation(out=ob[j][:], in_=ps_m[i][:], func=AF.Sigmoid),
                         reads=[r_psm[i]], writes=[r_ob[j]])
                    P.dma("sp", lambda e, j=j: e.dma_start(out=dst[row0:row0 + 128, c0:c0 + 512], in_=ob[j][:]),
                          reads=[r_ob[j]], writes=[rdst])
                else:
                    q = cnt["q"] % 2
                    cnt["q"] += 1
                    a = cnt["a"] % 2
                    cnt["a"] += 1
                    cb = ti % 2
                    P.op("act", lambda e, i=i, q=q: e.activation(out=sqq[q][:], in_=ps_m[i][:], func=AF.Square),
                         reads=[r_psm[i]], writes=[r_sqq[q]])
                    P.op("pe", lambda e, a=a, q=q: e.matmul(ps_a[a][:], lhsT=self.blk_f[:], rhs=sqq[q][:],
                                                            start=True, stop=True),
                         reads=[r_sqq[q], self.r_const], writes=[r_psa[a]])
                    P.op("act", lambda e, a=a, q=q: e.activation(out=rs[q][:], in_=ps_a[a][:], func=AF.Sqrt,
                                                                 bias=self.eps_c[:], scale=1.0 / 64),
                         reads=[r_psa[a], self.r_const], writes=[r_rs[q]])
                    P.op("dve", lambda e, q=q: e.reciprocal(out=rs[q][:], in_=rs[q][:]), reads=[r_rs[q]], writes=[r_rs[q]])
                    P.op("dve", lambda e, i=i, q=q: e.scalar_tensor_tensor(
                        out=qn[q][:], in0=ps_m[i][:], scalar=qg[:, kind:kind + 1], in1=rs[q][:], op0=ALU.mult,
                        op1=ALU.mult), reads=[r_psm[i], r_qg, r_rs[q]], writes=[r_qn[q]])
                    a2 = cnt["a"] % 2
                    cnt["a"] += 1
                    P.op("pe", lambda e, a2=a2, q=q: e.matmul(ps_a[a2][:], lhsT=self.rot_f[:], rhs=qn[q][:],
                                                              start=True, stop=True),
                         reads=[r_qn[q], self.r_const], writes=[r_psa[a2]])
                    P.op("pool", lambda e, q=q, cb=cb: e.tensor_tensor(out=t1[q][:], in0=qn[q][:], in1=cs[cb][:, 0, :],
                                                                       op=ALU.mult),
                         reads=[r_qn[q], r_cs[cb]], writes=[r_t1[q]])
                    P.op("dve", lambda e, q=q, cb=cb, a2=a2: e.tensor_tensor(out=t2[q][:], in0=ps_a[a2][:],
                                                                             in1=cs[cb][:, 1, :], op=ALU.mult),
                         reads=[r_psa[a2], r_cs[cb]], writes=[r_t2[q]])
                    j = cnt["o"] % 3
                    cnt["o"] += 1
                    P.op("pool", lambda e, q=q, j=j: e.tensor_tensor(out=ob[j][:], in0=t1[q][:], in1=t2[q][:], op=ALU.add),
                         reads=[r_t1[q], r_t2[q]], writes=[r_ob[j]])
                    P.dma("sp", lambda e, j=j: e.dma_start(out=dst[row0:row0 + 128, c0:c0 + 512], in_=ob[j][:]),
                          reads=[r_ob[j]], writes=[rdst])

            for ti in range(NT):
                c0 = ti * 512
                P.dma("sp", lambda e, c0=c0: e.dma_start(out=xt[:], in_=xtiled[:, :, c0:c0 + 512]),
                      reads=[rx_in], writes=[r_xt])
                P.dma("sp", lambda e, c0=c0, ti=ti: e.dma_start(out=cs[ti % 2][:, 0, :], in_=d["ropec"][:, c0:c0 + 512]),
                      writes=[r_cs[ti % 2]])
                P.dma("sp", lambda e, c0=c0, ti=ti: e.dma_start(out=cs[ti % 2][:, 1, :], in_=d["ropes"][:, c0:c0 + 512]),
                      writes=[r_cs[ti % 2]])
                self.rms_tile(xt, r_xt, hT, r_hT, (sqt, r_sqt, ps_ss, r_pss, rstd, r_rstd))
                import os
                PARTS = os.environ.get("PROJ_PARTS", "qdgt")
                for m in range(4 if "q" in PARTS else 0):
                    fm_group(lambda kt, m=m: win[:, kt, OQA + m * 128:OQA + (m + 1) * 128], 0, d["qaT"], rd["qaT"],
                             m * 128, c0, ti)
                for g in range(2 if "q" in PARTS else 0):
                    fm_group(lambda kt, g=g: wk2[:, kt, g * 128:(g + 1) * 128], 1, d["kaT"], rd["kaT"], g * 128, c0, ti)
                for m in range(12 if "d" in PARTS else 0):
                    fm_group(lambda kt, m=m: win[:, kt, OQD + m * 128:OQD + (m + 1) * 128], "f32", d["dpre"], rd["dpre"],
                             m * 128, c0 + 2, ti)
                for m in range(16 if "g" in PARTS else 0):
                    fm_group(lambda kt, m=m: win[:, kt, OG + m * 128:OG + (m + 1) * 128], "sig", d["gT"], rd["gT"],
                             m * 128, c0, ti)
                for sub in range(4 if "t" in PARTS else 0):
                    r0 = c0 + sub * 128
                    i = cnt["m"] % 3
                    cnt["m"] += 1
                    for kt in range(8):
                        P.op("pe", lambda e, kt=kt, i=i, sub=sub: e.matmul(
                            ps_m[i][:, 0:128], lhsT=hT[:, kt, sub * 128:(sub + 1) * 128], rhs=win[:, kt, OVA:OVA + 128],
                            start=(kt == 0), stop=(kt == 7)), reads=[r_win, r_hT], writes=[r_psm[i]])
                    b = sub % 2
                    for g in range(2):
                        P.op("act", lambda e, i=i, b=b, g=g: e.copy(out=vb[b][:, g * 65:g * 65 + 64],
                                                                     in_=ps_m[i][:, g * 64:(g + 1) * 64]),
                             reads=[r_psm[i]], writes=[r_vb[b]])
                    P.dma("sp", lambda e, b=b, r0=r0: e.dma_start(out=d["va"][r0:r0 + 128, :], in_=vb[b][:]),
                          reads=[r_vb[b]], writes=[rd["va"]])
                    for (oc, ncol, dst, rdst) in ((OBD, 32, d["bd"], rd["bd"]), (OZ, 512, d["z"], rd["z"])):
                        i = cnt["m"] % 3
                        cnt["m"] += 1
                        for kt in range(8):
                            P.op("pe", lambda e, kt=kt, i=i, sub=sub, oc=oc, ncol=ncol: e.matmul(
                                ps_m[i][:, 0:ncol], lhsT=hT[:, kt, sub * 128:(sub + 1) * 128],
                                rhs=win[:, kt, oc:oc + ncol], start=(kt == 0), stop=(kt == 7)),
                                reads=[r_win, r_hT], writes=[r_psm[i]])
                        j = cnt["f"] % 3
                        cnt["f"] += 1
                        P.op("act", lambda e, i=i, j=j, ncol=ncol: e.copy(out=of[j][:, 0:ncol], in_=ps_m[i][:, 0:ncol]),
                             reads=[r_psm[i]], writes=[r_of[j]])
                        P.dma("sp", lambda e, j=j, r0=r0, ncol=ncol, dst=dst: e.dma_start(
                            out=dst[r0:r0 + 128, :], in_=of[j][:, 0:ncol]), reads=[r_of[j]], writes=[rdst])
            P.barrier()

    def attention(self, l):
        P, nc, T, d, rd = self.P, self.nc, self.T, self.d, self.rd
        R = P.res
        NKT = T // 128
        NQC = T // 512
        with contextlib.ExitStack() as st:
            kg = P.sbuf("kg", [128, T], BF16, st)
            vg = P.sbuf("vg", [128, NKT, 65], BF16, st)
            qt = [P.sbuf("qt", [128, 512], BF16, st) for _ in range(2)]
            pT = [P.sbuf("pT", [128, 512], BF16, st) for _ in range(4)]
            rsum = P.sbuf("rsum", [128, 512], F32, st)
            ocp = [P.sbuf("ocp", [64, 512], F32, st) for _ in range(2)]
            oo = [P.sbuf("oo", [64, 512], BF16, st) for _ in range(2)]
            ps_s = [P.psum("ps_s", [128, 512], F32, st) for _ in range(4)]
            ps_o = [P.psum("ps_o", [128, 512], F32, st) for _ in range(2)]
            ps_b = P.psum("ps_b", [128, 512], F32, st)
            r_kg, r_vg, r_rsum, r_psb = R(), R(), R(), R()
            r_qt, r_ocp, r_oo, r_pso = ([R(), R()] for _ in range(4))
            r_pT, r_pss = ([R(), R(), R(), R()] for _ in range(2))
            LA = 2
            for g in range(2):
                P.dma("sp", lambda e, g=g: e.dma_start(out=kg[:], in_=d["kaT"][g * 128:(g + 1) * 128, :]),
                      reads=[rd["kaT"]], writes=[r_kg])
                P.dma("sp", lambda e, g=g: e.dma_start(
                    out=vg[:], in_=d["va"][:, g * 65:(g + 1) * 65].rearrange("(kt p) c -> p kt c", p=128)),
                    reads=[rd["va"]], writes=[r_vg])
                tiles = []
                for qc in range(NQC):
                    for pair in range(2):
                        for h2 in range(2):
                            tiles.append((qc, pair, h2))
                items = [(ti_, kt) for ti_ in range(len(tiles)) for kt in range(NKT)]
                NI = len(items)

                def emit_S(idx):
                    ti_, kt = items[idx]
                    qc, pair, h2 = tiles[ti_]
                    qb = (qc * 2 + pair) % 2
                    if h2 == 0 and kt == 0:
                        mrow = (g * 2 + pair) * 128
                        P.dma("sp", lambda e, qb=qb, mrow=mrow, qc=qc: e.dma_start(
                            out=qt[qb][:], in_=d["qaT"][mrow:mrow + 128, qc * 512:(qc + 1) * 512]),
                            reads=[rd["qaT"]], writes=[r_qt[qb]])
                    pl, ph = h2 * 64, h2 * 64 + 64
                    s_ = idx % 4
                    P.op("pe", lambda e, s_=s_, kt=kt, qb=qb, pl=pl, ph=ph: e.matmul(
                        ps_s[s_][:], lhsT=kg[pl:ph, kt * 128:(kt + 1) * 128], rhs=qt[qb][pl:ph, :],
                        start=True, stop=True), reads=[r_kg, r_qt[qb]], writes=[r_pss[s_]])

                def emit_PV(idx):
                    ti_, kt = items[idx]
                    qc, pair, h2 = tiles[ti_]
                    head = g * 4 + pair * 2 + h2
                    ob_ = ti_ % 2
                    s_ = idx % 4
                    P.op("act", lambda e, s_=s_: e.activation(out=pT[s_][:], in_=ps_s[s_][:], func=AF.Exp),
                         reads=[r_pss[s_]], writes=[r_pT[s_]])
                    P.op("pe", lambda e, s_=s_, kt=kt, ob_=ob_: e.matmul(
                        ps_o[ob_][0:65, :], lhsT=vg[:, kt, :], rhs=pT[s_][:], start=(kt == 0),
                        stop=(kt == NKT - 1)), reads=[r_vg, r_pT[s_]], writes=[r_pso[ob_]])
                    if kt == NKT - 1:
                        P.op("dve", lambda e, ob_=ob_: e.reciprocal(out=rsum[64:65, :], in_=ps_o[ob_][64:65, :]),
                             reads=[r_pso[ob_]], writes=[r_rsum])
                        P.op("pe", lambda e: e.matmul(ps_b[0:64, :], lhsT=self.ones_f[64:65, 0:64], rhs=rsum[64:65, :],
                                                      start=True, stop=True),
                             reads=[r_rsum, self.r_const], writes=[r_psb])
                        P.op("act", lambda e, ob_=ob_: e.copy(out=ocp[ob_][:], in_=ps_o[ob_][0:64, :]),
                             reads=[r_pso[ob_]], writes=[r_ocp[ob_]])
                        P.op("dve", lambda e, ob_=ob_: e.tensor_tensor(out=oo[ob_][:], in0=ocp[ob_][:],
                                                                       in1=ps_b[0:64, :], op=ALU.mult),
                             reads=[r_ocp[ob_], r_psb], writes=[r_oo[ob_]])
                        P.dma("sp", lambda e, ob_=ob_, head=head, qc=qc: e.dma_start(
                            out=d["oaT"][head * 64:(head + 1) * 64, qc * 512:(qc + 1) * 512], in_=oo[ob_][:]),
                            reads=[r_oo[ob_]], writes=[rd["oaT"]])

                for idx in range(NI + LA):
                    if idx < NI:
                        emit_S(idx)
                    if idx - LA >= 0:
                        emit_PV(idx - LA)
            P.barrier()

    def merge(self, l, xin, xout, rx_in, rx_out):
        P, nc, T, d, rd = self.P, self.nc, self.T, self.d, self.rd
        R = P.res
        NT = T // 512
        with contextlib.ExitStack() as st:
            woa = P.sbuf("woa", [128, 4, D], BF16, st)
            wod = P.sbuf("wod", [128, 4, D], BF16, st)
            wo = P.sbuf("wo", [128, 8, D], BF16, st)
            stg = [P.sbuf("stg", [128, 1024], F32, st) for _ in range(2)]
            oa = [P.sbuf("oa", [128, 4, 512], BF16, st) for _ in range(2)]
            od = [P.sbuf("od", [128, 4, 512], BF16, st) for _ in range(2)]
            gt = [P.sbuf("gt", [128, 2, 512], BF16, st) for _ in range(2)]
            ta = [P.sbuf("ta", [128, 512], F32, st) for _ in range(2)]
            mx = P.sbuf("mx", [128, 8, 512], BF16, st)
            xr = [P.sbuf("xr", [128, 512], F32, st) for _ in range(2)]
            xo = [P.sbuf("xo", [128, 512], F32, st) for _ in range(2)]
            ps_a = [P.psum("ps_a", [128, 512], F32, st) for _ in range(2)]
            ps_d = [P.psum("ps_d", [128, 512], F32, st) for _ in range(2)]
            ps_y = [P.psum("ps_y", [128, 512], F32, st) for _ in range(2)]
            r_woa, r_wod, r_wo, r_mx = R(), R(), R(), R()
            r_stg, r_oa, r_od, r_gt, r_ta, r_xr, r_xo, r_psa, r_psd, r_psy = ([R(), R()] for _ in range(10))
            self.load_w(st, woa, r_woa, d["w_o_attn"][l], D, stg=stg, rstg=r_stg)
            self.load_w(st, wod, r_wod, d["w_o_dn"][l], D, stg=stg, rstg=r_stg)
            self.load_w(st, wo, r_wo, d["w_out"][l], D, stg=stg, rstg=r_stg)
            oaT = d["oaT"].rearrange("(kt p) n -> p kt n", p=128)
            odT = d["odT"].rearrange("(kt p) n -> p kt n", p=128)
            k = 0
            for ti in range(NT):
                c0 = ti * 512
                b = ti % 2
                P.dma("sp", lambda e, b=b, c0=c0: e.dma_start(out=oa[b][:], in_=oaT[:, :, c0:c0 + 512]),
                      reads=[rd["oaT"]], writes=[r_oa[b]])
                P.dma("sp", lambda e, b=b, c0=c0: e.dma_start(out=od[b][:], in_=odT[:, :, c0:c0 + 512]),
                      reads=[rd["odT"]], writes=[r_od[b]])
                for mo in range(8):
                    q = k % 2
                    k += 1
                    for br in range(2):
                        P.dma("sp", lambda e, q=q, br=br, mo=mo, c0=c0: e.dma_start(
                            out=gt[q][:, br, :], in_=d["gT"][br * D + mo * 128:br * D + (mo + 1) * 128, c0:c0 + 512]),
                            reads=[rd["gT"]], writes=[r_gt[q]])
                    for kt in range(4):
                        P.op("pe", lambda e, q=q, kt=kt, mo=mo, b=b: e.matmul(
                            ps_a[q][:], lhsT=woa[:, kt, mo * 128:(mo + 1) * 128], rhs=oa[b][:, kt, :],
                            start=(kt == 0), stop=(kt == 3)), reads=[r_woa, r_oa[b]], writes=[r_psa[q]])
                    for kt in range(4):
                        P.op("pe", lambda e, q=q, kt=kt, mo=mo, b=b: e.matmul(
                            ps_d[q][:], lhsT=wod[:, kt, mo * 128:(mo + 1) * 128], rhs=od[b][:, kt, :],
                            start=(kt == 0), stop=(kt == 3)), reads=[r_wod, r_od[b]], writes=[r_psd[q]])
                    P.op("dve", lambda e, q=q: e.tensor_tensor(out=ta[q][:], in0=ps_a[q][:], in1=gt[q][:, 0, :],
                                                               op=ALU.mult), reads=[r_psa[q], r_gt[q]], writes=[r_ta[q]])
                    P.op("dve", lambda e, q=q: e.tensor_tensor(out=xo[q][:], in0=ps_d[q][:], in1=gt[q][:, 1, :],
                                                               op=ALU.mult), reads=[r_psd[q], r_gt[q]], writes=[r_xo[q]])
                    P.op("pool", lambda e, q=q, mo=mo: e.tensor_tensor(out=mx[:, mo, :], in0=ta[q][:], in1=xo[q][:],
                                                                       op=ALU.add),
                         reads=[r_ta[q], r_xo[q]], writes=[r_mx])
                for mo in range(8):
                    q = k % 2
                    k += 1
                    P.dma("sp", lambda e, q=q, mo=mo, c0=c0: e.dma_start(
                        out=xr[q][:], in_=xin[mo * 128:(mo + 1) * 128, c0:c0 + 512]), reads=[rx_in], writes=[r_xr[q]])
                    for kt in range(8):
                        P.op("pe", lambda e, q=q, kt=kt, mo=mo: e.matmul(
                            ps_y[q][:], lhsT=wo[:, kt, mo * 128:(mo + 1) * 128], rhs=mx[:, kt, :],
                            start=(kt == 0), stop=(kt == 7)), reads=[r_wo, r_mx], writes=[r_psy[q]])
                    P.op("dve", lambda e, q=q: e.tensor_tensor(out=xo[q][:], in0=ps_y[q][:], in1=xr[q][:], op=ALU.add),
                         reads=[r_psy[q], r_xr[q]], writes=[r_xo[q]])
                    P.dma("sp", lambda e, q=q, mo=mo, c0=c0: e.dma_start(
                        out=xout[mo * 128:(mo + 1) * 128, c0:c0 + 512], in_=xo[q][:]), reads=[r_xo[q]], writes=[rx_out])
            P.barrier()

    def zero_dram(self, ap, rres, bf=False):
        P = self.P
        z = self.zero_b if bf else self.zero_f
        rows, cols = ap.shape
        for r0 in range(0, rows, 128):
            rn = min(128, rows - r0)
            for c0 in range(0, cols, 512):
                cn = min(512, cols - c0)
                P.dma("sp", lambda e, r0=r0, rn=rn, c0=c0, cn=cn: e.dma_start(
                    out=ap[r0:r0 + rn, c0:c0 + cn], in_=z[0:rn, 0:cn]), reads=[self.r_const], writes=[rres])


    def deltanet(self, l):
        P, nc, T, d, rd = self.P, self.nc, self.T, self.d, self.rd
        R = P.res
        NT = T // 512
        NCP = T // 128
        self.zero_dram(d["dpre"][:, 0:2], rd["dpre"])
        self.zero_dram(d["dpre"][:, T + 2:T + 4], rd["dpre"])
        with contextlib.ExitStack() as st:
            cwd = P.sbuf("cwd", [128, 5, 12], F32, st)
            xin = [P.sbuf("xin", [128, 516], F32, st) for _ in range(2)]
            acc = [P.sbuf("acc", [128, 512], F32, st) for _ in range(2)]
            sl = [P.sbuf("sl", [128, 512], F32, st) for _ in range(2)]
            sq = [P.sbuf("sq", [128, 512], F32, st) for _ in range(2)]
            rs = [P.sbuf("rs", [128, 512], F32, st) for _ in range(2)]
            ob = [P.sbuf("ob", [128, 512], BF16, st) for _ in range(2)]
            tmo = [P.sbuf("tmo", [128, 4, 128], BF16, st) for _ in range(2)]
            ps = [P.psum("ps", [128, 512], F32, st) for _ in range(2)]
            pst = [P.psum("pst", [128, 4, 128], BF16, st) for _ in range(2)]
            r_cwd = R()
            r_xin, r_acc, r_sl, r_sq, r_rs, r_ob, r_tmo, r_ps, r_pst = ([R(), R()] for _ in range(9))
            for tap in range(5):
                P.dma("sp", lambda e, tap=tap: e.dma_start(
                    out=cwd[:, tap, :], in_=d["dn_conv_w"][l][tap, :].rearrange("(m p) -> p m", p=128),
                    allow_slow_non_contiguous=True), writes=[r_cwd])
            it = 0
            for m in range(12):
                for ti in range(NT):
                    b = it % 2
                    it += 1
                    c0 = ti * 512
                    P.dma("sp", lambda e, b=b, m=m, c0=c0: e.dma_start(
                        out=xin[b][:], in_=d["dpre"][m * 128:(m + 1) * 128, c0:c0 + 516]),
                        reads=[rd["dpre"]], writes=[r_xin[b]])
                    P.op("act", lambda e, b=b, m=m: e.activation(out=acc[b][:], in_=xin[b][:, 0:512], func=AF.Copy,
                                                                 scale=cwd[:, 0, m:m + 1]),
                         reads=[r_xin[b], r_cwd], writes=[r_acc[b]])
                    for tap in range(1, 5):
                        P.op("dve", lambda e, b=b, m=m, tap=tap: e.scalar_tensor_tensor(
                            out=acc[b][:], in0=xin[b][:, tap:tap + 512], scalar=cwd[:, tap, m:m + 1], in1=acc[b][:],
                            op0=ALU.mult, op1=ALU.add), reads=[r_xin[b], r_cwd, r_acc[b]], writes=[r_acc[b]])
                    P.op("act", lambda e, b=b: e.activation(out=sl[b][:], in_=acc[b][:], func=AF.Silu),
                         reads=[r_acc[b]], writes=[r_sl[b]])
                    if m < 8:
                        P.op("act", lambda e, b=b: e.activation(out=sq[b][:], in_=sl[b][:], func=AF.Square),
                             reads=[r_sl[b]], writes=[r_sq[b]])
                        P.op("pe", lambda e, b=b: e.matmul(ps[b][:], lhsT=self.blk_f, rhs=sq[b][:], start=True, stop=True),
                             reads=[r_sq[b], self.r_const], writes=[r_ps[b]])
                        P.op("act", lambda e, b=b: e.activation(out=rs[b][:], in_=ps[b][:], func=AF.Sqrt,
                                                                bias=self.eps_c[:], scale=1.0),
                             reads=[r_ps[b], self.r_const], writes=[r_rs[b]])
                        P.op("dve", lambda e, b=b: e.reciprocal(out=rs[b][:], in_=rs[b][:]), reads=[r_rs[b]], writes=[r_rs[b]])
                        scl = 0.125 if m < 4 else 1.0
                        P.op("dve", lambda e, b=b, scl=scl: e.scalar_tensor_tensor(
                            out=ob[b][:], in0=sl[b][:], scalar=scl, in1=rs[b][:], op0=ALU.mult, op1=ALU.mult),
                            reads=[r_sl[b], r_rs[b]], writes=[r_ob[b]])
                        dst, rdst = (d["dqT"], rd["dqT"]) if m < 4 else (d["dkT"], rd["dkT"])
                        P.dma("sp", lambda e, b=b, m=m, c0=c0, dst=dst: e.dma_start(
                            out=dst[(m % 4) * 128:(m % 4 + 1) * 128, c0:c0 + 512], in_=ob[b][:]),
                            reads=[r_ob[b]], writes=[rdst])
                    else:
                        P.op("pool", lambda e, b=b: e.tensor_copy(out=ob[b][:], in_=sl[b][:]), reads=[r_sl[b]],
                             writes=[r_ob[b]])
                    if m >= 4:
                        for sub in range(4):
                            P.op("pe", lambda e, b=b, sub=sub: e.transpose(
                                out=pst[b][:, sub, :], in_=ob[b][:, sub * 128:(sub + 1) * 128], identity=self.identb[:]),
                                reads=[r_ob[b], self.r_const], writes=[r_pst[b]])
                        P.op("act", lambda e, b=b: e.copy(out=tmo[b][:], in_=pst[b][:]), reads=[r_pst[b]], writes=[r_tmo[b]])
                        dst, rdst = (d["dk_tm"], rd["dk_tm"]) if m < 8 else (d["dv_tm"], rd["dv_tm"])
                        mc = (m % 4) * 128
                        P.dma("sp", lambda e, b=b, mc=mc, c0=c0, dst=dst: e.dma_start(
                            out=dst[c0:c0 + 512, mc:mc + 128].rearrange("(s p) c -> p s c", p=128), in_=tmo[b][:]),
                            reads=[r_tmo[b]], writes=[rdst])
            P.barrier()
        import os
        DN_STOP = os.environ.get("DN_STOP", "")
        if DN_STOP == "A":
            return
        with contextlib.ExitStack() as st:
            bdl = P.sbuf("bdl", [128, NCP, 32], F32, st)
            tmp16 = P.sbuf("tmp16", [128, NCP, 16], F32, st)
            dtb = P.sbuf("dtb", [128, 16], F32, st)
            nea = P.sbuf("nea", [128, 16], F32, st)
            names = ("beta", "nbeta", "g", "gc", "ngc", "be", "e2")
            ga = {n: P.sbuf(n, [128, 2, NCP, 8], F32, st) for n in names}
            r_gate = R()
            pq = [P.psum("pq", [128, 4, 128], F32, st) for _ in range(5)]
            psg = [pq[i][:].rearrange("p a d -> p (a d)") for i in range(2)]
            r_psg = [R(), R()]
            P.dma("sp", lambda e: e.dma_start(out=bdl[:], in_=d["bd"].rearrange("(cp p) c -> p cp c", p=128)),
                  reads=[rd["bd"]], writes=[r_gate])
            P.dma("sp", lambda e: e.dma_start(
                out=dtb[:], in_=d["dn_dt_bias"][l].rearrange("a h -> (a h)").partition_broadcast(128)), writes=[r_gate])
            P.dma("sp", lambda e: e.dma_start(
                out=nea[:], in_=d["dn_a_log"][l].rearrange("a h -> (a h)").partition_broadcast(128)), writes=[r_gate])
            G = [r_gate]
            P.op("act", lambda e: e.activation(out=nea[:], in_=nea[:], func=AF.Exp), reads=G, writes=G)
            P.op("dve", lambda e: e.tensor_scalar(out=nea[:], in0=nea[:], scalar1=-1.0, scalar2=None, op0=ALU.mult),
                 reads=G, writes=G)
            P.op("dve", lambda e: e.tensor_tensor(out=tmp16[:], in0=bdl[:, :, 16:32],
                                                  in1=dtb[:].unsqueeze(1).broadcast_to([128, NCP, 16]), op=ALU.add),
                 reads=G, writes=G)
            P.op("act", lambda e: e.activation(out=tmp16[:], in_=tmp16[:], func=AF.Exp), reads=G, writes=G)
            P.op("act", lambda e: e.activation(out=tmp16[:], in_=tmp16[:], func=AF.Ln, bias=self.ones_f[:, 0:1], scale=1.0),
                 reads=G + [self.r_const], writes=G)
            for dr in range(2):
                P.op("dve", lambda e, dr=dr: e.tensor_tensor(
                    out=ga["g"][:, dr], in0=tmp16[:, :, dr * 8:(dr + 1) * 8],
                    in1=nea[:, dr * 8:(dr + 1) * 8].unsqueeze(1).broadcast_to([128, NCP, 8]), op=ALU.mult),
                    reads=G, writes=G)
                P.op("act", lambda e, dr=dr: e.activation(out=ga["beta"][:, dr], in_=bdl[:, :, dr * 8:(dr + 1) * 8],
                                                          func=AF.Sigmoid), reads=G, writes=G)
            P.op("dve", lambda e: e.tensor_scalar(out=ga["nbeta"][:], in0=ga["beta"][:], scalar1=-1.0, scalar2=None,
                                                  op0=ALU.mult), reads=G, writes=G)
            NB = NCP * 8
            for dr in range(2):
                gflat = ga["g"][:, dr].rearrange("p c h -> p (c h)")
                gcflat = ga["gc"][:, dr].rearrange("p c h -> p (c h)")
                e2flat = ga["e2"][:, dr].rearrange("p c h -> p (c h)")
                for c0 in range(0, NB, 512):
                    cn = min(512, NB - c0)
                    P.op("pe", lambda e, dr=dr, c0=c0, cn=cn, gflat=gflat: e.matmul(
                        psg[0][:, 0:cn], lhsT=self.tri[dr], rhs=gflat[:, c0:c0 + cn], start=True, stop=True),
                        reads=G + [self.r_const], writes=[r_psg[0]])
                    P.op("act", lambda e, c0=c0, cn=cn, gcflat=gcflat: e.copy(out=gcflat[:, c0:c0 + cn], in_=psg[0][:, 0:cn]),
                         reads=[r_psg[0]], writes=G)
                    P.op("pe", lambda e, c0=c0, cn=cn, gflat=gflat: e.matmul(
                        psg[1][:, 0:cn], lhsT=self.blk_f, rhs=gflat[:, c0:c0 + cn], start=True, stop=True),
                        reads=G + [self.r_const], writes=[r_psg[1]])
                    P.op("dve", lambda e, c0=c0, cn=cn, gcflat=gcflat, e2flat=e2flat: e.tensor_tensor(
                        out=e2flat[:, c0:c0 + cn], in0=psg[1][:, 0:cn], in1=gcflat[:, c0:c0 + cn], op=ALU.subtract),
                        reads=[r_psg[1]] + G, writes=G)
            P.op("act", lambda e: e.activation(out=ga["e2"][:], in_=ga["e2"][:], func=AF.Exp), reads=G, writes=G)
            P.op("dve", lambda e: e.tensor_scalar(out=ga["ngc"][:], in0=ga["gc"][:], scalar1=-1.0, scalar2=None,
                                                  op0=ALU.mult), reads=G, writes=G)
            P.op("act", lambda e: e.activation(out=ga["be"][:], in_=ga["gc"][:], func=AF.Exp), reads=G, writes=G)
            P.op("pool", lambda e: e.tensor_tensor(out=ga["be"][:], in0=ga["be"][:], in1=ga["beta"][:], op=ALU.mult),
                 reads=G, writes=G)

            def sb2(name, shape, dt, n=2):
                return [P.sbuf(name, shape, dt, st) for _ in range(n)], [R() for _ in range(n)]
            kq, r_kq = sb2("kq", [128, 2, 128], BF16, 3)
            ktm, r_ktm = sb2("ktm", [128, 2, 64], BF16, 3)
            vtm, r_vtm = sb2("vtm", [128, 2, 64], BF16, 3)
            G2, r_G2 = sb2("G2", [128, 2, 64], F32)
            Eg, r_Eg = sb2("Eg", [128, 128], F32)
            qdec, r_qdec = sb2("qdec", [128, 128], BF16)
            gtc, r_gtc = sb2("gtc", [128, 2], F32)
            vb_, r_vb = sb2("vb", [128, 2, 64], BF16)
            kbe, r_kbe = sb2("kbe", [128, 2, 64], BF16)
            kdec, r_kdec = sb2("kdec", [128, 2, 64], BF16)
            G1, r_G1 = sb2("G1", [128, 128], F32)
            G1n, r_G1n = sb2("G1n", [128, 128], F32)
            bcol, r_bcol = sb2("bcol", [128, 16], F32)
            Dm, r_Dm = sb2("Dm", [128, 128], F32)
            DmT, r_DmT = sb2("DmT", [128, 128], F32)
            Nn, r_Nn = sb2("Nn", [128, 128], BF16, 4)
            Mm, r_Mm = sb2("Mm", [128, 128], BF16, 4)
            TT, r_TT = sb2("TT", [128, 128], BF16)
            qkT, r_qkT = sb2("qkT", [128, 128], BF16, 4)
            u_, r_u = sb2("u", [128, 128], F32)
            wT2, r_wT2 = sb2("wT2", [128, 128], BF16)
            vn, r_vn = sb2("vn", [128, 128], BF16)
            osb, r_osb = sb2("osb", [128, 128], F32, 3)
            tS, r_tS = sb2("tS", [128, 128], F32)
            Sf = P.sbuf("Sf", [128, 128], F32, st)
            Sb = P.sbuf("Sb", [128, 128], BF16, st)
            r_S, r_Sb = R(), R()
            P.barrier()
            r_pq = [[R() for _ in range(4)] for _ in range(5)]
            ptb = P.psum("ptb", [128, 4, 128], BF16, st)
            r_ptb = [R() for _ in range(4)]
            psc = P.psum("psc", [128, 4, 128], F32, st)
            r_psc = [R() for _ in range(4)]
            cq = {}

            def scan_ps(kind):
                i = {"V": 0, "O": 1, "S": 2}[kind]
                return psc[:, i, :], r_psc[0]

            def PQ(kind, nb, bank):
                i = cq.get(kind, 0)
                cq[kind] = i + 1
                qn_ = nb[i % len(nb)]
                return pq[bank][:, qn_, :], r_pq[bank][0]

            cnt = {}

            def rot(name, n):
                i = cnt.get(name, 0)
                cnt[name] = i + 1
                return i % n

            CUT = os.environ.get("DN_CUT", "")
            ACTV = os.environ.get("ACTV", "")

            def ACTKW(b):
                if ACTV == "v1":
                    return dict()
                if ACTV == "v2":
                    return dict(bias=b)
                return dict(bias=b)

            class _Stop(Exception):
                pass

            def cut(tag):
                if CUT == tag:
                    raise _Stop()

            for hp in range(0 if DN_STOP == "B" else 4):
              try:
                  for dr in range(2):
                      P.op("pool", lambda e: e.memset(Sf[:], 0.0), writes=[r_S])
                      P.op("pool", lambda e: e.memset(Sb[:], 0.0), writes=[r_Sb])
                      odst, rodst = (d["o_f"], rd["o_f"]) if dr == 0 else (d["o_b"], rd["o_b"])
                      for step in range(NCP):
                          cp = step if dr == 0 else NCP - 1 - step
                          t0 = cp * 128
                          i3 = rot("ld", 3)
                          P.dma("sp", lambda e, i3=i3, hp=hp, t0=t0: e.dma_start(
                              out=kq[i3][:, 0, :], in_=d["dkT"][hp * 128:(hp + 1) * 128, t0:t0 + 128]),
                              reads=[rd["dkT"]], writes=[r_kq[i3]])
                          P.dma("sp", lambda e, i3=i3, hp=hp, t0=t0: e.dma_start(
                              out=kq[i3][:, 1, :], in_=d["dqT"][hp * 128:(hp + 1) * 128, t0:t0 + 128]),
                              reads=[rd["dqT"]], writes=[r_kq[i3]])
                          P.dma("sp", lambda e, i3=i3, hp=hp, t0=t0: e.dma_start(
                              out=ktm[i3][:].rearrange("p a d -> p (a d)"), in_=d["dk_tm"][t0:t0 + 128, hp * 128:(hp + 1) * 128]),
                              reads=[rd["dk_tm"]], writes=[r_ktm[i3]])
                          P.dma("sp", lambda e, i3=i3, hp=hp, t0=t0: e.dma_start(
                              out=vtm[i3][:].rearrange("p a d -> p (a d)"), in_=d["dv_tm"][t0:t0 + 128, hp * 128:(hp + 1) * 128]),
                              reads=[rd["dv_tm"]], writes=[r_vtm[i3]])
                          kT2 = kq[i3][:, 0, :]
                          qT2 = kq[i3][:, 1, :]
                          i2 = rot("u", 2)

                          def col(name, h0, n=2, dr=dr, cp=cp):
                              return ga[name][:, dr, cp, h0:h0 + n]
                          P.op("dve", lambda e, i2=i2, c=col("g", 2 * hp): e.tensor_copy(
                              out=G2[i2][:], in_=c.unsqueeze(2).broadcast_to([128, 2, 64])), reads=G, writes=[r_G2[i2]])
                          G2f = G2[i2][:].rearrange("p a d -> p (a d)")
                          pE, rpE = PQ("E", [0, 1], 0)
                          P.op("pe", lambda e, pE=pE, G2f=G2f, dr=dr: e.matmul(pE, lhsT=G2f, rhs=self.tri[dr], start=True, stop=True),
                               reads=[r_G2[i2], self.r_const], writes=[rpE])
                          P.op("act", lambda e, pE=pE, i2=i2: e.activation(out=Eg[i2][:], in_=pE, func=AF.Exp),
                               reads=[rpE], writes=[r_Eg[i2]])
                          P.op("dve", lambda e, i2=i2, qT2=qT2: e.tensor_tensor(out=qdec[i2][:], in0=qT2, in1=Eg[i2][:], op=ALU.mult),
                               reads=[r_kq[i3], r_Eg[i2]], writes=[r_qdec[i2]])
                          pG, rpG = PQ("G", [2, 3], 0)
                          P.op("pe", lambda e, pG=pG, G2f=G2f: e.matmul(pG[:, 0:2], lhsT=G2f, rhs=self.selc, start=True, stop=True),
                               reads=[r_G2[i2], self.r_const], writes=[rpG])
                          P.op("act", lambda e, pG=pG, i2=i2: e.activation(out=gtc[i2][:], in_=pG[:, 0:2], func=AF.Exp),
                               reads=[rpG], writes=[r_gtc[i2]])
                          for (dst_, rdst_, src_, rsrc_, cname) in ((vb_, r_vb, vtm, r_vtm, "beta"), (kbe, r_kbe, ktm, r_ktm, "be"),
                                                                    (kdec, r_kdec, ktm, r_ktm, "e2")):
                              P.op("dve", lambda e, i2=i2, i3=i3, dst_=dst_, src_=src_, c=col(cname, 2 * hp): e.tensor_tensor(
                                  out=dst_[i2][:], in0=src_[i3][:], in1=c.unsqueeze(2).broadcast_to([128, 2, 64]), op=ALU.mult),
                                  reads=[rsrc_[i3]] + G, writes=[rdst_[i2]])
                          cut("C1")
                          pU, rpU = PQ("U", [0, 1], 3)
                          qk_idx = []
                          for a in range(2):
                              h = 2 * hp + a
                              pa = slice(64 * a, 64 * a + 64)
                              ia = rot("a", 2)
                              P.op("dve", lambda e, ia=ia, c=col("g", h, 1): e.tensor_copy(
                                  out=G1[ia][:], in_=c.broadcast_to([128, 128])), reads=G, writes=[r_G1[ia]])
                              P.op("dve", lambda e, ia=ia, c=col("gc", h, 1): e.tensor_copy(out=bcol[ia][:, 0:1], in_=c),
                                   reads=G, writes=[r_bcol[ia]])
                              P.op("dve", lambda e, ia=ia, c=col("ngc", h, 1): e.tensor_copy(out=bcol[ia][:, 8:9], in_=c),
                                   reads=G, writes=[r_bcol[ia]])
                              cut("C1a")
                              pA, rpA = PQ("A", [0, 1], 1)
                              P.op("dve", lambda e, ia=ia, c=col("g", h, 1): e.tensor_scalar(
                                  out=G1n[ia][:], in0=c.broadcast_to([128, 128]), scalar1=-1.0, scalar2=None, op0=ALU.mult),
                                  reads=G, writes=[r_G1n[ia]])
                              P.op("pe", lambda e, pA=pA, ia=ia, dr=dr: e.matmul(pA, lhsT=G1n[ia][:], rhs=self.tri[dr], start=True, stop=False),
                                   reads=[r_G1n[ia], self.r_const], writes=[rpA])
                              P.op("pe", lambda e, pA=pA, dr=dr: e.matmul(pA, lhsT=self.identf, rhs=self.maskS[dr], start=False, stop=True),
                                   reads=[self.r_const], writes=[rpA])
                              cut("C1b")
                              P.op("act", lambda e, pA=pA, ia=ia, c=col("gc", h, 1): e.activation(
                                  out=Dm[ia][:], in_=pA, func=AF.Exp, **ACTKW(bcol[ia][:, 0:1])), reads=[rpA, r_bcol[ia]], writes=[r_Dm[ia]])
                              cut("C1c")
                              pB, rpB = PQ("B", [2, 3], 1)
                              P.op("pe", lambda e, pB=pB, ia=ia, dr=dr: e.matmul(pB, lhsT=(G1n if os.environ.get("BX") == "1" else G1)[ia][:], rhs=self.tri[dr], start=True, stop=False),
                                   reads=[r_G1[ia], r_G1n[ia], self.r_const], writes=[rpB])
                              P.op("pe", lambda e, pB=pB, dr=dr: e.matmul(pB, lhsT=self.identf, rhs=self.maskI[dr], start=False, stop=True),
                                   reads=[self.r_const], writes=[rpB])
                              cut("C1c2")
                              P.op("act", lambda e, pB=pB, ia=ia, c=col("ngc", h, 1): e.activation(
                                  out=DmT[ia][:], in_=pB, func=AF.Exp, bias=bcol[ia][:, 8:9]), reads=[rpB, r_bcol[ia]], writes=[r_DmT[ia]])
                              cut("C1d")
                              pK, rpK = PQ("K", [0, 1], 2)
                              P.op("pe", lambda e, pK=pK, kT2=kT2, pa=pa: e.matmul(pK, lhsT=kT2[pa, :], rhs=kT2[pa, :], start=True, stop=True),
                                   reads=[r_kq[i3]], writes=[rpK])
                              n0 = rot("N", 4)
                              P.op("dve", lambda e, pK=pK, n0=n0, ia=ia, c=col("nbeta", h, 1): e.scalar_tensor_tensor(
                                  out=Nn[n0][:], in0=pK, scalar=c, in1=Dm[ia][:], op0=ALU.mult, op1=ALU.mult),
                                  reads=[rpK, r_Dm[ia]] + G, writes=[r_Nn[n0]])
                              cut("C1e")
                              pQ_, rpQ = PQ("Q", [2, 3], 2)
                              P.op("pe", lambda e, pQ_=pQ_, kT2=kT2, qT2=qT2, pa=pa: e.matmul(pQ_, lhsT=kT2[pa, :], rhs=qT2[pa, :],
                                                                                            start=True, stop=True),
                                   reads=[r_kq[i3]], writes=[rpQ])
                              iq = rot("qk", 4)
                              qk_idx.append(iq)
                              P.op("dve", lambda e, pQ_=pQ_, iq=iq, ia=ia: e.tensor_tensor(out=qkT[iq][:], in0=pQ_, in1=DmT[ia][:], op=ALU.mult),
                                   reads=[rpQ, r_DmT[ia]], writes=[r_qkT[iq]])
                              cut("C2")
                              tq = rot("tb", 4)
                              P.op("pe", lambda e, tq=tq, n0=n0: e.transpose(out=ptb[:, tq, :], in_=Nn[n0][:], identity=self.identb[:]),
                                   reads=[r_Nn[n0], self.r_const], writes=[r_ptb[0]])
                              m0 = rot("M", 4)
                              P.op("act", lambda e, tq=tq, m0=m0: e.copy(out=Mm[m0][:], in_=ptb[:, tq, :]),
                                   reads=[r_ptb[0]], writes=[r_Mm[m0]])
                              P.op("pool", lambda e, ia=ia, m0=m0: e.tensor_tensor(out=TT[ia][:], in0=Mm[m0][:], in1=self.identb[:], op=ALU.add),
                                   reads=[r_Mm[m0], self.r_const], writes=[r_TT[ia]])
                              nk, mk = n0, m0
                              for k in range(1, 6):
                                  pN, rpN = PQ("N", [0, 1], 4)
                                  P.op("pe", lambda e, pN=pN, nk=nk, mk=mk: e.matmul(pN, lhsT=Mm[mk][:], rhs=Nn[nk][:], start=True, stop=True),
                                       reads=[r_Mm[mk], r_Nn[nk]], writes=[rpN])
                                  n1 = rot("N", 4)
                                  P.op("act", lambda e, pN=pN, n1=n1: e.copy(out=Nn[n1][:], in_=pN), reads=[rpN], writes=[r_Nn[n1]])
                                  m1 = mk
                                  if k < 5:
                                      pM, rpM = PQ("Mq", [2], 4)
                                      P.op("pe", lambda e, pM=pM, nk=nk, mk=mk: e.matmul(pM, lhsT=Nn[nk][:], rhs=Mm[mk][:], start=True, stop=True),
                                           reads=[r_Mm[mk], r_Nn[nk]], writes=[rpM])
                                      m1 = rot("M", 4)
                                      P.op("dve", lambda e, pM=pM, m1=m1: e.tensor_copy(out=Mm[m1][:], in_=pM), reads=[rpM], writes=[r_Mm[m1]])
                                  pT_, rpT = PQ("T", [3], 4)
                                  P.op("pe", lambda e, pT_=pT_, n1=n1, ia=ia: e.matmul(pT_, lhsT=Nn[n1][:], rhs=TT[ia][:], start=True, stop=True),
                                       reads=[r_Nn[n1], r_TT[ia]], writes=[rpT])
                                  P.op("dve", lambda e, pT_=pT_, ia=ia: e.tensor_tensor(out=TT[ia][:], in0=pT_, in1=TT[ia][:], op=ALU.add),
                                       reads=[rpT, r_TT[ia]], writes=[r_TT[ia]])
                                  nk, mk = n1, m1
                              cut("C3")
                              P.op("pe", lambda e, pU=pU, ia=ia, i2=i2, a=a: e.matmul(pU[:, 64 * a:64 * a + 64], lhsT=TT[ia][:], rhs=vb_[i2][:, a, :],
                                                                                     start=True, stop=True),
                                   reads=[r_TT[ia], r_vb[i2]], writes=[rpU])
                              pW, rpW = PQ("W", [2, 3], 3)
                              P.op("pe", lambda e, pW=pW, ia=ia, i2=i2: e.matmul(pW, lhsT=kbe[i2][:].rearrange("p a d -> p (a d)"), rhs=TT[ia][:],
                                                                                start=True, stop=True),
                                   reads=[r_TT[ia], r_kbe[i2]], writes=[rpW])
                              P.op("act", lambda e, pW=pW, i2=i2, pa=pa: e.copy(out=wT2[i2][pa, :], in_=pW[pa, :]),
                                   reads=[rpW], writes=[r_wT2[i2]])
                          P.op("act", lambda e, pU=pU, i2=i2: e.copy(out=u_[i2][:], in_=pU), reads=[rpU], writes=[r_u[i2]])
                          cut("C4")
                          for c2 in ((0, 1) if dr == 0 else (1, 0)):
                              ch = slice(64 * c2, 64 * c2 + 64)
                              pV, rpV = PQ("sV", [0, 1], 2) if False else scan_ps("V")
                              P.op("pe", lambda e, pV=pV, i2=i2: e.matmul(pV, lhsT=wT2[i2][:], rhs=Sb[:], start=True, stop=True),
                                   reads=[r_wT2[i2], r_Sb], writes=[rpV])
                              iv = rot("vn", 2)
                              P.op("dve", lambda e, pV=pV, i2=i2, iv=iv, ch=ch: e.tensor_tensor(out=vn[iv][ch, :], in0=u_[i2][ch, :], in1=pV[ch, :],
                                                                                             op=ALU.subtract),
                                   reads=[rpV, r_u[i2]], writes=[r_vn[iv]])
                              pO, rpO = scan_ps("O")
                              P.op("pe", lambda e, pO=pO, i2=i2: e.matmul(pO, lhsT=qdec[i2][:], rhs=Sb[:], start=True, stop=False),
                                   reads=[r_qdec[i2], r_Sb], writes=[rpO])
                              for a in range(2):
                                  P.op("pe", lambda e, pO=pO, a=a, iv=iv, ch=ch, iq=qk_idx[a]: e.matmul(
                                      pO[:, 64 * a:64 * a + 64], lhsT=qkT[iq][ch, :], rhs=vn[iv][ch, 64 * a:64 * a + 64],
                                      start=False, stop=(a == 1)), reads=[r_qkT[qk_idx[a]], r_vn[iv]], writes=[rpO])
                              io = rot("o", 3)
                              P.op("act", lambda e, pO=pO, io=io, ch=ch: e.copy(out=osb[io][ch, :], in_=pO[ch, :]),
                                   reads=[rpO], writes=[r_osb[io]])
                              P.dma("sp", lambda e, io=io, ch=ch, t0=t0, c2=c2, hp=hp, odst=odst: e.dma_start(
                                  out=odst[t0 + 64 * c2:t0 + 64 * c2 + 64, hp * 128:(hp + 1) * 128], in_=osb[io][ch, :]),
                                  reads=[r_osb[io]], writes=[rodst])
                              pS, rpS = scan_ps("S")
                              P.op("pe", lambda e, pS=pS, i2=i2, iv=iv, ch=ch: e.matmul(
                                  pS, lhsT=kdec[i2][ch].rearrange("p a d -> p (a d)"), rhs=vn[iv][ch, :], start=True, stop=True),
                                  reads=[r_kdec[i2], r_vn[iv]], writes=[rpS])
                              its = rot("tS", 2)
                              P.op("dve", lambda e, pS=pS, its=its: e.tensor_tensor(out=tS[its][:], in0=pS, in1=self.blk_f, op=ALU.mult),
                                   reads=[rpS, self.r_const], writes=[r_tS[its]])
                              P.op("dve", lambda e, its=its, i2=i2, c2=c2: e.scalar_tensor_tensor(
                                  out=Sf[:], in0=Sf[:], scalar=gtc[i2][:, c2:c2 + 1], in1=tS[its][:], op0=ALU.mult, op1=ALU.add),
                                  reads=[r_S, r_gtc[i2], r_tS[its]], writes=[r_S])
                              P.op("pool", lambda e: e.tensor_copy(out=Sb[:], in_=Sf[:]), reads=[r_S], writes=[r_Sb])
              except _Stop:
                break
            P.barrier()
        if DN_STOP in ("B", "C"):
            return
        with contextlib.ExitStack() as st:
            gng = P.sbuf("gng", [128, 64], F32, st)
            of_ = [P.sbuf("of", [128, 512], F32, st) for _ in range(2)]
            ob_ = [P.sbuf("obk", [128, 512], F32, st) for _ in range(2)]
            zt = [P.sbuf("zt", [128, 512], F32, st) for _ in range(2)]
            sq = [P.sbuf("sqd", [128, 512], F32, st) for _ in range(2)]
            ss = [P.sbuf("ss", [128, 8], F32, st) for _ in range(2)]
            obf = [P.sbuf("obf", [128, 512], BF16, st) for _ in range(2)]
            oT = [P.sbuf("oT", [128, 4, 128], BF16, st) for _ in range(2)]
            pst = [P.psum("pstd", [128, 4, 128], BF16, st) for _ in range(2)]
            r_gng = R()
            r_of, r_ob, r_zt, r_sq, r_ss, r_obf, r_oT, r_pst = ([R(), R()] for _ in range(8))
            P.dma("sp", lambda e: e.dma_start(out=gng[:], in_=d["dn_out_norm_g"][l].partition_broadcast(128)), writes=[r_gng])
            odT = d["odT"].rearrange("(kt p) n -> p kt n", p=128)
            for tt in range(NCP):
                b = tt % 2
                t0 = tt * 128
                P.dma("sp", lambda e, b=b, t0=t0: e.dma_start(out=of_[b][:], in_=d["o_f"][t0:t0 + 128, :]), reads=[rd["o_f"]], writes=[r_of[b]])
                P.dma("sp", lambda e, b=b, t0=t0: e.dma_start(out=ob_[b][:], in_=d["o_b"][t0:t0 + 128, :]), reads=[rd["o_b"]], writes=[r_ob[b]])
                P.dma("sp", lambda e, b=b, t0=t0: e.dma_start(out=zt[b][:], in_=d["z"][t0:t0 + 128, :]), reads=[rd["z"]], writes=[r_zt[b]])
                P.op("pool", lambda e, b=b: e.tensor_tensor(out=of_[b][:], in0=of_[b][:], in1=ob_[b][:], op=ALU.add),
                     reads=[r_of[b], r_ob[b]], writes=[r_of[b]])
                P.op("act", lambda e, b=b: e.activation(out=sq[b][:], in_=of_[b][:], func=AF.Square), reads=[r_of[b]], writes=[r_sq[b]])
                P.op("dve", lambda e, b=b: e.tensor_reduce(out=ss[b][:], in_=sq[b][:].rearrange("p (h d) -> p h d", h=8),
                                                           axis=mybir.AxisListType.X, op=ALU.add), reads=[r_sq[b]], writes=[r_ss[b]])
                P.op("act", lambda e, b=b: e.activation(out=ss[b][:], in_=ss[b][:], func=AF.Sqrt, bias=self.eps_c[:], scale=1.0 / 64),
                     reads=[r_ss[b], self.r_const], writes=[r_ss[b]])
                P.op("dve", lambda e, b=b: e.reciprocal(out=ss[b][:], in_=ss[b][:]), reads=[r_ss[b]], writes=[r_ss[b]])
                P.op("dve", lambda e, b=b: e.tensor_tensor(
                    out=of_[b][:].rearrange("p (h d) -> p h d", h=8), in0=of_[b][:].rearrange("p (h d) -> p h d", h=8),
                    in1=ss[b][:].unsqueeze(2).broadcast_to([128, 8, 64]), op=ALU.mult), reads=[r_of[b], r_ss[b]], writes=[r_of[b]])
                P.op("dve", lambda e, b=b: e.tensor_tensor(
                    out=of_[b][:].rearrange("p (h d) -> p h d", h=8), in0=of_[b][:].rearrange("p (h d) -> p h d", h=8),
                    in1=gng[:].unsqueeze(1).broadcast_to([128, 8, 64]), op=ALU.mult), reads=[r_of[b], r_gng], writes=[r_of[b]])
                P.op("act", lambda e, b=b: e.activation(out=zt[b][:], in_=zt[b][:], func=AF.Silu), reads=[r_zt[b]], writes=[r_zt[b]])
                P.op("dve", lambda e, b=b: e.tensor_tensor(out=obf[b][:], in0=of_[b][:], in1=zt[b][:], op=ALU.mult),
                     reads=[r_of[b], r_zt[b]], writes=[r_obf[b]])
                for kt in range(4):
                    P.op("pe", lambda e, b=b, kt=kt: e.transpose(out=pst[b][:, kt, :], in_=obf[b][:, kt * 128:(kt + 1) * 128],
                                                                 identity=self.identb[:]),
                         reads=[r_obf[b], self.r_const], writes=[r_pst[b]])
                P.op("act", lambda e, b=b: e.copy(out=oT[b][:], in_=pst[b][:]), reads=[r_pst[b]], writes=[r_oT[b]])
                P.dma("sp", lambda e, b=b, t0=t0: e.dma_start(out=odT[:, :, t0:t0 + 128], in_=oT[b][:]),
                      reads=[r_oT[b]], writes=[rd["odT"]])
            P.barrier()


WEIGHT_SPECS = [
    ("norm_mix_g", (DEPTH, D)), ("w_in", (DEPTH, D, NIN)), ("q_norm_g", (DEPTH, 64)), ("k_norm_g", (DEPTH, 64)),
    ("dn_conv_w", (DEPTH, 5, 1536)), ("dn_a_log", (DEPTH, 2, 8)), ("dn_dt_bias", (DEPTH, 2, 8)),
    ("dn_out_norm_g", (DEPTH, 64)), ("w_o_attn", (DEPTH, 512, D)), ("w_o_dn", (DEPTH, 512, D)),
    ("w_out", (DEPTH, D, D)), ("norm_ffn_g", (DEPTH, D)), ("w_up", (DEPTH, D, 2 * DFF)),
    ("ffn_conv_w", (DEPTH, 3, 2 * DFF)), ("w_down", (DEPTH, DFF, D)),
]


def host_consts(T):
    blk = np.zeros((128, 128), np.float32)
    blk[:64, :64] = 1
    blk[64:, 64:] = 1
    rot = np.zeros((128, 128), np.float32)
    for h2 in range(2):
        for ax in range(2):
            for f in range(16):
                m0 = h2 * 64 + ax * 32 + f
                m1 = m0 + 16
                rot[m1, m0] = -1.0
                rot[m0, m1] = 1.0
    ident = np.eye(128, dtype=np.float32)
    p = np.arange(128)[:, None]
    f = np.arange(128)[None, :]
    same = (p // 64) == (f // 64)
    triF = (same & (p <= f)).astype(np.float32)
    triB = (same & (p >= f)).astype(np.float32)
    BIG = 30000.0
    mSf = np.where(same & (p > f), 0.0, -BIG).astype(np.float32)
    mSb = np.where(same & (p < f), 0.0, -BIG).astype(np.float32)
    mIf = np.where(same & (f >= p), 0.0, -BIG).astype(np.float32)
    mIb = np.where(same & (f <= p), 0.0, -BIG).astype(np.float32)
    z = np.zeros((128, 128), np.float32)
    selc = np.zeros((128, 2), np.float32)
    selc[:64, 0] = 1
    selc[64:, 1] = 1
    cf = np.concatenate([blk, rot, ident, triF, triB, mSf, mSb, mIf, mIb, -triF, -triB, z, z, selc], axis=1)
    t = np.arange(T)
    row = (t // 64).astype(np.float32)
    col = (t % 64).astype(np.float32)
    inv = (np.float32(10000.0) ** (-np.arange(16, dtype=np.float32) / np.float32(16))).astype(np.float32)
    ang = np.stack([row[:, None] * inv, col[:, None] * inv], axis=1)
    c = np.cos(ang).astype(np.float32)
    sn = np.sin(ang).astype(np.float32)
    C = np.zeros((128, T), np.float32)
    S = np.zeros((128, T), np.float32)
    for h2 in range(2):
        for ax in range(2):
            for half in range(2):
                r0 = h2 * 64 + ax * 32 + half * 16
                C[r0:r0 + 16] = c[:, ax, :].T
                S[r0:r0 + 16] = sn[:, ax, :].T
    return dict(cf32=cf, ropec=C, ropes=S)


def build(T=SEQ, depth=DEPTH, with_dn=True, with_ffn=True, debug=False, stages="padm"):
    nc = bass.Bass("TRN2", target_bir_lowering=False)
    B = Builder(nc, T=T)
    B.debug = debug
    P = B.P
    x0 = B.dram_in("xT", [D, T], F32)
    for name, shp in WEIGHT_SPECS:
        B.dram_in(name, list(shp), F32)
    B.dram_in("cf32", [128, NCF], F32)
    B.dram_in("ropec", [128, T], F32)
    B.dram_in("ropes", [128, T], F32)
    out = B.dram_out("yT", [D, T], F32)
    xa = B.dram_tmp("xa", [D, T], F32)
    xb = B.dram_tmp("xb", [D, T], F32)
    B.dram_tmp("qaT", [512, T], BF16)
    B.dram_tmp("kaT", [256, T], BF16)
    B.dram_tmp("va", [T, 130], BF16)
    B.dram_tmp("dpre", [1536, T + 4], F32)
    B.dram_tmp("bd", [T, 32], F32)
    B.dram_tmp("z", [T, 512], F32)
    B.dram_tmp("gT", [2048, T], BF16)
    B.dram_tmp("oaT", [512, T], BF16)
    B.dram_tmp("odT", [512, T], BF16)
    B.dram_tmp("dqT", [512, T], BF16)
    B.dram_tmp("dkT", [512, T], BF16)
    B.dram_tmp("dk_tm", [T, 512], BF16)
    B.dram_tmp("dv_tm", [T, 512], BF16)
    B.dram_tmp("o_f", [T, 512], F32)
    B.dram_tmp("o_b", [T, 512], F32)
    B.rd = {k: P.res(k) for k in ("qaT", "kaT", "va", "dpre", "bd", "z", "gT", "oaT", "odT", "dqT", "dkT", "dk_tm", "dv_tm",
                                    "o_f", "o_b")}
    rx = {"x0": P.res(), "xa": P.res(), "xb": P.res(), "out": P.res()}
    with P.stack:
        B.load_consts()
        tch = P.sbuf("touch", [1, 16], F32)
        rt = P.res()
        for name, shp in WEIGHT_SPECS:
            ap = B.d[name]
            idx = tuple([0] * (len(shp) - 1))
            P.dma("sp", lambda e, ap=ap, idx=idx: e.dma_start(out=tch[0:1, 0:8], in_=ap[idx][0:8].rearrange("(o n) -> o n", o=1)),
                  writes=[rt])
        import os
        if not with_dn and "zero" not in os.environ.get("SKIP", ""):
            B.zero_dram(B.d["odT"], B.rd["odT"], bf=True)
        P.barrier()
        for l in range(depth):
            src, rsrc = (x0, rx["x0"]) if l == 0 else (xa, rx["xa"])
            last = (l == depth - 1)
            if "p" in stages:
                B.proj(l, src, rsrc)
            if "a" in stages:
                B.attention(l)
            if with_dn and "d" in stages:
                B.deltanet(l)
            if "m" not in stages:
                continue
            if with_ffn:
                B.merge(l, src, xb, rsrc, rx["xb"])
                dst, rdst = (out, rx["out"]) if last else (xa, rx["xa"])
                B.ffn(l, xb, dst, rx["xb"], rdst)
            else:
                B.merge(l, src, out, rsrc, rx["out"])
        P.emit()
        B.nc_stats = P.stats
    return nc, B


_CACHE = {}


def kernel(**inputs):
    T = SEQ
    if "nc" not in _CACHE:
        _CACHE["nc"] = build(T=T, depth=DEPTH)[0]
        _CACHE["consts"] = host_consts(T)
    nc = _CACHE["nc"]
    x = np.asarray(inputs["x"], dtype=np.float32)
    nb = x.shape[0]
    base = {k: np.ascontiguousarray(np.asarray(inputs[k], dtype=np.float32)) for k, _ in WEIGHT_SPECS}
    base.update(_CACHE["consts"])
    in_maps = []
    for c in range(8):
        m = dict(base)
        m["xT"] = np.ascontiguousarray(x[c % nb].T)
        in_maps.append(m)
    res = run_bass_kernel_spmd(nc, in_maps, core_ids=list(range(8)))
    out = np.stack([np.ascontiguousarray(res.results[b]["yT"].T) for b in range(nb)], axis=0)
    return out.astype(np.float32)
```

```python
import contextlib
import numpy as np
import ml_dtypes
import concourse.bass as bass
import concourse.mybir as mybir
from concourse.bass_utils import run_bass_kernel_spmd

F32 = mybir.dt.float32
BF16 = mybir.dt.bfloat16
ALU = mybir.AluOpType
AF = mybir.ActivationFunctionType

D = 1024
SEQ = 8192
DEPTH = 4
NIN = 4896
DFF = 2816
EPS = 1e-6

SEM_WRAP = 6000
DMA_RING = 8
NCF = 13 * 128 + 2


class Res:
    __slots__ = ("name", "last_w", "readers")

    def __init__(self, name):
        self.name = name
        self.last_w = None
        self.readers = []


class Op:
    __slots__ = ("eng", "fn", "deps", "is_dma", "has_dep", "sem", "val", "slot_prev", "barrier")

    def __init__(self, eng, fn, is_dma):
        self.eng = eng
        self.fn = fn
        self.deps = []
        self.is_dma = is_dma
        self.has_dep = False
        self.sem = None
        self.val = 0
        self.slot_prev = None
        self.barrier = False


class Prog:
    ENGS = ("pe", "act", "dve", "pool", "sp")

    def __init__(self, nc):
        self.nc = nc
        self.ops = []
        self.stack = contextlib.ExitStack()
        self.res_all = []
        self.n = 0
        self.last_op = {e: None for e in self.ENGS}
        self.dma_last = {e: {} for e in self.ENGS}
        self.dma_cnt = {e: 0 for e in self.ENGS}
        self.bar_deps = {e: None for e in self.ENGS}

    def sbuf(self, name, shape, dt, stack=None):
        self.n += 1
        t = (stack or self.stack).enter_context(self.nc.sbuf_tensor(f"{name}_{self.n}", list(shape), dt))
        return t

    def psum(self, name, shape, dt, stack=None):
        self.n += 1
        t = (stack or self.stack).enter_context(self.nc.psum_tensor(f"{name}_{self.n}", list(shape), dt))
        return t

    def res(self, name="r"):
        r = Res(name)
        self.res_all.append(r)
        return r

    def _add(self, eng, fn, reads, writes, is_dma):
        o = Op(eng, fn, is_dma)
        deps = []
        seen = set()

        def add(d):
            if d is not None and id(d) not in seen:
                seen.add(id(d))
                deps.append(d)

        for r in reads:
            add(r.last_w)
        for w in writes:
            add(w.last_w)
            for rd in w.readers:
                add(rd)
        for r in reads:
            if not is_dma:
                r.readers = [x for x in r.readers if x.is_dma or x.eng != eng]
            r.readers.append(o)
        for w in writes:
            w.last_w = o
            w.readers = []
        if self.bar_deps[eng] is not None:
            for d in self.bar_deps[eng]:
                add(d)
            self.bar_deps[eng] = None
        if eng == "pe" and not is_dma:
            deps = [d for d in deps if d.is_dma or d.eng != "pe"]
        o.deps = deps
        for d in deps:
            d.has_dep = True
        if is_dma:
            i = self.dma_cnt[eng]
            self.dma_cnt[eng] += 1
            slot = i % DMA_RING
            o.val = 16 * (i // DMA_RING + 1)
            o.sem = (eng, slot)
            o.slot_prev = self.dma_last[eng].get(slot)
            self.dma_last[eng][slot] = o
        else:
            self.last_op[eng] = o
        self.ops.append(o)
        return o

    def op(self, eng, fn, reads=(), writes=()):
        return self._add(eng, fn, reads, writes, False)

    def dma(self, eng, fn, reads=(), writes=()):
        return self._add(eng, fn, reads, writes, True)

    def _tails(self):
        b = [o for o in self.last_op.values() if o is not None]
        for q in self.dma_last.values():
            b.extend(q.values())
        for o in b:
            o.has_dep = True
        return b

    def barrier(self):
        b = self._tails()
        for e in self.ENGS:
            self.bar_deps[e] = list(b)

    def emit(self):
        nc = self.nc
        ops = self.ops
        final = self._tails()
        per_eng = {e: [] for e in self.ENGS}
        cnt = {e: 0 for e in self.ENGS}
        semlist = {e: [] for e in self.ENGS}
        dma_ring = {}
        stack = self.stack

        def new_sem(name):
            return stack.enter_context(nc.semaphore(name))

        for o in ops:
            e = o.eng
            if o.is_dma:
                if o.sem not in dma_ring:
                    dma_ring[o.sem] = new_sem(f"dq_{o.sem[0]}_{o.sem[1]}")
                o.sem = dma_ring[o.sem]
            elif o.has_dep:
                c = cnt[e]
                cnt[e] += 1
                si = c // SEM_WRAP
                if len(semlist[e]) <= si:
                    semlist[e].append(new_sem(f"s_{e}_{si}"))
                o.sem = semlist[e][si]
                o.val = c % SEM_WRAP + 1
            per_eng[e].append(o)
        self.stats = dict(cnt=cnt, dma=dict(self.dma_cnt), nops=len(ops),
                          nsem=sum(len(v) for v in semlist.values()) + len(dma_ring))

        engmap = {"pe": "tensor", "act": "scalar", "dve": "vector", "pool": "gpsimd", "sp": "sync"}

        def run_engine(ename, eng):
            waited = {}

            def wait(sem, val):
                k = id(sem)
                if waited.get(k, 0) >= val:
                    return
                waited[k] = val
                eng.wait_ge(sem, val)

            for o in per_eng[ename]:
                for d in o.deps:
                    if d.sem is None:
                        continue
                    if (not d.is_dma) and d.eng == ename and ename == "pe":
                        continue
                    wait(d.sem, d.val)
                if o.is_dma:
                    if o.slot_prev is not None:
                        wait(o.slot_prev.sem, o.slot_prev.val)
                    o.fn(eng).then_inc(o.sem, 16)
                else:
                    ins = o.fn(eng)
                    if o.sem is not None:
                        ins.then_inc(o.sem, 1)
            for d in final:
                wait(d.sem, d.val)

        with nc.Block() as block:
            for ename in self.ENGS:
                getattr(block, engmap[ename])(lambda eng, ename=ename: run_engine(ename, eng))


def _bf(a):
    return np.ascontiguousarray(a).astype(ml_dtypes.bfloat16)


class Builder:
    def __init__(self, nc, T=SEQ):
        self.nc = nc
        self.T = T
        self.P = Prog(nc)
        self.d = {}
        self.debug = False

    def dram_in(self, name, shape, dt):
        self.d[name] = self.nc.dram_tensor(name, list(shape), dt, kind="ExternalInput").ap()
        return self.d[name]

    def dram_out(self, name, shape, dt):
        self.d[name] = self.nc.dram_tensor(name, list(shape), dt, kind="ExternalOutput").ap()
        return self.d[name]

    def dram_tmp(self, name, shape, dt):
        self.d[name] = self.nc.dram_tensor(name, list(shape), dt,
                                           kind="ExternalOutput" if self.debug else "Internal").ap()
        return self.d[name]

    def load_consts(self):
        P, nc = self.P, self.nc
        self.ones_f = P.sbuf("ones_f", [128, 128], F32)
        r = self.r_const = P.res("const")
        P.op("pool", lambda e: e.memset(self.ones_f[:], 1.0), writes=[r])
        self.eps_c = P.sbuf("eps_c", [128, 1], F32)
        P.op("pool", lambda e: e.memset(self.eps_c[:], EPS), writes=[r])
        self.cf = P.sbuf("cf", [128, NCF], F32)
        P.dma("sp", lambda e: e.dma_start(out=self.cf[:], in_=self.d["cf32"][:, :]), writes=[r])
        self.blk_f = self.cf[:, 0:128]
        self.rot_f = self.cf[:, 128:256]
        c = lambda i: self.cf[:, i * 128:(i + 1) * 128]
        self.identf = c(2)
        self.tri = [c(3), c(4)]
        self.maskS = [c(5), c(6)]
        self.maskI = [c(7), c(8)]
        self.ntri = [c(9), c(10)]
        self.selc = self.cf[:, 13 * 128:13 * 128 + 2]
        self.identb = P.sbuf("identb", [128, 128], BF16)
        P.op("dve", lambda e: e.tensor_copy(out=self.identb[:], in_=self.identf), reads=[r], writes=[r])
        self.zero_f = P.sbuf("zero_f", [128, 512], F32)
        P.op("pool", lambda e: e.memset(self.zero_f[:], 0.0), writes=[r])
        self.zero_b = P.sbuf("zero_b", [128, 512], BF16)
        P.op("pool", lambda e: e.memset(self.zero_b[:], 0.0), writes=[r])

    def ffn(self, l, xin, xout, rx_in, rx_out):
        P, nc, T = self.P, self.nc, self.T
        d = self.d
        W = 510
        ntile = (T + W - 1) // W
        HC = DFF // 2
        NJ = HC // 128
        with contextlib.ExitStack() as st:
            g_sb = P.sbuf("ffn_g", [128, 8], F32, st)
            cw = P.sbuf("ffn_cw", [128, 3, 2 * NJ], F32, st)
            wup = P.sbuf("wup", [128, 8, 2 * HC], BF16, st)
            wdn = P.sbuf("wdn", [128, NJ, D], BF16, st)
            stg = [P.sbuf("stg", [128, HC], F32, st) for _ in range(2)]
            xt = [P.sbuf("xt", [128, 8, 512], F32, st) for _ in range(2)]
            xr = [P.sbuf("xr", [128, 512], F32, st) for _ in range(2)]
            sq = P.sbuf("sq", [128, 8, 512], F32, st)
            rstd = P.sbuf("rstd", [128, 512], F32, st)
            hT = P.sbuf("hT", [128, 8, 512], BF16, st)
            act = P.sbuf("act", [128, NJ, 512], BF16, st)
            tg = [P.sbuf("tg", [128, 512], F32, st) for _ in range(2)]
            tv = [P.sbuf("tv", [128, 512], F32, st) for _ in range(2)]
            sg = [P.sbuf("sg", [128, 512], F32, st) for _ in range(2)]
            xo = [P.sbuf("xo", [128, 512], F32, st) for _ in range(2)]
            ps_ss = P.psum("ps_ss", [128, 512], F32, st)
            ps_g = [P.psum("ps_g", [128, 512], F32, st) for _ in range(2)]
            ps_v = [P.psum("ps_v", [128, 512], F32, st) for _ in range(2)]
            ps_y = [P.psum("ps_y", [128, 512], F32, st) for _ in range(2)]
            R = P.res
            r_g, r_cw, r_wup, r_wdn = R(), R(), R(), R()
            r_stg = [R(), R()]
            r_xt = [R(), R()]
            r_xr = [R(), R()]
            r_sq, r_rstd, r_hT, r_act = R(), R(), R(), R()
            r_tg, r_tv, r_sg, r_xo = [R(), R()], [R(), R()], [R(), R()], [R(), R()]
            r_pss = R()
            r_psg, r_psv, r_psy = [R(), R()], [R(), R()], [R(), R()]

            P.dma("sp", lambda e: e.dma_start(out=g_sb[:], in_=d["norm_ffn_g"][l].rearrange("(kt p) -> p kt", p=128),
                                              allow_slow_non_contiguous=True), writes=[r_g])
            xtiled_in = xin.rearrange("(kt p) n -> p kt n", p=128)
            xtiled_out = xout.rearrange("(kt p) n -> p kt n", p=128)
            for ph in range(2):
                c0 = ph * HC
                for tap in range(3):
                    for half in range(2):
                        P.dma("sp", lambda e, tap=tap, half=half, c0=c0: e.dma_start(
                            out=cw[:, tap, half * NJ:(half + 1) * NJ],
                            in_=d["ffn_conv_w"][l][tap, half * DFF + c0:half * DFF + c0 + HC].rearrange("(m p) -> p m", p=128),
                            allow_slow_non_contiguous=True), writes=[r_cw])
                k = 0
                for kt in range(8):
                    for half in range(2):
                        col = half * DFF + c0
                        s = k % 2
                        k += 1
                        P.dma("sp", lambda e, s=s, kt=kt, col=col: e.dma_start(
                            out=stg[s][:], in_=d["w_up"][l][kt * 128:(kt + 1) * 128, col:col + HC]),
                            writes=[r_stg[s]])
                        if half == 0:
                            P.op("dve", lambda e, s=s, kt=kt, half=half: e.tensor_scalar(
                                out=wup[:, kt, half * HC:(half + 1) * HC], in0=stg[s][:], scalar1=g_sb[:, kt:kt + 1],
                                scalar2=None, op0=ALU.mult), reads=[r_stg[s], r_g], writes=[r_wup])
                        else:
                            P.op("act", lambda e, s=s, kt=kt, half=half: e.activation(
                                out=wup[:, kt, half * HC:(half + 1) * HC], in_=stg[s][:], func=AF.Copy,
                                scale=g_sb[:, kt:kt + 1]), reads=[r_stg[s], r_g], writes=[r_wup])
                for j in range(NJ):
                    s = k % 2
                    k += 1
                    P.dma("sp", lambda e, s=s, j=j, c0=c0: e.dma_start(
                        out=stg[s][:, 0:D], in_=d["w_down"][l][c0 + j * 128:c0 + (j + 1) * 128, :]),
                        writes=[r_stg[s]])
                    if j % 2 == 0:
                        P.op("dve", lambda e, s=s, j=j: e.tensor_copy(out=wdn[:, j, :], in_=stg[s][:, 0:D]),
                             reads=[r_stg[s]], writes=[r_wdn])
                    else:
                        P.op("act", lambda e, s=s, j=j: e.copy(out=wdn[:, j, :], in_=stg[s][:, 0:D]),
                             reads=[r_stg[s]], writes=[r_wdn])
                for ti in range(ntile):
                    b = ti % 2
                    s0 = ti * W
                    nout = min(W, T - s0)
                    lo = max(s0 - 1, 0)
                    hi = min(s0 + nout + 1, T)
                    off = lo - (s0 - 1)
                    full = (off == 0 and hi - lo == 512)
                    if not full:
                        P.op("pool", lambda e, b=b: e.memset(xt[b][:], 0.0), writes=[r_xt[b]])
                    P.dma("sp", lambda e, b=b, lo=lo, hi=hi, off=off: e.dma_start(
                        out=xt[b][:, :, off:off + hi - lo], in_=xtiled_in[:, :, lo:hi]), reads=[rx_in], writes=[r_xt[b]])
                    P.op("act", lambda e, b=b: e.activation(out=sq[:], in_=xt[b][:], func=AF.Square),
                         reads=[r_xt[b]], writes=[r_sq])
                    for kt in range(8):
                        P.op("pe", lambda e, kt=kt: e.matmul(ps_ss[:], lhsT=self.ones_f[:], rhs=sq[:, kt, :],
                                                             start=(kt == 0), stop=(kt == 7)),
                             reads=[r_sq, self.r_const], writes=[r_pss])
                    P.op("act", lambda e: e.activation(out=rstd[:], in_=ps_ss[:], func=AF.Sqrt, bias=self.eps_c[:],
                                                       scale=1.0 / D), reads=[r_pss, self.r_const], writes=[r_rstd])
                    P.op("dve", lambda e: e.reciprocal(out=rstd[:], in_=rstd[:]), reads=[r_rstd], writes=[r_rstd])
                    for kt in range(8):
                        P.op("dve", lambda e, b=b, kt=kt: e.scalar_tensor_tensor(
                            out=hT[:, kt, :], in0=xt[b][:, kt, :], scalar=1.0, in1=rstd[:], op0=ALU.mult,
                            op1=ALU.mult), reads=[r_xt[b], r_rstd], writes=[r_hT])
                    for j in range(NJ):
                        q = j % 2
                        for kt in range(8):
                            P.op("pe", lambda e, q=q, kt=kt, j=j: e.matmul(
                                ps_g[q][:], lhsT=wup[:, kt, j * 128:(j + 1) * 128], rhs=hT[:, kt, :],
                                start=(kt == 0), stop=(kt == 7)), reads=[r_wup, r_hT], writes=[r_psg[q]])
                        for kt in range(8):
                            P.op("pe", lambda e, q=q, kt=kt, j=j: e.matmul(
                                ps_v[q][:], lhsT=wup[:, kt, HC + j * 128:HC + (j + 1) * 128], rhs=hT[:, kt, :],
                                start=(kt == 0), stop=(kt == 7)), reads=[r_wup, r_hT], writes=[r_psv[q]])
                        for (ps, rps, t, rt, jj) in ((ps_g[q], r_psg[q], tg[q], r_tg[q], j),
                                                     (ps_v[q], r_psv[q], tv[q], r_tv[q], NJ + j)):
                            P.op("act", lambda e, ps=ps, t=t, jj=jj: e.activation(
                                out=t[:, 1:511], in_=ps[:, 0:510], func=AF.Copy, scale=cw[:, 0, jj:jj + 1]),
                                reads=[rps, r_cw], writes=[rt])
                            for tap in (1, 2):
                                P.op("dve", lambda e, ps=ps, t=t, jj=jj, tap=tap: e.scalar_tensor_tensor(
                                    out=t[:, 1:511], in0=ps[:, tap:tap + 510], scalar=cw[:, tap, jj:jj + 1],
                                    in1=t[:, 1:511], op0=ALU.mult, op1=ALU.add),
                                    reads=[rps, r_cw, rt], writes=[rt])
                        P.op("act", lambda e, q=q: e.activation(out=sg[q][:, 1:511], in_=tg[q][:, 1:511], func=AF.Silu),
                             reads=[r_tg[q]], writes=[r_sg[q]])
                        P.op("pool", lambda e, q=q, j=j: e.tensor_tensor(
                            out=act[:, j, 1:511], in0=sg[q][:, 1:511], in1=tv[q][:, 1:511], op=ALU.mult),
                            reads=[r_sg[q], r_tv[q]], writes=[r_act])
                    for mo in range(8):
                        q = mo % 2
                        if ph == 1:
                            P.dma("sp", lambda e, q=q, mo=mo, s0=s0, nout=nout: e.dma_start(
                                out=xr[q][:, 1:1 + nout], in_=xout[mo * 128:(mo + 1) * 128, s0:s0 + nout]),
                                reads=[rx_out], writes=[r_xr[q]])
                        for j in range(NJ):
                            P.op("pe", lambda e, q=q, j=j, mo=mo: e.matmul(
                                ps_y[q][:, 1:511], lhsT=wdn[:, j, mo * 128:(mo + 1) * 128], rhs=act[:, j, 1:511],
                                start=(j == 0), stop=(j == NJ - 1)), reads=[r_wdn, r_act], writes=[r_psy[q]])
                        if ph == 1:
                            P.op("dve", lambda e, q=q: e.tensor_tensor(
                                out=xo[q][:, 1:511], in0=ps_y[q][:, 1:511], in1=xr[q][:, 1:511], op=ALU.add),
                                reads=[r_psy[q], r_xr[q]], writes=[r_xo[q]])
                        else:
                            P.op("dve", lambda e, q=q, mo=mo, b=b: e.tensor_tensor(
                                out=xo[q][:, 1:511], in0=ps_y[q][:, 1:511], in1=xt[b][:, mo, 1:511], op=ALU.add),
                                reads=[r_psy[q], r_xt[b]], writes=[r_xo[q]])
                        P.dma("sp", lambda e, q=q, mo=mo, s0=s0, nout=nout: e.dma_start(
                            out=xout[mo * 128:(mo + 1) * 128, s0:s0 + nout], in_=xo[q][:, 1:1 + nout]),
                            reads=[r_xo[q]], writes=[rx_out])
            P.barrier()

    def load_w(self, st, dst, rdst, src, ncols, scale=None, rscale=None, stg=None, rstg=None, eng="pool"):
        P = self.P
        nk = src.shape[0] // 128
        CH = stg[0].shape[1]
        k = 0
        for kt in range(nk):
            for c0 in range(0, ncols, CH):
                cn = min(CH, ncols - c0)
                s = k % 2
                k += 1
                P.dma("sp", lambda e, s=s, kt=kt, c0=c0, cn=cn: e.dma_start(
                    out=stg[s][:, 0:cn], in_=src[kt * 128:(kt + 1) * 128, c0:c0 + cn]), writes=[rstg[s]])
                if k % 2 == 0:
                    if scale is not None:
                        P.op("dve", lambda e, s=s, kt=kt, c0=c0, cn=cn: e.tensor_scalar(
                            out=dst[:, kt, c0:c0 + cn], in0=stg[s][:, 0:cn], scalar1=scale[:, kt:kt + 1], scalar2=None,
                            op0=ALU.mult), reads=[rstg[s], rscale], writes=[rdst])
                    else:
                        P.op("dve", lambda e, s=s, kt=kt, c0=c0, cn=cn: e.tensor_copy(
                            out=dst[:, kt, c0:c0 + cn], in_=stg[s][:, 0:cn]), reads=[rstg[s]], writes=[rdst])
                else:
                    if scale is not None:
                        P.op("act", lambda e, s=s, kt=kt, c0=c0, cn=cn: e.activation(
                            out=dst[:, kt, c0:c0 + cn], in_=stg[s][:, 0:cn], func=AF.Copy, scale=scale[:, kt:kt + 1]),
                            reads=[rstg[s], rscale], writes=[rdst])
                    else:
                        P.op("act", lambda e, s=s, kt=kt, c0=c0, cn=cn: e.copy(
                            out=dst[:, kt, c0:c0 + cn], in_=stg[s][:, 0:cn]), reads=[rstg[s]], writes=[rdst])

    def rms_tile(self, xt, r_xt, hT, r_hT, tmp):
        P = self.P
        sqt, r_sqt, ps_ss, r_pss, rstd, r_rstd = tmp
        for kt in range(8):
            s = kt % 2
            P.op("act", lambda e, s=s, kt=kt: e.activation(out=sqt[s][:], in_=xt[:, kt, :], func=AF.Square),
                 reads=[r_xt], writes=[r_sqt[s]])
            P.op("pe", lambda e, s=s, kt=kt: e.matmul(ps_ss[:], lhsT=self.ones_f[:], rhs=sqt[s][:],
                                                      start=(kt == 0), stop=(kt == 7)),
                 reads=[r_sqt[s], self.r_const], writes=[r_pss])
        P.op("act", lambda e: e.activation(out=rstd[:], in_=ps_ss[:], func=AF.Sqrt, bias=self.eps_c[:],
                                           scale=1.0 / D), reads=[r_pss, self.r_const], writes=[r_rstd])
        P.op("dve", lambda e: e.reciprocal(out=rstd[:], in_=rstd[:]), reads=[r_rstd], writes=[r_rstd])
        for kt in range(8):
            P.op("dve" if kt % 2 == 0 else "pool", lambda e, kt=kt: e.tensor_tensor(
                out=hT[:, kt, :], in0=xt[:, kt, :], in1=rstd[:], op=ALU.mult),
                reads=[r_xt, r_rstd], writes=[r_hT])

    def proj(self, l, xin, rx_in):
        P, nc, T, d = self.P, self.nc, self.T, self.d
        R = P.res
        NT = T // 512
        OQA, OKA, OVA, OQD, OBD, OZ, OG = 0, 512, 640, 768, 2304, 2336, 2848
        with contextlib.ExitStack() as st:
            g_sb = P.sbuf("mix_g", [128, 8], F32, st)
            win = P.sbuf("win", [128, 8, NIN], BF16, st)
            wk2 = P.sbuf("wk2", [128, 8, 256], BF16, st)
            stg = [P.sbuf("stg", [128, 1224], F32, st) for _ in range(2)]
            qg = P.sbuf("qg", [128, 2], F32, st)
            xt = P.sbuf("xt", [128, 8, 512], F32, st)
            hT = P.sbuf("hT", [128, 8, 512], BF16, st)
            sqt = [P.sbuf("sqt", [128, 512], F32, st) for _ in range(2)]
            rstd = P.sbuf("rstd", [128, 512], F32, st)
            cs = [P.sbuf("cs", [128, 2, 512], F32, st) for _ in range(2)]
            sqq = [P.sbuf("sqq", [128, 512], F32, st) for _ in range(2)]
            rs = [P.sbuf("rs", [128, 512], F32, st) for _ in range(2)]
            qn = [P.sbuf("qn", [128, 512], F32, st) for _ in range(2)]
            t1 = [P.sbuf("t1", [128, 512], F32, st) for _ in range(2)]
            t2 = [P.sbuf("t2", [128, 512], F32, st) for _ in range(2)]
            ob = [P.sbuf("ob", [128, 512], BF16, st) for _ in range(3)]
            of = [P.sbuf("of", [128, 512], F32, st) for _ in range(3)]
            vb = [P.sbuf("vb", [128, 130], BF16, st) for _ in range(2)]
            ps_ss = P.psum("ps_ss", [128, 512], F32, st)
            ps_m = [P.psum("ps_m", [128, 512], F32, st) for _ in range(3)]
            ps_a = [P.psum("ps_a", [128, 512], F32, st) for _ in range(2)]
            r_g, r_win, r_wk2, r_qg, r_xt, r_hT, r_rstd, r_pss = R(), R(), R(), R(), R(), R(), R(), R()
            r_stg, r_sqt, r_cs, r_sqq, r_rs, r_qn, r_t1, r_t2 = ([R(), R()] for _ in range(8))
            r_ob, r_of, r_psm = ([R(), R(), R()] for _ in range(3))
            r_vb, r_psa = [R(), R()], [R(), R()]
            rd = self.rd

            P.dma("sp", lambda e: e.dma_start(out=g_sb[:], in_=d["norm_mix_g"][l].rearrange("(kt p) -> p kt", p=128),
                                              allow_slow_non_contiguous=True), writes=[r_g])
            import os
            SKIP = os.environ.get("SKIP", "")
            for h2 in range(0 if "qg" in SKIP else 2):
                P.dma("sp", lambda e, h2=h2: e.dma_start(out=qg[h2 * 64:(h2 + 1) * 64, 0:1],
                                                         in_=d["q_norm_g"][l].rearrange("(p o) -> p o", o=1),
                                                         allow_slow_non_contiguous=True), writes=[r_qg])
                P.dma("sp", lambda e, h2=h2: e.dma_start(out=qg[h2 * 64:(h2 + 1) * 64, 1:2],
                                                         in_=d["k_norm_g"][l].rearrange("(p o) -> p o", o=1),
                                                         allow_slow_non_contiguous=True), writes=[r_qg])
            P.op("dve", lambda e: e.tensor_scalar(out=qg[:, 0:1], in0=qg[:, 0:1], scalar1=0.125, scalar2=None,
                                                  op0=ALU.mult), reads=[r_qg], writes=[r_qg])
            self.load_w(st, win, r_win, d["w_in"][l], NIN, scale=g_sb, rscale=r_g, stg=stg, rstg=r_stg)
            for kt in range(0 if "wk2" in SKIP else 8):
                for g in range(2):
                    for dup in range(2):
                        P.op("pool", lambda e, kt=kt, g=g, dup=dup: e.tensor_copy(
                            out=wk2[:, kt, g * 128 + dup * 64:g * 128 + dup * 64 + 64],
                            in_=win[:, kt, OKA + g * 64:OKA + g * 64 + 64]), reads=[r_win], writes=[r_wk2])
            for b in range(0 if "vb" in SKIP else 2):
                for g in range(2):
                    P.op("pool", lambda e, b=b, g=g: e.memset(vb[b][:, g * 65 + 64:g * 65 + 65], 1.0), writes=[r_vb[b]])
            xtiled = xin.rearrange("(kt p) n -> p kt n", p=128)
            cnt = {"m": 0, "a": 0, "o": 0, "f": 0, "q": 0}

            def fm_group(lhs_fn, kind, dst, rdst, row0, c0, ti):
                i = cnt["m"] % 3
                cnt["m"] += 1
                for kt in range(8):
                    P.op("pe", lambda e, kt=kt, i=i: e.matmul(ps_m[i][:], lhsT=lhs_fn(kt), rhs=hT[:, kt, :],
                                                              start=(kt == 0), stop=(kt == 7)),
                         reads=[r_win, r_wk2, r_hT], writes=[r_psm[i]])
                if kind == "f32":
                    j = cnt["f"] % 3
                    cnt["f"] += 1
                    P.op("act", lambda e, i=i, j=j: e.copy(out=of[j][:], in_=ps_m[i][:]), reads=[r_psm[i]],
                         writes=[r_of[j]])
                    P.dma("sp", lambda e, j=j: e.dma_start(out=dst[row0:row0 + 128, c0:c0 + 512], in_=of[j][:]),
                          reads=[r_of[j]], writes=[rdst])
                elif kind == "sig":
                    j = cnt["o"] % 3
                    cnt["o"] += 1
                    P.op("act", lambda e, i=i, j=j: e.activation(out=ob[j][:], in_=ps_m[i][:], func=AF.Sigmoid),
                         reads=[r_psm[i]], writes=[r_ob[j]])
                    P.dma("sp", lambda e, j=j: e.dma_start(out=dst[row0:row0 + 128, c0:c0 + 512], in_=ob[j][:]),
                          reads=[r_ob[j]], writes=[rdst])
                else:
                    q = cnt["q"] % 2
                    cnt["q"] += 1
                    a = cnt["a"] % 2
                    cnt["a"] += 1
                    cb = ti % 2
                    P.op("act", lambda e, i=i, q=q: e.activation(out=sqq[q][:], in_=ps_m[i][:], func=AF.Square),
                         reads=[r_psm[i]], writes=[r_sqq[q]])
                    P.op("pe", lambda e, a=a, q=q: e.matmul(ps_a[a][:], lhsT=self.blk_f[:], rhs=sqq[q][:],
                                                            start=True, stop=True),
                         reads=[r_sqq[q], self.r_const], writes=[r_psa[a]])
                    P.op("act", lambda e, a=a, q=q: e.activation(out=rs[q][:], in_=ps_a[a][:], func=AF.Sqrt,
                                                                 bias=self.eps_c[:], scale=1.0 / 64),
                         reads=[r_psa[a], self.r_const], writes=[r_rs[q]])
                    P.op("dve", lambda e, q=q: e.reciprocal(out=rs[q][:], in_=rs[q][:]), reads=[r_rs[q]], writes=[r_rs[q]])
                    P.op("dve", lambda e, i=i, q=q: e.scalar_tensor_tensor(
                        out=qn[q][:], in0=ps_m[i][:], scalar=qg[:, kind:kind + 1], in1=rs[q][:], op0=ALU.mult,
                        op1=ALU.mult), reads=[r_psm[i], r_qg, r_rs[q]], writes=[r_qn[q]])
                    a2 = cnt["a"] % 2
                    cnt["a"] += 1
                    P.op("pe", lambda e, a2=a2, q=q: e.matmul(ps_a[a2][:], lhsT=self.rot_f[:], rhs=qn[q][:],
                                                              start=True, stop=True),
                         reads=[r_qn[q], self.r_const], writes=[r_psa[a2]])
                    P.op("pool", lambda e, q=q, cb=cb: e.tensor_tensor(out=t1[q][:], in0=qn[q][:], in1=cs[cb][:, 0, :],
                                                                       op=ALU.mult),
                         reads=[r_qn[q], r_cs[cb]], writes=[r_t1[q]])
                    P.op("dve", lambda e, q=q, cb=cb, a2=a2: e.tensor_tensor(out=t2[q][:], in0=ps_a[a2][:],
                                                                             in1=cs[cb][:, 1, :], op=ALU.mult),
                         reads=[r_psa[a2], r_cs[cb]], writes=[r_t2[q]])
                    j = cnt["o"] % 3
                    cnt["o"] += 1
                    P.op("pool", lambda e, q=q, j=j: e.tensor_tensor(out=ob[j][:], in0=t1[q][:], in1=t2[q][:], op=ALU.add),
                         reads=[r_t1[q], r_t2[q]], writes=[r_ob[j]])
                    P.dma("sp", lambda e, j=j: e.dma_start(out=dst[row0:row0 + 128, c0:c0 + 512], in_=ob[j][:]),
                          reads=[r_ob[j]], writes=[rdst])

            for ti in range(NT):
                c0 = ti * 512
                P.dma("sp", lambda e, c0=c0: e.dma_start(out=xt[:], in_=xtiled[:, :, c0:c0 + 512]),
                      reads=[rx_in], writes=[r_xt])
                P.dma("sp", lambda e, c0=c0, ti=ti: e.dma_start(out=cs[ti % 2][:, 0, :], in_=d["ropec"][:, c0:c0 + 512]),
                      writes=[r_cs[ti % 2]])
                P.dma("sp", lambda e, c0=c0, ti=ti: e.dma_start(out=cs[ti % 2][:, 1, :], in_=d["ropes"][:, c0:c0 + 512]),
                      writes=[r_cs[ti % 2]])
                self.rms_tile(xt, r_xt, hT, r_hT, (sqt, r_sqt, ps_ss, r_pss, rstd, r_rstd))
                import os
                PARTS = os.environ.get("PROJ_PARTS", "qdgt")
                for m in range(4 if "q" in PARTS else 0):
                    fm_group(lambda kt, m=m: win[:, kt, OQA + m * 128:OQA + (m + 1) * 128], 0, d["qaT"], rd["qaT"],
                             m * 128, c0, ti)
                for g in range(2 if "q" in PARTS else 0):
                    fm_group(lambda kt, g=g: wk2[:, kt, g * 128:(g + 1) * 128], 1, d["kaT"], rd["kaT"], g * 128, c0, ti)
                for m in range(12 if "d" in PARTS else 0):
                    fm_group(lambda kt, m=m: win[:, kt, OQD + m * 128:OQD + (m + 1) * 128], "f32", d["dpre"], rd["dpre"],
                             m * 128, c0 + 2, ti)
                for m in range(16 if "g" in PARTS else 0):
                    fm_group(lambda kt, m=m: win[:, kt, OG + m * 128:OG + (m + 1) * 128], "sig", d["gT"], rd["gT"],
                             m * 128, c0, ti)
                for sub in range(4 if "t" in PARTS else 0):
                    r0 = c0 + sub * 128
                    i = cnt["m"] % 3
                    cnt["m"] += 1
                    for kt in range(8):
                        P.op("pe", lambda e, kt=kt, i=i, sub=sub: e.matmul(
                            ps_m[i][:, 0:128], lhsT=hT[:, kt, sub * 128:(sub + 1) * 128], rhs=win[:, kt, OVA:OVA + 128],
                            start=(kt == 0), stop=(kt == 7)), reads=[r_win, r_hT], writes=[r_psm[i]])
                    b = sub % 2
                    for g in range(2):
                        P.op("act", lambda e, i=i, b=b, g=g: e.copy(out=vb[b][:, g * 65:g * 65 + 64],
                                                                     in_=ps_m[i][:, g * 64:(g + 1) * 64]),
                             reads=[r_psm[i]], writes=[r_vb[b]])
                    P.dma("sp", lambda e, b=b, r0=r0: e.dma_start(out=d["va"][r0:r0 + 128, :], in_=vb[b][:]),
                          reads=[r_vb[b]], writes=[rd["va"]])
                    for (oc, ncol, dst, rdst) in ((OBD, 32, d["bd"], rd["bd"]), (OZ, 512, d["z"], rd["z"])):
                        i = cnt["m"] % 3
                        cnt["m"] += 1
                        for kt in range(8):
                            P.op("pe", lambda e, kt=kt, i=i, sub=sub, oc=oc, ncol=ncol: e.matmul(
                                ps_m[i][:, 0:ncol], lhsT=hT[:, kt, sub * 128:(sub + 1) * 128],
                                rhs=win[:, kt, oc:oc + ncol], start=(kt == 0), stop=(kt == 7)),
                                reads=[r_win, r_hT], writes=[r_psm[i]])
                        j = cnt["f"] % 3
                        cnt["f"] += 1
                        P.op("act", lambda e, i=i, j=j, ncol=ncol: e.copy(out=of[j][:, 0:ncol], in_=ps_m[i][:, 0:ncol]),
                             reads=[r_psm[i]], writes=[r_of[j]])
                        P.dma("sp", lambda e, j=j, r0=r0, ncol=ncol, dst=dst: e.dma_start(
                            out=dst[r0:r0 + 128, :], in_=of[j][:, 0:ncol]), reads=[r_of[j]], writes=[rdst])
            P.barrier()

    def attention(self, l):
        P, nc, T, d, rd = self.P, self.nc, self.T, self.d, self.rd
        R = P.res
        NKT = T // 128
        NQC = T // 512
        with contextlib.ExitStack() as st:
            kg = P.sbuf("kg", [128, T], BF16, st)
            vg = P.sbuf("vg", [128, NKT, 65], BF16, st)
            qt = [P.sbuf("qt", [128, 512], BF16, st) for _ in range(2)]
            pT = [P.sbuf("pT", [128, 512], BF16, st) for _ in range(4)]
            rsum = P.sbuf("rsum", [128, 512], F32, st)
            ocp = [P.sbuf("ocp", [64, 512], F32, st) for _ in range(2)]
            oo = [P.sbuf("oo", [64, 512], BF16, st) for _ in range(2)]
            ps_s = [P.psum("ps_s", [128, 512], F32, st) for _ in range(4)]
            ps_o = [P.psum("ps_o", [128, 512], F32, st) for _ in range(2)]
            ps_b = P.psum("ps_b", [128, 512], F32, st)
            r_kg, r_vg, r_rsum, r_psb = R(), R(), R(), R()
            r_qt, r_ocp, r_oo, r_pso = ([R(), R()] for _ in range(4))
            r_pT, r_pss = ([R(), R(), R(), R()] for _ in range(2))
            LA = 2
            for g in range(2):
                P.dma("sp", lambda e, g=g: e.dma_start(out=kg[:], in_=d["kaT"][g * 128:(g + 1) * 128, :]),
                      reads=[rd["kaT"]], writes=[r_kg])
                P.dma("sp", lambda e, g=g: e.dma_start(
                    out=vg[:], in_=d["va"][:, g * 65:(g + 1) * 65].rearrange("(kt p) c -> p kt c", p=128)),
                    reads=[rd["va"]], writes=[r_vg])
                tiles = []
                for qc in range(NQC):
                    for pair in range(2):
                        for h2 in range(2):
                            tiles.append((qc, pair, h2))
                items = [(ti_, kt) for ti_ in range(len(tiles)) for kt in range(NKT)]
                NI = len(items)

                def emit_S(idx):
                    ti_, kt = items[idx]
                    qc, pair, h2 = tiles[ti_]
                    qb = (qc * 2 + pair) % 2
                    if h2 == 0 and kt == 0:
                        mrow = (g * 2 + pair) * 128
                        P.dma("sp", lambda e, qb=qb, mrow=mrow, qc=qc: e.dma_start(
                            out=qt[qb][:], in_=d["qaT"][mrow:mrow + 128, qc * 512:(qc + 1) * 512]),
                            reads=[rd["qaT"]], writes=[r_qt[qb]])
                    pl, ph = h2 * 64, h2 * 64 + 64
                    s_ = idx % 4
                    P.op("pe", lambda e, s_=s_, kt=kt, qb=qb, pl=pl, ph=ph: e.matmul(
                        ps_s[s_][:], lhsT=kg[pl:ph, kt * 128:(kt + 1) * 128], rhs=qt[qb][pl:ph, :],
                        start=True, stop=True), reads=[r_kg, r_qt[qb]], writes=[r_pss[s_]])

                def emit_PV(idx):
                    ti_, kt = items[idx]
                    qc, pair, h2 = tiles[ti_]
                    head = g * 4 + pair * 2 + h2
                    ob_ = ti_ % 2
                    s_ = idx % 4
                    P.op("act", lambda e, s_=s_: e.activation(out=pT[s_][:], in_=ps_s[s_][:], func=AF.Exp),
                         reads=[r_pss[s_]], writes=[r_pT[s_]])
                    P.op("pe", lambda e, s_=s_, kt=kt, ob_=ob_: e.matmul(
                        ps_o[ob_][0:65, :], lhsT=vg[:, kt, :], rhs=pT[s_][:], start=(kt == 0),
                        stop=(kt == NKT - 1)), reads=[r_vg, r_pT[s_]], writes=[r_pso[ob_]])
                    if kt == NKT - 1:
                        P.op("dve", lambda e, ob_=ob_: e.reciprocal(out=rsum[64:65, :], in_=ps_o[ob_][64:65, :]),
                             reads=[r_pso[ob_]], writes=[r_rsum])
                        P.op("pe", lambda e: e.matmul(ps_b[0:64, :], lhsT=self.ones_f[64:65, 0:64], rhs=rsum[64:65, :],
                                                      start=True, stop=True),
                             reads=[r_rsum, self.r_const], writes=[r_psb])
                        P.op("act", lambda e, ob_=ob_: e.copy(out=ocp[ob_][:], in_=ps_o[ob_][0:64, :]),
                             reads=[r_pso[ob_]], writes=[r_ocp[ob_]])
                        P.op("dve", lambda e, ob_=ob_: e.tensor_tensor(out=oo[ob_][:], in0=ocp[ob_][:],
                                                                       in1=ps_b[0:64, :], op=ALU.mult),
                             reads=[r_ocp[ob_], r_psb], writes=[r_oo[ob_]])
                        P.dma("sp", lambda e, ob_=ob_, head=head, qc=qc: e.dma_start(
                            out=d["oaT"][head * 64:(head + 1) * 64, qc * 512:(qc + 1) * 512], in_=oo[ob_][:]),
                            reads=[r_oo[ob_]], writes=[rd["oaT"]])

                for idx in range(NI + LA):
                    if idx < NI:
                        emit_S(idx)
                    if idx - LA >= 0:
                        emit_PV(idx - LA)
            P.barrier()

    def merge(self, l, xin, xout, rx_in, rx_out):
        P, nc, T, d, rd = self.P, self.nc, self.T, self.d, self.rd
        R = P.res
        NT = T // 512
        with contextlib.ExitStack() as st:
            woa = P.sbuf("woa", [128, 4, D], BF16, st)
            wod = P.sbuf("wod", [128, 4, D], BF16, st)
            wo = P.sbuf("wo", [128, 8, D], BF16, st)
            stg = [P.sbuf("stg", [128, 1024], F32, st) for _ in range(2)]
            oa = [P.sbuf("oa", [128, 4, 512], BF16, st) for _ in range(2)]
            od = [P.sbuf("od", [128, 4, 512], BF16, st) for _ in range(2)]
            gt = [P.sbuf("gt", [128, 2, 512], BF16, st) for _ in range(2)]
            ta = [P.sbuf("ta", [128, 512], F32, st) for _ in range(2)]
            mx = P.sbuf("mx", [128, 8, 512], BF16, st)
            xr = [P.sbuf("xr", [128, 512], F32, st) for _ in range(2)]
            xo = [P.sbuf("xo", [128, 512], F32, st) for _ in range(2)]
            ps_a = [P.psum("ps_a", [128, 512], F32, st) for _ in range(2)]
            ps_d = [P.psum("ps_d", [128, 512], F32, st) for _ in range(2)]
            ps_y = [P.psum("ps_y", [128, 512], F32, st) for _ in range(2)]
            r_woa, r_wod, r_wo, r_mx = R(), R(), R(), R()
            r_stg, r_oa, r_od, r_gt, r_ta, r_xr, r_xo, r_psa, r_psd, r_psy = ([R(), R()] for _ in range(10))
            self.load_w(st, woa, r_woa, d["w_o_attn"][l], D, stg=stg, rstg=r_stg)
            self.load_w(st, wod, r_wod, d["w_o_dn"][l], D, stg=stg, rstg=r_stg)
            self.load_w(st, wo, r_wo, d["w_out"][l], D, stg=stg, rstg=r_stg)
            oaT = d["oaT"].rearrange("(kt p) n -> p kt n", p=128)
            odT = d["odT"].rearrange("(kt p) n -> p kt n", p=128)
            k = 0
            for ti in range(NT):
                c0 = ti * 512
                b = ti % 2
                P.dma("sp", lambda e, b=b, c0=c0: e.dma_start(out=oa[b][:], in_=oaT[:, :, c0:c0 + 512]),
                      reads=[rd["oaT"]], writes=[r_oa[b]])
                P.dma("sp", lambda e, b=b, c0=c0: e.dma_start(out=od[b][:], in_=odT[:, :, c0:c0 + 512]),
                      reads=[rd["odT"]], writes=[r_od[b]])
                for mo in range(8):
                    q = k % 2
                    k += 1
                    for br in range(2):
                        P.dma("sp", lambda e, q=q, br=br, mo=mo, c0=c0: e.dma_start(
                            out=gt[q][:, br, :], in_=d["gT"][br * D + mo * 128:br * D + (mo + 1) * 128, c0:c0 + 512]),
                            reads=[rd["gT"]], writes=[r_gt[q]])
                    for kt in range(4):
                        P.op("pe", lambda e, q=q, kt=kt, mo=mo, b=b: e.matmul(
                            ps_a[q][:], lhsT=woa[:, kt, mo * 128:(mo + 1) * 128], rhs=oa[b][:, kt, :],
                            start=(kt == 0), stop=(kt == 3)), reads=[r_woa, r_oa[b]], writes=[r_psa[q]])
                    for kt in range(4):
                        P.op("pe", lambda e, q=q, kt=kt, mo=mo, b=b: e.matmul(
                            ps_d[q][:], lhsT=wod[:, kt, mo * 128:(mo + 1) * 128], rhs=od[b][:, kt, :],
                            start=(kt == 0), stop=(kt == 3)), reads=[r_wod, r_od[b]], writes=[r_psd[q]])
                    P.op("dve", lambda e, q=q: e.tensor_tensor(out=ta[q][:], in0=ps_a[q][:], in1=gt[q][:, 0, :],
                                                               op=ALU.mult), reads=[r_psa[q], r_gt[q]], writes=[r_ta[q]])
                    P.op("dve", lambda e, q=q: e.tensor_tensor(out=xo[q][:], in0=ps_d[q][:], in1=gt[q][:, 1, :],
                                                               op=ALU.mult), reads=[r_psd[q], r_gt[q]], writes=[r_xo[q]])
                    P.op("pool", lambda e, q=q, mo=mo: e.tensor_tensor(out=mx[:, mo, :], in0=ta[q][:], in1=xo[q][:],
                                                                       op=ALU.add),
                         reads=[r_ta[q], r_xo[q]], writes=[r_mx])
                for mo in range(8):
                    q = k % 2
                    k += 1
                    P.dma("sp", lambda e, q=q, mo=mo, c0=c0: e.dma_start(
                        out=xr[q][:], in_=xin[mo * 128:(mo + 1) * 128, c0:c0 + 512]), reads=[rx_in], writes=[r_xr[q]])
                    for kt in range(8):
                        P.op("pe", lambda e, q=q, kt=kt, mo=mo: e.matmul(
                            ps_y[q][:], lhsT=wo[:, kt, mo * 128:(mo + 1) * 128], rhs=mx[:, kt, :],
                            start=(kt == 0), stop=(kt == 7)), reads=[r_wo, r_mx], writes=[r_psy[q]])
                    P.op("dve", lambda e, q=q: e.tensor_tensor(out=xo[q][:], in0=ps_y[q][:], in1=xr[q][:], op=ALU.add),
                         reads=[r_psy[q], r_xr[q]], writes=[r_xo[q]])
                    P.dma("sp", lambda e, q=q, mo=mo, c0=c0: e.dma_start(
                        out=xout[mo * 128:(mo + 1) * 128, c0:c0 + 512], in_=xo[q][:]), reads=[r_xo[q]], writes=[rx_out])
            P.barrier()

    def zero_dram(self, ap, rres, bf=False):
        P = self.P
        z = self.zero_b if bf else self.zero_f
        rows, cols = ap.shape
        for r0 in range(0, rows, 128):
            rn = min(128, rows - r0)
            for c0 in range(0, cols, 512):
                cn = min(512, cols - c0)
                P.dma("sp", lambda e, r0=r0, rn=rn, c0=c0, cn=cn: e.dma_start(
                    out=ap[r0:r0 + rn, c0:c0 + cn], in_=z[0:rn, 0:cn]), reads=[self.r_const], writes=[rres])


    def deltanet(self, l):
        P, nc, T, d, rd = self.P, self.nc, self.T, self.d, self.rd
        R = P.res
        NT = T // 512
        NCP = T // 128
        self.zero_dram(d["dpre"][:, 0:2], rd["dpre"])
        self.zero_dram(d["dpre"][:, T + 2:T + 4], rd["dpre"])
        with contextlib.ExitStack() as st:
            cwd = P.sbuf("cwd", [128, 5, 12], F32, st)
            xin = [P.sbuf("xin", [128, 516], F32, st) for _ in range(2)]
            acc = [P.sbuf("acc", [128, 512], F32, st) for _ in range(2)]
            sl = [P.sbuf("sl", [128, 512], F32, st) for _ in range(2)]
            sq = [P.sbuf("sq", [128, 512], F32, st) for _ in range(2)]
            rs = [P.sbuf("rs", [128, 512], F32, st) for _ in range(2)]
            ob = [P.sbuf("ob", [128, 512], BF16, st) for _ in range(2)]
            tmo = [P.sbuf("tmo", [128, 4, 128], BF16, st) for _ in range(2)]
            ps = [P.psum("ps", [128, 512], F32, st) for _ in range(2)]
            pst = [P.psum("pst", [128, 4, 128], BF16, st) for _ in range(2)]
            r_cwd = R()
            r_xin, r_acc, r_sl, r_sq, r_rs, r_ob, r_tmo, r_ps, r_pst = ([R(), R()] for _ in range(9))
            for tap in range(5):
                P.dma("sp", lambda e, tap=tap: e.dma_start(
                    out=cwd[:, tap, :], in_=d["dn_conv_w"][l][tap, :].rearrange("(m p) -> p m", p=128),
                    allow_slow_non_contiguous=True), writes=[r_cwd])
            it = 0
            for m in range(12):
                for ti in range(NT):
                    b = it % 2
                    it += 1
                    c0 = ti * 512
                    P.dma("sp", lambda e, b=b, m=m, c0=c0: e.dma_start(
                        out=xin[b][:], in_=d["dpre"][m * 128:(m + 1) * 128, c0:c0 + 516]),
                        reads=[rd["dpre"]], writes=[r_xin[b]])
                    P.op("act", lambda e, b=b, m=m: e.activation(out=acc[b][:], in_=xin[b][:, 0:512], func=AF.Copy,
                                                                 scale=cwd[:, 0, m:m + 1]),
                         reads=[r_xin[b], r_cwd], writes=[r_acc[b]])
                    for tap in range(1, 5):
                        P.op("dve", lambda e, b=b, m=m, tap=tap: e.scalar_tensor_tensor(
                            out=acc[b][:], in0=xin[b][:, tap:tap + 512], scalar=cwd[:, tap, m:m + 1], in1=acc[b][:],
                            op0=ALU.mult, op1=ALU.add), reads=[r_xin[b], r_cwd, r_acc[b]], writes=[r_acc[b]])
                    P.op("act", lambda e, b=b: e.activation(out=sl[b][:], in_=acc[b][:], func=AF.Silu),
                         reads=[r_acc[b]], writes=[r_sl[b]])
                    if m < 8:
                        P.op("act", lambda e, b=b: e.activation(out=sq[b][:], in_=sl[b][:], func=AF.Square),
                             reads=[r_sl[b]], writes=[r_sq[b]])
                        P.op("pe", lambda e, b=b: e.matmul(ps[b][:], lhsT=self.blk_f, rhs=sq[b][:], start=True, stop=True),
                             reads=[r_sq[b], self.r_const], writes=[r_ps[b]])
                        P.op("act", lambda e, b=b: e.activation(out=rs[b][:], in_=ps[b][:], func=AF.Sqrt,
                                                                bias=self.eps_c[:], scale=1.0),
                             reads=[r_ps[b], self.r_const], writes=[r_rs[b]])
                        P.op("dve", lambda e, b=b: e.reciprocal(out=rs[b][:], in_=rs[b][:]), reads=[r_rs[b]], writes=[r_rs[b]])
                        scl = 0.125 if m < 4 else 1.0
                        P.op("dve", lambda e, b=b, scl=scl: e.scalar_tensor_tensor(
                            out=ob[b][:], in0=sl[b][:], scalar=scl, in1=rs[b][:], op0=ALU.mult, op1=ALU.mult),
                            reads=[r_sl[b], r_rs[b]], writes=[r_ob[b]])
                        dst, rdst = (d["dqT"], rd["dqT"]) if m < 4 else (d["dkT"], rd["dkT"])
                        P.dma("sp", lambda e, b=b, m=m, c0=c0, dst=dst: e.dma_start(
                            out=dst[(m % 4) * 128:(m % 4 + 1) * 128, c0:c0 + 512], in_=ob[b][:]),
                            reads=[r_ob[b]], writes=[rdst])
                    else:
                        P.op("pool", lambda e, b=b: e.tensor_copy(out=ob[b][:], in_=sl[b][:]), reads=[r_sl[b]],
                             writes=[r_ob[b]])
                    if m >= 4:
                        for sub in range(4):
                            P.op("pe", lambda e, b=b, sub=sub: e.transpose(
                                out=pst[b][:, sub, :], in_=ob[b][:, sub * 128:(sub + 1) * 128], identity=self.identb[:]),
                                reads=[r_ob[b], self.r_const], writes=[r_pst[b]])
                        P.op("act", lambda e, b=b: e.copy(out=tmo[b][:], in_=pst[b][:]), reads=[r_pst[b]], writes=[r_tmo[b]])
                        dst, rdst = (d["dk_tm"], rd["dk_tm"]) if m < 8 else (d["dv_tm"], rd["dv_tm"])
                        mc = (m % 4) * 128
                        P.dma("sp", lambda e, b=b, mc=mc, c0=c0, dst=dst: e.dma_start(
                            out=dst[c0:c0 + 512, mc:mc + 128].rearrange("(s p) c -> p s c", p=128), in_=tmo[b][:]),
                            reads=[r_tmo[b]], writes=[rdst])
            P.barrier()
        import os
        DN_STOP = os.environ.get("DN_STOP", "")
        if DN_STOP == "A":
            return
        with contextlib.ExitStack() as st:
            bdl = P.sbuf("bdl", [128, NCP, 32], F32, st)
            tmp16 = P.sbuf("tmp16", [128, NCP, 16], F32, st)
            dtb = P.sbuf("dtb", [128, 16], F32, st)
            nea = P.sbuf("nea", [128, 16], F32, st)
            names = ("beta", "nbeta", "g", "gc", "ngc", "be", "e2")
            ga = {n: P.sbuf(n, [128, 2, NCP, 8], F32, st) for n in names}
            r_gate = R()
            pq = [P.psum("pq", [128, 4, 128], F32, st) for _ in range(5)]
            psg = [pq[i][:].rearrange("p a d -> p (a d)") for i in range(2)]
            r_psg = [R(), R()]
            P.dma("sp", lambda e: e.dma_start(out=bdl[:], in_=d["bd"].rearrange("(cp p) c -> p cp c", p=128)),
                  reads=[rd["bd"]], writes=[r_gate])
            P.dma("sp", lambda e: e.dma_start(
                out=dtb[:], in_=d["dn_dt_bias"][l].rearrange("a h -> (a h)").partition_broadcast(128)), writes=[r_gate])
            P.dma("sp", lambda e: e.dma_start(
                out=nea[:], in_=d["dn_a_log"][l].rearrange("a h -> (a h)").partition_broadcast(128)), writes=[r_gate])
            G = [r_gate]
            P.op("act", lambda e: e.activation(out=nea[:], in_=nea[:], func=AF.Exp), reads=G, writes=G)
            P.op("dve", lambda e: e.tensor_scalar(out=nea[:], in0=nea[:], scalar1=-1.0, scalar2=None, op0=ALU.mult),
                 reads=G, writes=G)
            P.op("dve", lambda e: e.tensor_tensor(out=tmp16[:], in0=bdl[:, :, 16:32],
                                                  in1=dtb[:].unsqueeze(1).broadcast_to([128, NCP, 16]), op=ALU.add),
                 reads=G, writes=G)
            P.op("act", lambda e: e.activation(out=tmp16[:], in_=tmp16[:], func=AF.Exp), reads=G, writes=G)
            P.op("act", lambda e: e.activation(out=tmp16[:], in_=tmp16[:], func=AF.Ln, bias=self.ones_f[:, 0:1], scale=1.0),
                 reads=G + [self.r_const], writes=G)
            for dr in range(2):
                P.op("dve", lambda e, dr=dr: e.tensor_tensor(
                    out=ga["g"][:, dr], in0=tmp16[:, :, dr * 8:(dr + 1) * 8],
                    in1=nea[:, dr * 8:(dr + 1) * 8].unsqueeze(1).broadcast_to([128, NCP, 8]), op=ALU.mult),
                    reads=G, writes=G)
                P.op("act", lambda e, dr=dr: e.activation(out=ga["beta"][:, dr], in_=bdl[:, :, dr * 8:(dr + 1) * 8],
                                                          func=AF.Sigmoid), reads=G, writes=G)
            P.op("dve", lambda e: e.tensor_scalar(out=ga["nbeta"][:], in0=ga["beta"][:], scalar1=-1.0, scalar2=None,
                                                  op0=ALU.mult), reads=G, writes=G)
            NB = NCP * 8
            for dr in range(2):
                gflat = ga["g"][:, dr].rearrange("p c h -> p (c h)")
                gcflat = ga["gc"][:, dr].rearrange("p c h -> p (c h)")
                e2flat = ga["e2"][:, dr].rearrange("p c h -> p (c h)")
                for c0 in range(0, NB, 512):
                    cn = min(512, NB - c0)
                    P.op("pe", lambda e, dr=dr, c0=c0, cn=cn, gflat=gflat: e.matmul(
                        psg[0][:, 0:cn], lhsT=self.tri[dr], rhs=gflat[:, c0:c0 + cn], start=True, stop=True),
                        reads=G + [self.r_const], writes=[r_psg[0]])
                    P.op("act", lambda e, c0=c0, cn=cn, gcflat=gcflat: e.copy(out=gcflat[:, c0:c0 + cn], in_=psg[0][:, 0:cn]),
                         reads=[r_psg[0]], writes=G)
                    P.op("pe", lambda e, c0=c0, cn=cn, gflat=gflat: e.matmul(
                        psg[1][:, 0:cn], lhsT=self.blk_f, rhs=gflat[:, c0:c0 + cn], start=True, stop=True),
                        reads=G + [self.r_const], writes=[r_psg[1]])
                    P.op("dve", lambda e, c0=c0, cn=cn, gcflat=gcflat, e2flat=e2flat: e.tensor_tensor(
                        out=e2flat[:, c0:c0 + cn], in0=psg[1][:, 0:cn], in1=gcflat[:, c0:c0 + cn], op=ALU.subtract),
                        reads=[r_psg[1]] + G, writes=G)
            P.op("act", lambda e: e.activation(out=ga["e2"][:], in_=ga["e2"][:], func=AF.Exp), reads=G, writes=G)
            P.op("dve", lambda e: e.tensor_scalar(out=ga["ngc"][:], in0=ga["gc"][:], scalar1=-1.0, scalar2=None,
                                                  op0=ALU.mult), reads=G, writes=G)
            P.op("act", lambda e: e.activation(out=ga["be"][:], in_=ga["gc"][:], func=AF.Exp), reads=G, writes=G)
            P.op("pool", lambda e: e.tensor_tensor(out=ga["be"][:], in0=ga["be"][:], in1=ga["beta"][:], op=ALU.mult),
                 reads=G, writes=G)

            def sb2(name, shape, dt, n=2):
                return [P.sbuf(name, shape, dt, st) for _ in range(n)], [R() for _ in range(n)]
            kq, r_kq = sb2("kq", [128, 2, 128], BF16, 3)
            ktm, r_ktm = sb2("ktm", [128, 2, 64], BF16, 3)
            vtm, r_vtm = sb2("vtm", [128, 2, 64], BF16, 3)
            G2, r_G2 = sb2("G2", [128, 2, 64], F32)
            Eg, r_Eg = sb2("Eg", [128, 128], F32)
            qdec, r_qdec = sb2("qdec", [128, 128], BF16)
            gtc, r_gtc = sb2("gtc", [128, 2], F32)
            vb_, r_vb = sb2("vb", [128, 2, 64], BF16)
            kbe, r_kbe = sb2("kbe", [128, 2, 64], BF16)
            kdec, r_kdec = sb2("kdec", [128, 2, 64], BF16)
            G1, r_G1 = sb2("G1", [128, 128], F32)
            G1n, r_G1n = sb2("G1n", [128, 128], F32)
            bcol, r_bcol = sb2("bcol", [128, 16], F32)
            Dm, r_Dm = sb2("Dm", [128, 128], F32)
            DmT, r_DmT = sb2("DmT", [128, 128], F32)
            Nn, r_Nn = sb2("Nn", [128, 128], BF16, 4)
            Mm, r_Mm = sb2("Mm", [128, 128], BF16, 4)
            TT, r_TT = sb2("TT", [128, 128], BF16)
            qkT, r_qkT = sb2("qkT", [128, 128], BF16, 4)
            u_, r_u = sb2("u", [128, 128], F32)
            wT2, r_wT2 = sb2("wT2", [128, 128], BF16)
            vn, r_vn = sb2("vn", [128, 128], BF16)
            osb, r_osb = sb2("osb", [128, 128], F32, 3)
            tS, r_tS = sb2("tS", [128, 128], F32)
            Sf = P.sbuf("Sf", [128, 128], F32, st)
            Sb = P.sbuf("Sb", [128, 128], BF16, st)
            r_S, r_Sb = R(), R()
            P.barrier()
            r_pq = [[R() for _ in range(4)] for _ in range(5)]
            ptb = P.psum("ptb", [128, 4, 128], BF16, st)
            r_ptb = [R() for _ in range(4)]
            psc = P.psum("psc", [128, 4, 128], F32, st)
            r_psc = [R() for _ in range(4)]
            cq = {}

            def scan_ps(kind):
                i = {"V": 0, "O": 1, "S": 2}[kind]
                return psc[:, i, :], r_psc[0]

            def PQ(kind, nb, bank):
                i = cq.get(kind, 0)
                cq[kind] = i + 1
                qn_ = nb[i % len(nb)]
                return pq[bank][:, qn_, :], r_pq[bank][0]

            cnt = {}

            def rot(name, n):
                i = cnt.get(name, 0)
                cnt[name] = i + 1
                return i % n

            CUT = os.environ.get("DN_CUT", "")
            ACTV = os.environ.get("ACTV", "")

            def ACTKW(b):
                if ACTV == "v1":
                    return dict()
                if ACTV == "v2":
                    return dict(bias=b)
                return dict(bias=b)

            class _Stop(Exception):
                pass

            def cut(tag):
                if CUT == tag:
                    raise _Stop()

            for hp in range(0 if DN_STOP == "B" else 4):
              try:
                  for dr in range(2):
                      P.op("pool", lambda e: e.memset(Sf[:], 0.0), writes=[r_S])
                      P.op("pool", lambda e: e.memset(Sb[:], 0.0), writes=[r_Sb])
                      odst, rodst = (d["o_f"], rd["o_f"]) if dr == 0 else (d["o_b"], rd["o_b"])
                      for step in range(NCP):
                          cp = step if dr == 0 else NCP - 1 - step
                          t0 = cp * 128
                          i3 = rot("ld", 3)
                          P.dma("sp", lambda e, i3=i3, hp=hp, t0=t0: e.dma_start(
                              out=kq[i3][:, 0, :], in_=d["dkT"][hp * 128:(hp + 1) * 128, t0:t0 + 128]),
                              reads=[rd["dkT"]], writes=[r_kq[i3]])
                          P.dma("sp", lambda e, i3=i3, hp=hp, t0=t0: e.dma_start(
                              out=kq[i3][:, 1, :], in_=d["dqT"][hp * 128:(hp + 1) * 128, t0:t0 + 128]),
                              reads=[rd["dqT"]], writes=[r_kq[i3]])
                          P.dma("sp", lambda e, i3=i3, hp=hp, t0=t0: e.dma_start(
                              out=ktm[i3][:].rearrange("p a d -> p (a d)"), in_=d["dk_tm"][t0:t0 + 128, hp * 128:(hp + 1) * 128]),
                              reads=[rd["dk_tm"]], writes=[r_ktm[i3]])
                          P.dma("sp", lambda e, i3=i3, hp=hp, t0=t0: e.dma_start(
                              out=vtm[i3][:].rearrange("p a d -> p (a d)"), in_=d["dv_tm"][t0:t0 + 128, hp * 128:(hp + 1) * 128]),
                              reads=[rd["dv_tm"]], writes=[r_vtm[i3]])
                          kT2 = kq[i3][:, 0, :]
                          qT2 = kq[i3][:, 1, :]
                          i2 = rot("u", 2)

                          def col(name, h0, n=2, dr=dr, cp=cp):
                              return ga[name][:, dr, cp, h0:h0 + n]
                          P.op("dve", lambda e, i2=i2, c=col("g", 2 * hp): e.tensor_copy(
                              out=G2[i2][:], in_=c.unsqueeze(2).broadcast_to([128, 2, 64])), reads=G, writes=[r_G2[i2]])
                          G2f = G2[i2][:].rearrange("p a d -> p (a d)")
                          pE, rpE = PQ("E", [0, 1], 0)
                          P.op("pe", lambda e, pE=pE, G2f=G2f, dr=dr: e.matmul(pE, lhsT=G2f, rhs=self.tri[dr], start=True, stop=True),
                               reads=[r_G2[i2], self.r_const], writes=[rpE])
                          P.op("act", lambda e, pE=pE, i2=i2: e.activation(out=Eg[i2][:], in_=pE, func=AF.Exp),
                               reads=[rpE], writes=[r_Eg[i2]])
                          P.op("dve", lambda e, i2=i2, qT2=qT2: e.tensor_tensor(out=qdec[i2][:], in0=qT2, in1=Eg[i2][:], op=ALU.mult),
                               reads=[r_kq[i3], r_Eg[i2]], writes=[r_qdec[i2]])
                          pG, rpG = PQ("G", [2, 3], 0)
                          P.op("pe", lambda e, pG=pG, G2f=G2f: e.matmul(pG[:, 0:2], lhsT=G2f, rhs=self.selc, start=True, stop=True),
                               reads=[r_G2[i2], self.r_const], writes=[rpG])
                          P.op("act", lambda e, pG=pG, i2=i2: e.activation(out=gtc[i2][:], in_=pG[:, 0:2], func=AF.Exp),
                               reads=[rpG], writes=[r_gtc[i2]])
                          for (dst_, rdst_, src_, rsrc_, cname) in ((vb_, r_vb, vtm, r_vtm, "beta"), (kbe, r_kbe, ktm, r_ktm, "be"),
                                                                    (kdec, r_kdec, ktm, r_ktm, "e2")):
                              P.op("dve", lambda e, i2=i2, i3=i3, dst_=dst_, src_=src_, c=col(cname, 2 * hp): e.tensor_tensor(
                                  out=dst_[i2][:], in0=src_[i3][:], in1=c.unsqueeze(2).broadcast_to([128, 2, 64]), op=ALU.mult),
                                  reads=[rsrc_[i3]] + G, writes=[rdst_[i2]])
                          cut("C1")
                          pU, rpU = PQ("U", [0, 1], 3)
                          qk_idx = []
                          for a in range(2):
                              h = 2 * hp + a
                              pa = slice(64 * a, 64 * a + 64)
                              ia = rot("a", 2)
                              P.op("dve", lambda e, ia=ia, c=col("g", h, 1): e.tensor_copy(
                                  out=G1[ia][:], in_=c.broadcast_to([128, 128])), reads=G, writes=[r_G1[ia]])
                              P.op("dve", lambda e, ia=ia, c=col("gc", h, 1): e.tensor_copy(out=bcol[ia][:, 0:1], in_=c),
                                   reads=G, writes=[r_bcol[ia]])
                              P.op("dve", lambda e, ia=ia, c=col("ngc", h, 1): e.tensor_copy(out=bcol[ia][:, 8:9], in_=c),
                                   reads=G, writes=[r_bcol[ia]])
                              cut("C1a")
                              pA, rpA = PQ("A", [0, 1], 1)
                              P.op("dve", lambda e, ia=ia, c=col("g", h, 1): e.tensor_scalar(
                                  out=G1n[ia][:], in0=c.broadcast_to([128, 128]), scalar1=-1.0, scalar2=None, op0=ALU.mult),
                                  reads=G, writes=[r_G1n[ia]])
                              P.op("pe", lambda e, pA=pA, ia=ia, dr=dr: e.matmul(pA, lhsT=G1n[ia][:], rhs=self.tri[dr], start=True, stop=False),
                                   reads=[r_G1n[ia], self.r_const], writes=[rpA])
                              P.op("pe", lambda e, pA=pA, dr=dr: e.matmul(pA, lhsT=self.identf, rhs=self.maskS[dr], start=False, stop=True),
                                   reads=[self.r_const], writes=[rpA])
                              cut("C1b")
                              P.op("act", lambda e, pA=pA, ia=ia, c=col("gc", h, 1): e.activation(
                                  out=Dm[ia][:], in_=pA, func=AF.Exp, **ACTKW(bcol[ia][:, 0:1])), reads=[rpA, r_bcol[ia]], writes=[r_Dm[ia]])
                              cut("C1c")
                              pB, rpB = PQ("B", [2, 3], 1)
                              P.op("pe", lambda e, pB=pB, ia=ia, dr=dr: e.matmul(pB, lhsT=(G1n if os.environ.get("BX") == "1" else G1)[ia][:], rhs=self.tri[dr], start=True, stop=False),
                                   reads=[r_G1[ia], r_G1n[ia], self.r_const], writes=[rpB])
                              P.op("pe", lambda e, pB=pB, dr=dr: e.matmul(pB, lhsT=self.identf, rhs=self.maskI[dr], start=False, stop=True),
                                   reads=[self.r_const], writes=[rpB])
                              cut("C1c2")
                              P.op("act", lambda e, pB=pB, ia=ia, c=col("ngc", h, 1): e.activation(
                                  out=DmT[ia][:], in_=pB, func=AF.Exp, bias=bcol[ia][:, 8:9]), reads=[rpB, r_bcol[ia]], writes=[r_DmT[ia]])
                              cut("C1d")
                              pK, rpK = PQ("K", [0, 1], 2)
                              P.op("pe", lambda e, pK=pK, kT2=kT2, pa=pa: e.matmul(pK, lhsT=kT2[pa, :], rhs=kT2[pa, :], start=True, stop=True),
                                   reads=[r_kq[i3]], writes=[rpK])
                              n0 = rot("N", 4)
                              P.op("dve", lambda e, pK=pK, n0=n0, ia=ia, c=col("nbeta", h, 1): e.scalar_tensor_tensor(
                                  out=Nn[n0][:], in0=pK, scalar=c, in1=Dm[ia][:], op0=ALU.mult, op1=ALU.mult),
                                  reads=[rpK, r_Dm[ia]] + G, writes=[r_Nn[n0]])
                              cut("C1e")
                              pQ_, rpQ = PQ("Q", [2, 3], 2)
                              P.op("pe", lambda e, pQ_=pQ_, kT2=kT2, qT2=qT2, pa=pa: e.matmul(pQ_, lhsT=kT2[pa, :], rhs=qT2[pa, :],
                                                                                            start=True, stop=True),
                                   reads=[r_kq[i3]], writes=[rpQ])
                              iq = rot("qk", 4)
                              qk_idx.append(iq)
                              P.op("dve", lambda e, pQ_=pQ_, iq=iq, ia=ia: e.tensor_tensor(out=qkT[iq][:], in0=pQ_, in1=DmT[ia][:], op=ALU.mult),
                                   reads=[rpQ, r_DmT[ia]], writes=[r_qkT[iq]])
                              cut("C2")
                              tq = rot("tb", 4)
                              P.op("pe", lambda e, tq=tq, n0=n0: e.transpose(out=ptb[:, tq, :], in_=Nn[n0][:], identity=self.identb[:]),
                                   reads=[r_Nn[n0], self.r_const], writes=[r_ptb[0]])
                              m0 = rot("M", 4)
                              P.op("act", lambda e, tq=tq, m0=m0: e.copy(out=Mm[m0][:], in_=ptb[:, tq, :]),
                                   reads=[r_ptb[0]], writes=[r_Mm[m0]])
                              P.op("pool", lambda e, ia=ia, m0=m0: e.tensor_tensor(out=TT[ia][:], in0=Mm[m0][:], in1=self.identb[:], op=ALU.add),
                                   reads=[r_Mm[m0], self.r_const], writes=[r_TT[ia]])
                              nk, mk = n0, m0
                              for k in range(1, 6):
                                  pN, rpN = PQ("N", [0, 1], 4)
                                  P.op("pe", lambda e, pN=pN, nk=nk, mk=mk: e.matmul(pN, lhsT=Mm[mk][:], rhs=Nn[nk][:], start=True, stop=True),
                                       reads=[r_Mm[mk], r_Nn[nk]], writes=[rpN])
                                  n1 = rot("N", 4)
                                  P.op("act", lambda e, pN=pN, n1=n1: e.copy(out=Nn[n1][:], in_=pN), reads=[rpN], writes=[r_Nn[n1]])
                                  m1 = mk
                                  if k < 5:
                                      pM, rpM = PQ("Mq", [2], 4)
                                      P.op("pe", lambda e, pM=pM, nk=nk, mk=mk: e.matmul(pM, lhsT=Nn[nk][:], rhs=Mm[mk][:], start=True, stop=True),
                                           reads=[r_Mm[mk], r_Nn[nk]], writes=[rpM])
                                      m1 = rot("M", 4)
                                      P.op("dve", lambda e, pM=pM, m1=m1: e.tensor_copy(out=Mm[m1][:], in_=pM), reads=[rpM], writes=[r_Mm[m1]])
                                  pT_, rpT = PQ("T", [3], 4)
                                  P.op("pe", lambda e, pT_=pT_, n1=n1, ia=ia: e.matmul(pT_, lhsT=Nn[n1][:], rhs=TT[ia][:], start=True, stop=True),
                                       reads=[r_Nn[n1], r_TT[ia]], writes=[rpT])
                                  P.op("dve", lambda e, pT_=pT_, ia=ia: e.tensor_tensor(out=TT[ia][:], in0=pT_, in1=TT[ia][:], op=ALU.add),
                                       reads=[rpT, r_TT[ia]], writes=[r_TT[ia]])
                                  nk, mk = n1, m1
                              cut("C3")
                              P.op("pe", lambda e, pU=pU, ia=ia, i2=i2, a=a: e.matmul(pU[:, 64 * a:64 * a + 64], lhsT=TT[ia][:], rhs=vb_[i2][:, a, :],
                                                                                     start=True, stop=True),
                                   reads=[r_TT[ia], r_vb[i2]], writes=[rpU])
                              pW, rpW = PQ("W", [2, 3], 3)
                              P.op("pe", lambda e, pW=pW, ia=ia, i2=i2: e.matmul(pW, lhsT=kbe[i2][:].rearrange("p a d -> p (a d)"), rhs=TT[ia][:],
                                                                                start=True, stop=True),
                                   reads=[r_TT[ia], r_kbe[i2]], writes=[rpW])
                              P.op("act", lambda e, pW=pW, i2=i2, pa=pa: e.copy(out=wT2[i2][pa, :], in_=pW[pa, :]),
                                   reads=[rpW], writes=[r_wT2[i2]])
                          P.op("act", lambda e, pU=pU, i2=i2: e.copy(out=u_[i2][:], in_=pU), reads=[rpU], writes=[r_u[i2]])
                          cut("C4")
                          for c2 in ((0, 1) if dr == 0 else (1, 0)):
                              ch = slice(64 * c2, 64 * c2 + 64)
                              pV, rpV = PQ("sV", [0, 1], 2) if False else scan_ps("V")
                              P.op("pe", lambda e, pV=pV, i2=i2: e.matmul(pV, lhsT=wT2[i2][:], rhs=Sb[:], start=True, stop=True),
                                   reads=[r_wT2[i2], r_Sb], writes=[rpV])
                              iv = rot("vn", 2)
                              P.op("dve", lambda e, pV=pV, i2=i2, iv=iv, ch=ch: e.tensor_tensor(out=vn[iv][ch, :], in0=u_[i2][ch, :], in1=pV[ch, :],
                                                                                             op=ALU.subtract),
                                   reads=[rpV, r_u[i2]], writes=[r_vn[iv]])
                              pO, rpO = scan_ps("O")
                              P.op("pe", lambda e, pO=pO, i2=i2: e.matmul(pO, lhsT=qdec[i2][:], rhs=Sb[:], start=True, stop=False),
                                   reads=[r_qdec[i2], r_Sb], writes=[rpO])
                              for a in range(2):
                                  P.op("pe", lambda e, pO=pO, a=a, iv=iv, ch=ch, iq=qk_idx[a]: e.matmul(
                                      pO[:, 64 * a:64 * a + 64], lhsT=qkT[iq][ch, :], rhs=vn[iv][ch, 64 * a:64 * a + 64],
                                      start=False, stop=(a == 1)), reads=[r_qkT[qk_idx[a]], r_vn[iv]], writes=[rpO])
                              io = rot("o", 3)
                              P.op("act", lambda e, pO=pO, io=io, ch=ch: e.copy(out=osb[io][ch, :], in_=pO[ch, :]),
                                   reads=[rpO], writes=[r_osb[io]])
                              P.dma("sp", lambda e, io=io, ch=ch, t0=t0, c2=c2, hp=hp, odst=odst: e.dma_start(
                                  out=odst[t0 + 64 * c2:t0 + 64 * c2 + 64, hp * 128:(hp + 1) * 128], in_=osb[io][ch, :]),
                                  reads=[r_osb[io]], writes=[rodst])
                              pS, rpS = scan_ps("S")
                              P.op("pe", lambda e, pS=pS, i2=i2, iv=iv, ch=ch: e.matmul(
                                  pS, lhsT=kdec[i2][ch].rearrange("p a d -> p (a d)"), rhs=vn[iv][ch, :], start=True, stop=True),
                                  reads=[r_kdec[i2], r_vn[iv]], writes=[rpS])
                              its = rot("tS", 2)
                              P.op("dve", lambda e, pS=pS, its=its: e.tensor_tensor(out=tS[its][:], in0=pS, in1=self.blk_f, op=ALU.mult),
                                   reads=[rpS, self.r_const], writes=[r_tS[its]])
                              P.op("dve", lambda e, its=its, i2=i2, c2=c2: e.scalar_tensor_tensor(
                                  out=Sf[:], in0=Sf[:], scalar=gtc[i2][:, c2:c2 + 1], in1=tS[its][:], op0=ALU.mult, op1=ALU.add),
                                  reads=[r_S, r_gtc[i2], r_tS[its]], writes=[r_S])
                              P.op("act", lambda e: e.copy(out=Sb[:], in_=Sf[:]), reads=[r_S], writes=[r_Sb])
              except _Stop:
                break
            P.barrier()
        if DN_STOP in ("B", "C"):
            return
        with contextlib.ExitStack() as st:
            gng = P.sbuf("gng", [128, 64], F32, st)
            of_ = [P.sbuf("of", [128, 512], F32, st) for _ in range(2)]
            ob_ = [P.sbuf("obk", [128, 512], F32, st) for _ in range(2)]
            zt = [P.sbuf("zt", [128, 512], F32, st) for _ in range(2)]
            sq = [P.sbuf("sqd", [128, 512], F32, st) for _ in range(2)]
            ss = [P.sbuf("ss", [128, 8], F32, st) for _ in range(2)]
            obf = [P.sbuf("obf", [128, 512], BF16, st) for _ in range(2)]
            oT = [P.sbuf("oT", [128, 4, 128], BF16, st) for _ in range(2)]
            pst = [P.psum("pstd", [128, 4, 128], BF16, st) for _ in range(2)]
            r_gng = R()
            r_of, r_ob, r_zt, r_sq, r_ss, r_obf, r_oT, r_pst = ([R(), R()] for _ in range(8))
            P.dma("sp", lambda e: e.dma_start(out=gng[:], in_=d["dn_out_norm_g"][l].partition_broadcast(128)), writes=[r_gng])
            odT = d["odT"].rearrange("(kt p) n -> p kt n", p=128)
            for tt in range(NCP):
                b = tt % 2
                t0 = tt * 128
                P.dma("sp", lambda e, b=b, t0=t0: e.dma_start(out=of_[b][:], in_=d["o_f"][t0:t0 + 128, :]), reads=[rd["o_f"]], writes=[r_of[b]])
                P.dma("sp", lambda e, b=b, t0=t0: e.dma_start(out=ob_[b][:], in_=d["o_b"][t0:t0 + 128, :]), reads=[rd["o_b"]], writes=[r_ob[b]])
                P.dma("sp", lambda e, b=b, t0=t0: e.dma_start(out=zt[b][:], in_=d["z"][t0:t0 + 128, :]), reads=[rd["z"]], writes=[r_zt[b]])
                P.op("pool", lambda e, b=b: e.tensor_tensor(out=of_[b][:], in0=of_[b][:], in1=ob_[b][:], op=ALU.add),
                     reads=[r_of[b], r_ob[b]], writes=[r_of[b]])
                P.op("act", lambda e, b=b: e.activation(out=sq[b][:], in_=of_[b][:], func=AF.Square), reads=[r_of[b]], writes=[r_sq[b]])
                P.op("dve", lambda e, b=b: e.tensor_reduce(out=ss[b][:], in_=sq[b][:].rearrange("p (h d) -> p h d", h=8),
                                                           axis=mybir.AxisListType.X, op=ALU.add), reads=[r_sq[b]], writes=[r_ss[b]])
                P.op("act", lambda e, b=b: e.activation(out=ss[b][:], in_=ss[b][:], func=AF.Sqrt, bias=self.eps_c[:], scale=1.0 / 64),
                     reads=[r_ss[b], self.r_const], writes=[r_ss[b]])
                P.op("dve", lambda e, b=b: e.reciprocal(out=ss[b][:], in_=ss[b][:]), reads=[r_ss[b]], writes=[r_ss[b]])
                P.op("dve", lambda e, b=b: e.tensor_tensor(
                    out=of_[b][:].rearrange("p (h d) -> p h d", h=8), in0=of_[b][:].rearrange("p (h d) -> p h d", h=8),
                    in1=ss[b][:].unsqueeze(2).broadcast_to([128, 8, 64]), op=ALU.mult), reads=[r_of[b], r_ss[b]], writes=[r_of[b]])
                P.op("dve", lambda e, b=b: e.tensor_tensor(
                    out=of_[b][:].rearrange("p (h d) -> p h d", h=8), in0=of_[b][:].rearrange("p (h d) -> p h d", h=8),
                    in1=gng[:].unsqueeze(1).broadcast_to([128, 8, 64]), op=ALU.mult), reads=[r_of[b], r_gng], writes=[r_of[b]])
                P.op("act", lambda e, b=b: e.activation(out=zt[b][:], in_=zt[b][:], func=AF.Silu), reads=[r_zt[b]], writes=[r_zt[b]])
                P.op("dve", lambda e, b=b: e.tensor_tensor(out=obf[b][:], in0=of_[b][:], in1=zt[b][:], op=ALU.mult),
                     reads=[r_of[b], r_zt[b]], writes=[r_obf[b]])
                for kt in range(4):
                    P.op("pe", lambda e, b=b, kt=kt: e.transpose(out=pst[b][:, kt, :], in_=obf[b][:, kt * 128:(kt + 1) * 128],
                                                                 identity=self.identb[:]),
                         reads=[r_obf[b], self.r_const], writes=[r_pst[b]])
                P.op("act", lambda e, b=b: e.copy(out=oT[b][:], in_=pst[b][:]), reads=[r_pst[b]], writes=[r_oT[b]])
                P.dma("sp", lambda e, b=b, t0=t0: e.dma_start(out=odT[:, :, t0:t0 + 128], in_=oT[b][:]),
                      reads=[r_oT[b]], writes=[rd["odT"]])
            P.barrier()


WEIGHT_SPECS = [
    ("norm_mix_g", (DEPTH, D)), ("w_in", (DEPTH, D, NIN)), ("q_norm_g", (DEPTH, 64)), ("k_norm_g", (DEPTH, 64)),
    ("dn_conv_w", (DEPTH, 5, 1536)), ("dn_a_log", (DEPTH, 2, 8)), ("dn_dt_bias", (DEPTH, 2, 8)),
    ("dn_out_norm_g", (DEPTH, 64)), ("w_o_attn", (DEPTH, 512, D)), ("w_o_dn", (DEPTH, 512, D)),
    ("w_out", (DEPTH, D, D)), ("norm_ffn_g", (DEPTH, D)), ("w_up", (DEPTH, D, 2 * DFF)),
    ("ffn_conv_w", (DEPTH, 3, 2 * DFF)), ("w_down", (DEPTH, DFF, D)),
]


def host_consts(T):
    blk = np.zeros((128, 128), np.float32)
    blk[:64, :64] = 1
    blk[64:, 64:] = 1
    rot = np.zeros((128, 128), np.float32)
    for h2 in range(2):
        for ax in range(2):
            for f in range(16):
                m0 = h2 * 64 + ax * 32 + f
                m1 = m0 + 16
                rot[m1, m0] = -1.0
                rot[m0, m1] = 1.0
    ident = np.eye(128, dtype=np.float32)
    p = np.arange(128)[:, None]
    f = np.arange(128)[None, :]
    same = (p // 64) == (f // 64)
    triF = (same & (p <= f)).astype(np.float32)
    triB = (same & (p >= f)).astype(np.float32)
    BIG = 30000.0
    mSf = np.where(same & (p > f), 0.0, -BIG).astype(np.float32)
    mSb = np.where(same & (p < f), 0.0, -BIG).astype(np.float32)
    mIf = np.where(same & (f >= p), 0.0, -BIG).astype(np.float32)
    mIb = np.where(same & (f <= p), 0.0, -BIG).astype(np.float32)
    z = np.zeros((128, 128), np.float32)
    selc = np.zeros((128, 2), np.float32)
    selc[:64, 0] = 1
    selc[64:, 1] = 1
    cf = np.concatenate([blk, rot, ident, triF, triB, mSf, mSb, mIf, mIb, -triF, -triB, z, z, selc], axis=1)
    t = np.arange(T)
    row = (t // 64).astype(np.float32)
    col = (t % 64).astype(np.float32)
    inv = (np.float32(10000.0) ** (-np.arange(16, dtype=np.float32) / np.float32(16))).astype(np.float32)
    ang = np.stack([row[:, None] * inv, col[:, None] * inv], axis=1)
    c = np.cos(ang).astype(np.float32)
    sn = np.sin(ang).astype(np.float32)
    C = np.zeros((128, T), np.float32)
    S = np.zeros((128, T), np.float32)
    for h2 in range(2):
        for ax in range(2):
            for half in range(2):
                r0 = h2 * 64 + ax * 32 + half * 16
                C[r0:r0 + 16] = c[:, ax, :].T
                S[r0:r0 + 16] = sn[:, ax, :].T
    return dict(cf32=cf, ropec=C, ropes=S)


def build(T=SEQ, depth=DEPTH, with_dn=True, with_ffn=True, debug=False, stages="padm"):
    nc = bass.Bass("TRN2", target_bir_lowering=False)
    B = Builder(nc, T=T)
    B.debug = debug
    P = B.P
    x0 = B.dram_in("xT", [D, T], F32)
    for name, shp in WEIGHT_SPECS:
        B.dram_in(name, list(shp), F32)
    B.dram_in("cf32", [128, NCF], F32)
    B.dram_in("ropec", [128, T], F32)
    B.dram_in("ropes", [128, T], F32)
    out = B.dram_out("yT", [D, T], F32)
    xa = B.dram_tmp("xa", [D, T], F32)
    xb = B.dram_tmp("xb", [D, T], F32)
    B.dram_tmp("qaT", [512, T], BF16)
    B.dram_tmp("kaT", [256, T], BF16)
    B.dram_tmp("va", [T, 130], BF16)
    B.dram_tmp("dpre", [1536, T + 4], F32)
    B.dram_tmp("bd", [T, 32], F32)
    B.dram_tmp("z", [T, 512], F32)
    B.dram_tmp("gT", [2048, T], BF16)
    B.dram_tmp("oaT", [512, T], BF16)
    B.dram_tmp("odT", [512, T], BF16)
    B.dram_tmp("dqT", [512, T], BF16)
    B.dram_tmp("dkT", [512, T], BF16)
    B.dram_tmp("dk_tm", [T, 512], BF16)
    B.dram_tmp("dv_tm", [T, 512], BF16)
    B.dram_tmp("o_f", [T, 512], F32)
    B.dram_tmp("o_b", [T, 512], F32)
    B.rd = {k: P.res(k) for k in ("qaT", "kaT", "va", "dpre", "bd", "z", "gT", "oaT", "odT", "dqT", "dkT", "dk_tm", "dv_tm",
                                    "o_f", "o_b")}
    rx = {"x0": P.res(), "xa": P.res(), "xb": P.res(), "out": P.res()}
    with P.stack:
        B.load_consts()
        tch = P.sbuf("touch", [1, 16], F32)
        rt = P.res()
        for name, shp in WEIGHT_SPECS:
            ap = B.d[name]
            idx = tuple([0] * (len(shp) - 1))
            P.dma("sp", lambda e, ap=ap, idx=idx: e.dma_start(out=tch[0:1, 0:8], in_=ap[idx][0:8].rearrange("(o n) -> o n", o=1)),
                  writes=[rt])
        import os
        if not with_dn and "zero" not in os.environ.get("SKIP", ""):
            B.zero_dram(B.d["odT"], B.rd["odT"], bf=True)
        P.barrier()
        for l in range(depth):
            src, rsrc = (x0, rx["x0"]) if l == 0 else (xa, rx["xa"])
            last = (l == depth - 1)
            if "p" in stages:
                B.proj(l, src, rsrc)
            if "a" in stages:
                B.attention(l)
            if with_dn and "d" in stages:
                B.deltanet(l)
            if "m" not in stages:
                continue
            if with_ffn:
                B.merge(l, src, xb, rsrc, rx["xb"])
                dst, rdst = (out, rx["out"]) if last else (xa, rx["xa"])
                B.ffn(l, xb, dst, rx["xb"], rdst)
            else:
                B.merge(l, src, out, rsrc, rx["out"])
        P.emit()
        B.nc_stats = P.stats
    return nc, B


_CACHE = {}


def kernel(**inputs):
    T = SEQ
    if "nc" not in _CACHE:
        _CACHE["nc"] = build(T=T, depth=DEPTH)[0]
        _CACHE["consts"] = host_consts(T)
    nc = _CACHE["nc"]
    x = np.asarray(inputs["x"], dtype=np.float32)
    nb = x.shape[0]
    base = {k: np.ascontiguousarray(np.asarray(inputs[k], dtype=np.float32)) for k, _ in WEIGHT_SPECS}
    base.update(_CACHE["consts"])
    in_maps = []
    for c in range(8):
        m = dict(base)
        m["xT"] = np.ascontiguousarray(x[c % nb].T)
        in_maps.append(m)
    res = run_bass_kernel_spmd(nc, in_maps, core_ids=list(range(8)))
    out = np.stack([np.ascontiguousarray(res.results[b]["yT"].T) for b in range(nb)], axis=0)
    return out.astype(np.float32)
```
